# Optimizing a Trainium2 kernel written in Bass

```python
import jax, jax.numpy as jnp
from jax import lax
import numpy as np

D_MODEL = 4096
BATCH = 4
SEQ = 2048
DEPTH = 1
DEC_BATCH = 128
DEC_SEQ = 4
PAST_LEN = 16384
PAGE_SIZE = 128

CONV_WIDTH = D_MODEL // 2
CONV_K = 31
RWKV_WIDTH = D_MODEL // 2
RWKV_HEAD = 64
RWKV_HEADS = RWKV_WIDTH // RWKV_HEAD
DECAY_RANK = 64
ICLR_RANK = 64
GATE_RANK = 128
SHIFT_W = 3 * RWKV_WIDTH + DECAY_RANK + ICLR_RANK + GATE_RANK
P_TOTAL = 2 * CONV_WIDTH + SHIFT_W + 2 * D_MODEL
D_FF = -(-8 * D_MODEL // 768) * 256
RMS_EPS = 1e-6
LN_EPS = 1e-5
GN_EPS = 64e-5

kernel_name = 'hybrid_conformer_rwkv7_gated_decode_step'


def rms_norm(x, g):
    xf = x.astype(jnp.float32)
    y = xf * lax.rsqrt(jnp.mean(xf * xf, axis=-1, keepdims=True) + RMS_EPS)
    return (y * g.astype(jnp.float32)).astype(x.dtype)


def layer_norm(x, g, b):
    xf = x.astype(jnp.float32)
    mu = jnp.mean(xf, axis=-1, keepdims=True)
    var = jnp.mean(jnp.square(xf - mu), axis=-1, keepdims=True)
    y = (xf - mu) * lax.rsqrt(var + LN_EPS)
    return (y * g.astype(jnp.float32) + b.astype(jnp.float32)).astype(x.dtype)


def conformer_branch(pc, conv_buf, conv_w, conv_b, ln_g, ln_b, w_out):
    u = pc[..., :CONV_WIDTH] * jax.nn.sigmoid(pc[..., CONV_WIDTH:])
    ext = jnp.concatenate([conv_buf.astype(u.dtype), u], axis=1)
    z = lax.conv_general_dilated(
        ext, conv_w[:, None, :].astype(u.dtype), window_strides=(1,), padding='VALID',
        dimension_numbers=('NWC', 'WIO', 'NWC'), feature_group_count=CONV_WIDTH) + conv_b
    z = jax.nn.silu(layer_norm(z, ln_g, ln_b))
    return z @ w_out, ext[:, ext.shape[1] - (CONV_K - 1):]


def wkv_scan(S0, r, w, k, v, a_vec, b_vec):
    def step(S, inp):
        r_t, w_t, k_t, v_t, a_t, b_t = inp
        Sa = jnp.einsum('bhvk,bhk->bhv', S, a_t)
        S = (S * w_t[:, :, None, :] + Sa[..., None] * b_t[:, :, None, :]
             + v_t[..., :, None] * k_t[:, :, None, :])
        y = jnp.einsum('bhvk,bhk->bhv', S, r_t)
        return S, y
    xs = (jnp.swapaxes(r, 0, 1), jnp.swapaxes(w, 0, 1), jnp.swapaxes(k, 0, 1),
          jnp.swapaxes(v, 0, 1), jnp.swapaxes(a_vec, 0, 1), jnp.swapaxes(b_vec, 0, 1))
    S, ys = lax.scan(step, S0, xs)
    return jnp.swapaxes(ys, 0, 1), S


def rwkv7_branch(pr, shift_prev, S0, mu, w0, w2, a0, a2, g2, k_k, k_a, r_k, gn_g, gn_b, w_out):
    B, T, _ = pr.shape
    f32 = jnp.float32
    shifted = jnp.concatenate([shift_prev[:, None].astype(pr.dtype), pr[:, :-1]], axis=1)
    m = pr + (shifted - pr) * mu
    D = RWKV_WIDTH
    r, k, v = m[..., :D], m[..., D:2 * D], m[..., 2 * D:3 * D]
    o = 3 * D
    wl = m[..., o:o + DECAY_RANK]
    o += DECAY_RANK
    al = m[..., o:o + ICLR_RANK]
    o += ICLR_RANK
    gl = m[..., o:o + GATE_RANK]
    w_log = -jax.nn.softplus(-(w0 + jnp.tanh(wl) @ w2)) - 0.5
    decay = jnp.exp(-jnp.exp(w_log.astype(f32)))
    a = jax.nn.sigmoid(a0 + al @ a2)
    g = jax.nn.sigmoid(gl) @ g2

    def hs(t):
        return t.reshape(B, T, RWKV_HEADS, RWKV_HEAD).astype(f32)

    kk = hs(k * k_k)
    kk = kk / jnp.maximum(jnp.sqrt(jnp.sum(kk * kk, axis=-1, keepdims=True)), 1e-12)
    k = k * (1.0 + (a - 1.0) * k_a)
    rh, kh, vh, ah = hs(r), hs(k), hs(v), hs(a)
    y, S = wkv_scan(S0.astype(f32), rh, hs(decay), kh, vh, -kk, kk * ah)
    ym = jnp.mean(y, axis=-1, keepdims=True)
    yv = jnp.mean(jnp.square(y - ym), axis=-1, keepdims=True)
    y = ((y - ym) * lax.rsqrt(yv + GN_EPS)).reshape(B, T, D) * gn_g.astype(f32) + gn_b.astype(f32)
    bonus = jnp.sum(rh * kh * r_k.astype(f32), axis=-1, keepdims=True) * vh
    y = ((y + bonus.reshape(B, T, D)) * g.astype(f32)).astype(pr.dtype)
    return y @ w_out, S, pr[:, -1]


def decoder_layer(x, wkv0, conv0, shift0,
                  ln_mix_pre, ln_mix_post, ln_ffn_pre, ln_ffn_post, w_in, b_gate,
                  conv_w, conv_b, conv_ln_g, conv_ln_b, w_conv_out, shift_mu,
                  w0, w2, a0, a2, g2, k_k, k_a, r_k, gn_g, gn_b, w_rwkv_out, w_o,
                  w_ffn_gate, w_ffn_up, w_ffn_down):
    B, T, _ = x.shape
    h = rms_norm(x, ln_mix_pre)
    p = h @ w_in
    c_end = 2 * CONV_WIDTH
    r_end = c_end + SHIFT_W
    out_c, conv_new = conformer_branch(p[..., :c_end], conv0, conv_w, conv_b,
                                       conv_ln_g, conv_ln_b, w_conv_out)
    out_r, wkv_new, shift_new = rwkv7_branch(p[..., c_end:r_end], shift0, wkv0, shift_mu,
                                             w0, w2, a0, a2, g2, k_k, k_a, r_k,
                                             gn_g, gn_b, w_rwkv_out)
    gates = jax.nn.sigmoid(p[..., r_end:].reshape(B, T, 2, D_MODEL) + b_gate)
    mixed = gates[:, :, 0] * out_c + gates[:, :, 1] * out_r
    x = x + rms_norm(mixed @ w_o, ln_mix_post)
    h2 = rms_norm(x, ln_ffn_pre)
    f = (jax.nn.silu(h2 @ w_ffn_gate) * (h2 @ w_ffn_up)) @ w_ffn_down
    x = x + rms_norm(f, ln_ffn_post)
    return x, wkv_new, conv_new, shift_new


def setup_inputs(seed: int = 0) -> dict:
    key = jax.random.key(seed)
    ks = list(jax.random.split(key, 40))

    def nrm(i, shape, s):
        return s * jax.random.normal(ks[i], shape, jnp.float32)

    L = DEPTH
    x_prompt = nrm(0, (BATCH, SEQ, D_MODEL), 1.0)
    x_sample = nrm(1, (DEC_BATCH, DEC_SEQ, D_MODEL), 1.0)
    state_wkv = nrm(2, (L, DEC_BATCH, RWKV_HEADS, RWKV_HEAD, RWKV_HEAD), 0.3)
    state_conv = nrm(3, (L, DEC_BATCH, CONV_K - 1, CONV_WIDTH), 0.5)
    state_shift = nrm(4, (L, DEC_BATCH, SHIFT_W), 1.0)
    ln_mix_pre = 1.0 + nrm(5, (L, D_MODEL), 0.01)
    ln_mix_post = 1.0 + nrm(6, (L, D_MODEL), 0.01)
    ln_ffn_pre = 1.0 + nrm(7, (L, D_MODEL), 0.01)
    ln_ffn_post = 1.0 + nrm(8, (L, D_MODEL), 0.01)
    w_in = nrm(9, (L, D_MODEL, P_TOTAL), D_MODEL ** -0.5)
    b_gate = nrm(10, (L, 2, D_MODEL), 0.1)
    conv_w = nrm(11, (L, CONV_K, CONV_WIDTH), CONV_K ** -0.5)
    conv_b = nrm(12, (L, CONV_WIDTH), 0.01)
    conv_ln_g = 1.0 + nrm(13, (L, CONV_WIDTH), 0.01)
    conv_ln_b = nrm(14, (L, CONV_WIDTH), 0.01)
    w_conv_out = nrm(15, (L, CONV_WIDTH, D_MODEL), CONV_WIDTH ** -0.5)
    shift_mu = jax.random.uniform(ks[16], (L, SHIFT_W), jnp.float32, 0.0, 1.0)
    w0 = jax.random.uniform(ks[17], (L, RWKV_WIDTH), jnp.float32, -5.0, -0.5)
    w2 = nrm(18, (L, DECAY_RANK, RWKV_WIDTH), 0.5 * DECAY_RANK ** -0.5)
    a0 = nrm(19, (L, RWKV_WIDTH), 0.1)
    a2 = nrm(20, (L, ICLR_RANK, RWKV_WIDTH), 0.5 * ICLR_RANK ** -0.5)
    g2 = nrm(21, (L, GATE_RANK, RWKV_WIDTH), GATE_RANK ** -0.5)
    k_k = 0.85 + nrm(22, (L, RWKV_WIDTH), 0.02)
    k_a = 1.0 + nrm(23, (L, RWKV_WIDTH), 0.02)
    r_k = nrm(24, (L, RWKV_HEADS, RWKV_HEAD), 0.1)
    gn_g = 1.0 + nrm(25, (L, RWKV_WIDTH), 0.01)
    gn_b = nrm(26, (L, RWKV_WIDTH), 0.01)
    w_rwkv_out = nrm(27, (L, RWKV_WIDTH, D_MODEL), RWKV_WIDTH ** -0.5)
    w_o = nrm(28, (L, D_MODEL, D_MODEL), D_MODEL ** -0.5)
    w_ffn_gate = nrm(29, (L, D_MODEL, D_FF), D_MODEL ** -0.5)
    w_ffn_up = nrm(30, (L, D_MODEL, D_FF), D_MODEL ** -0.5)
    w_ffn_down = nrm(31, (L, D_FF, D_MODEL), D_FF ** -0.5)
    return {'x_prompt': x_prompt, 'x_sample': x_sample,
            'state_wkv': state_wkv, 'state_conv': state_conv, 'state_shift': state_shift,
            'ln_mix_pre': ln_mix_pre, 'ln_mix_post': ln_mix_post,
            'ln_ffn_pre': ln_ffn_pre, 'ln_ffn_post': ln_ffn_post,
            'w_in': w_in, 'b_gate': b_gate, 'conv_w': conv_w, 'conv_b': conv_b,
            'conv_ln_g': conv_ln_g, 'conv_ln_b': conv_ln_b, 'w_conv_out': w_conv_out,
            'shift_mu': shift_mu, 'w0': w0, 'w2': w2, 'a0': a0, 'a2': a2, 'g2': g2,
            'k_k': k_k, 'k_a': k_a, 'r_k': r_k, 'gn_g': gn_g, 'gn_b': gn_b,
            'w_rwkv_out': w_rwkv_out, 'w_o': w_o,
            'w_ffn_gate': w_ffn_gate, 'w_ffn_up': w_ffn_up, 'w_ffn_down': w_ffn_down}


def reference(x_prompt, x_sample, state_wkv, state_conv, state_shift,
              ln_mix_pre, ln_mix_post, ln_ffn_pre, ln_ffn_post, w_in, b_gate,
              conv_w, conv_b, conv_ln_g, conv_ln_b, w_conv_out, shift_mu,
              w0, w2, a0, a2, g2, k_k, k_a, r_k, gn_g, gn_b, w_rwkv_out, w_o,
              w_ffn_gate, w_ffn_up, w_ffn_down):
    weights = (ln_mix_pre, ln_mix_post, ln_ffn_pre, ln_ffn_post, w_in, b_gate,
               conv_w, conv_b, conv_ln_g, conv_ln_b, w_conv_out, shift_mu,
               w0, w2, a0, a2, g2, k_k, k_a, r_k, gn_g, gn_b, w_rwkv_out, w_o,
               w_ffn_gate, w_ffn_up, w_ffn_down)
    bp = x_prompt.shape[0]
    yp, ys = x_prompt, x_sample
    wkv_p, conv_p, shift_p, wkv_s, conv_s, shift_s = [], [], [], [], [], []
    for l in range(DEPTH):
        lw = tuple(w[l] for w in weights)
        yp, s_wkv, s_conv, s_shift = decoder_layer(
            yp,
            jnp.zeros((bp, RWKV_HEADS, RWKV_HEAD, RWKV_HEAD), jnp.float32),
            jnp.zeros((bp, CONV_K - 1, CONV_WIDTH), x_prompt.dtype),
            jnp.zeros((bp, SHIFT_W), x_prompt.dtype), *lw)
        wkv_p.append(s_wkv.astype(x_prompt.dtype))
        conv_p.append(s_conv)
        shift_p.append(s_shift)
        ys, d_wkv, d_conv, d_shift = decoder_layer(
            ys, state_wkv[l], state_conv[l], state_shift[l], *lw)
        wkv_s.append(d_wkv.astype(state_wkv.dtype))
        conv_s.append(d_conv.astype(state_conv.dtype))
        shift_s.append(d_shift.astype(state_shift.dtype))
    return (yp, ys,
            jnp.stack(wkv_p), jnp.stack(conv_p), jnp.stack(shift_p),
            jnp.stack(wkv_s), jnp.stack(conv_s), jnp.stack(shift_s))
```

```python
import numpy as np
from contextlib import ExitStack
import concourse.bass as bass
import concourse.mybir as mybir
from concourse.bass_utils import run_bass_kernel_spmd

F32 = mybir.dt.float32
BF16 = mybir.dt.bfloat16
F32R = mybir.dt.float32r


def R(ap):
    return ap.bitcast(F32R)
AF = mybir.ActivationFunctionType
ALU = mybir.AluOpType

RMS_EPS = 1e-6
LN_EPS = 1e-5
GN_EPS = 64e-5
CONV_K = 31
HALO = CONV_K - 1
SCRATCH_KIND = "Internal"


class Cfg:
    def __init__(s, D=4096, SEQ=2048, BATCH=4, DEC_BATCH=128, DEC_SEQ=4):
        s.D = D
        s.SEQ = SEQ
        s.BATCH = BATCH
        s.DEC_BATCH = DEC_BATCH
        s.NCORES = 2 * BATCH
        s.CW = D // 2
        s.RW = D // 2
        s.NH = s.RW // 64
        s.NHP = s.RW // 128
        s.SHW = 3 * s.RW + 256
        s.PT = 2 * s.CW + s.SHW + 2 * D
        s.DFF = -(-8 * D // 768) * 256
        s.KD = D // 128
        s.KC = s.CW // 128
        s.KF = s.DFF // 128
        s.NST = s.SHW // 128
        s.TP = SEQ // 2
        s.NSQ = DEC_BATCH // s.NCORES
        s.TS = DEC_SEQ
        s.NTS = s.NSQ * s.TS
        s.NT2 = HALO + s.TP + s.NTS
        s.SMP0 = HALO + s.TP
        s.NB2 = s.TP + s.NTS
        o = 0
        s.pc = {}
        for name, n in [("g1", s.KD), ("g2", s.KD), ("g3", s.KD), ("g4", s.KD), ("bg", 2 * s.KD),
                        ("cb", s.KC), ("cg", s.KC), ("cbeta", s.KC), ("mu", s.NST),
                        ("w0", s.NHP), ("a0", s.NHP), ("kk", s.NHP), ("ka", s.NHP), ("rk", s.NHP),
                        ("gg", s.NHP), ("gb", s.NHP)]:
            s.pc[name] = o
            o += n
        s.NPC = o
        s.cc = {"ident": 0, "bdmean": 128, "bdsum": 256, "onesln": 384, "masks": 512, "onescol": 512 + 320}
        s.NCC = 512 + 320 + 1


def _cols(v):
    v = np.asarray(v, np.float32).reshape(-1, 128)
    return np.ascontiguousarray(v.T)


def make_pcols(cfg, I):
    parts = [_cols(I["ln_mix_pre"][0]), _cols(I["ln_mix_post"][0]), _cols(I["ln_ffn_pre"][0]),
             _cols(I["ln_ffn_post"][0]), _cols(I["b_gate"][0].reshape(-1)),
             _cols(I["conv_b"][0]), _cols(I["conv_ln_g"][0]), _cols(I["conv_ln_b"][0]),
             _cols(I["shift_mu"][0]), _cols(I["w0"][0]), _cols(I["a0"][0]), _cols(I["k_k"][0]),
             _cols(I["k_a"][0]), _cols(I["r_k"][0].reshape(-1)), _cols(I["gn_g"][0]), _cols(I["gn_b"][0])]
    out = np.ascontiguousarray(np.concatenate(parts, axis=1), dtype=np.float32)
    assert out.shape == (128, cfg.NPC), (out.shape, cfg.NPC)
    return out


def make_cwcols(cfg, I):
    cw = np.asarray(I["conv_w"][0], np.float32)
    return np.ascontiguousarray(cw.reshape(CONV_K, cfg.KC, 128).transpose(2, 1, 0).reshape(128, cfg.KC * CONV_K))


def make_consts(cfg):
    c = np.zeros((128, cfg.NCC), np.float32)
    c[:, 0:128] = np.eye(128)
    bd = np.zeros((128, 128), np.float32)
    bd[:64, :64] = 1.0
    bd[64:, 64:] = 1.0
    c[:, 128:256] = bd / 64.0
    c[:, 256:384] = bd
    c[:, 384:512] = 1.0 / cfg.CW
    s = np.arange(64)[:, None]
    t = np.arange(64)[None, :]
    strictT = (s < t).astype(np.float32)
    inclT = (s <= t).astype(np.float32)
    strictL = (t < s).astype(np.float32)
    m = np.stack([strictT, strictT, inclT, inclT, strictL], axis=1)
    c[:64, 512:512 + 320] = m.reshape(64, 320)
    c[:, 512 + 320] = 1.0
    return c


class StopBuild(Exception):
    pass


class Eng:
    def __init__(s, name, h, sem):
        s.name, s.h, s.sem, s.cnt, s.seen = name, h, sem, 0, {}


class Buf:
    def __init__(s, name, dsem=None):
        s.name, s.w, s.rd, s.dsem, s.dcnt = name, {}, {}, dsem, 0


class Sy:
    def __init__(s, nc, stack):
        s.nc, s.stack = nc, stack
        s.nsem = 0
        s.pe = Eng("pe", nc.tensor, s.newsem("pe"))
        s.act = Eng("act", nc.scalar, s.newsem("act"))
        s.dve = Eng("dve", nc.vector, s.newsem("dve"))
        s.pool = Eng("pool", nc.gpsimd, s.newsem("pool"))
        s.sp = Eng("sp", nc.sync, None)
        s.dbufs = []
        s.ninst = 0
        s.dead = False

    def newsem(s, name):
        s.nsem += 1
        return s.stack.enter_context(s.nc.semaphore("s%d_%s" % (s.nsem, name)))

    def buf(s, name, dma=False):
        b = Buf(name, s.newsem("d_" + name) if dma else None)
        if dma:
            s.dbufs.append(b)
        return b

    def _wait(s, e, key, sem, val):
        if s.dead:
            return
        if e.seen.get(key, 0) < val:
            e.h.wait_ge(sem, val)
            e.seen[key] = val

    def _deps(s, e, reads, writes):
        for b in reads:
            for e2, c in b.w.items():
                if not (e2 is e and e is s.pe):
                    s._wait(e, e2.name, e2.sem, c)
            if b.dcnt:
                s._wait(e, id(b), b.dsem, b.dcnt)
        for b in writes:
            for e2, c in b.w.items():
                if not (e2 is e and e is s.pe):
                    s._wait(e, e2.name, e2.sem, c)
            for e2, c in b.rd.items():
                if not (e2 is e and e is s.pe):
                    s._wait(e, e2.name, e2.sem, c)
            if b.dcnt:
                s._wait(e, id(b), b.dsem, b.dcnt)

    def op(s, e, reads, writes, fn):
        if s.dead:
            return None
        s._deps(e, reads, writes)
        ins = fn(e.h)
        e.cnt += 1
        ins.then_inc(e.sem, 1)
        for b in reads:
            b.rd[e] = e.cnt
        for b in writes:
            b.w[e] = e.cnt
        s.ninst += 1
        return ins

    def T(s, r, w, fn):
        return s.op(s.pe, r, w, fn)

    def A(s, r, w, fn):
        return s.op(s.act, r, w, fn)

    def V(s, r, w, fn):
        return s.op(s.dve, r, w, fn)

    def G(s, r, w, fn):
        return s.op(s.pool, r, w, fn)

    def dma(s, q, out, in_, wbuf=None, rbuf=None, **kw):
        if s.dead:
            return None
        b = wbuf if wbuf is not None else rbuf
        s._deps(q, [rbuf] if rbuf is not None else [], [wbuf] if wbuf is not None else [])
        ins = q.h.dma_start(out=out, in_=in_, **kw)
        ins.then_inc(b.dsem, 16)
        b.dcnt += 16
        s.ninst += 1
        return ins

    def barrier(s):
        engs = [s.pe, s.act, s.dve, s.pool]
        for e in engs + [s.sp]:
            for e2 in engs:
                if e2 is not e and e2.cnt:
                    s._wait(e, e2.name, e2.sem, e2.cnt)
            for b in s.dbufs:
                if b.dcnt:
                    s._wait(e, id(b), b.dsem, b.dcnt)

    def finish(s):
        for b in s.dbufs:
            if b.dcnt:
                s._wait(s.sp, id(b), b.dsem, b.dcnt)


class WStream:
    def __init__(s, sy, nc, stack, nslots, name, held, kcmax=32):
        s.sy, s.n, s.ahead = sy, nslots, nslots - held
        assert s.ahead >= 0
        s.tiles = [stack.enter_context(nc.sbuf_tensor("ws_%s_w%d" % (name, i), [128, kcmax * 128], BF16))
                   for i in range(nslots)]
        s.bufs = [sy.buf("%s_w%d" % (name, i), dma=True) for i in range(nslots)]
        s.queue = []
        s.issued = 0
        s.taken = 0
        s.bg = None

    def plan(s, specs):
        s.queue += specs

    def _issue(s):
        i = s.issued
        sl = i % s.n
        if s.queue[i][0] is None:
            _, src, kc, _ = s.queue[i]
            s.sy.dma(s.sy.pool, s.tiles[sl][:, 0:kc * 128], src, wbuf=s.bufs[sl])
        else:
            W, r0, kc, c0 = s.queue[i]
            dst = s.tiles[sl][:, 0:kc * 128].rearrange("p (k c) -> p k c", c=128)
            src = W[r0:r0 + kc * 128, c0:c0 + 128].rearrange("(k p) c -> p k c", p=128)
            s.sy.dma(s.sy.pool, dst, src, wbuf=s.bufs[sl])
        s.issued += 1
        if s.bg is not None:
            s.bg()

    def get(s):
        while s.issued < min(len(s.queue), s.taken + 1 + s.ahead):
            s._issue()
        i = s.taken
        s.taken += 1
        kc = s.queue[i][2]
        sl = i % s.n
        return s.bufs[sl], s.tiles[sl][:, 0:kc * 128].rearrange("p (k c) -> p k c", c=128)


def split_blocks(t0, n, maxn=512):
    nb = -(-n // maxn)
    base = n // nb
    rem = n % nb
    out = []
    o = t0
    for i in range(nb):
        sz = base + (1 if i < rem else 0)
        out.append((o, sz))
        o += sz
    return out


def tok_tiles(n):
    return [(i, min(128, n - i)) for i in range(0, n, 128)]


def build(cfg):
    nc = bass.Bass("TRN2", target_bir_lowering=False)
    D, CW, RW, NHP, NH, KD, KC, KF = cfg.D, cfg.CW, cfg.RW, cfg.NHP, cfg.NH, cfg.KD, cfg.KC, cfg.KF
    TP, NSQ, NTS, NT2, SMP0, NB2, SHW, NST, DFF = (cfg.TP, cfg.NSQ, cfg.NTS, cfg.NT2, cfg.SMP0, cfg.NB2,
                                                  cfg.SHW, cfg.NST, cfg.DFF)
    PC = cfg.pc
    CCo = cfg.cc

    def din(name, shape):
        return nc.dram_tensor(name, list(shape), F32, kind="ExternalInput").ap()

    def dout(name, shape):
        return nc.dram_tensor(name, list(shape), F32, kind="ExternalOutput").ap()

    xp = din("xp", [2 * TP, D])
    xs = din("xs", [NTS, D])
    swkv = din("swkv", [NSQ, NH, 64, 64])
    sconv = din("sconv", [NSQ * HALO, CW])
    sshift = din("sshift", [NSQ, SHW])
    w_in = din("w_in", [D, cfg.PT])
    wco = din("wco", [CW, D])
    wro = din("wro", [RW, D])
    wo = din("wo", [D, D])
    wg = din("wg", [D, DFF])
    wu = din("wu", [D, DFF])
    wd = din("wd", [DFF, D])
    pcols_d = din("pcols", [128, cfg.NPC])
    consts_d = din("consts", [128, cfg.NCC])
    w2a2_d = din("w2a2", [128, RW])
    g2_d = din("g2", [128, RW])
    cwc_d = din("cwc", [128, KC * CONV_K])

    y_d = dout("y", [NB2, D])
    wkvp_d = dout("wkvp", [NH, 64, 64])
    convp_d = dout("convp", [HALO, CW])
    shiftp_d = dout("shiftp", [NST, 128])
    wkvs_d = dout("wkvs", [NSQ, NH, 64, 64])
    convs_d = dout("convs", [NSQ * HALO, CW])
    shifts_d = dout("shifts", [NSQ, SHW])

    zsp = nc.dram_tensor("zsp", [KC, 128, NB2], BF16, kind=SCRATCH_KIND).ap()
    mixsp = nc.dram_tensor("mixsp", [KD, 128, NB2], BF16, kind=SCRATCH_KIND).ap()
    zfsp = nc.dram_tensor("zfsp", [KC, 128, NB2], F32, kind=SCRATCH_KIND).ap()
    ysp = nc.dram_tensor("ysp", [NHP, 128, NB2], BF16, kind=SCRATCH_KIND).ap()
    wsc_o = nc.dram_tensor("wsc_o", [KD, 128, KD * 128], BF16, kind=SCRATCH_KIND).ap()
    wsc_g = nc.dram_tensor("wsc_g", [KF, 128, KD * 128], BF16, kind=SCRATCH_KIND).ap()
    wsc_u = nc.dram_tensor("wsc_u", [KF, 128, KD * 128], BF16, kind=SCRATCH_KIND).ap()
    wsc_d = nc.dram_tensor("wsc_d", [KD, 128, KF * 128], BF16, kind=SCRATCH_KIND).ap()

    with ExitStack() as top:
        sy = Sy(nc, top)
        T, A, V, G = sy.T, sy.A, sy.V, sy.G

        uid = [0]

        def sb(stk, name, shape, dt=F32):
            uid[0] += 1
            if getattr(cfg, "dbg_sb", False):
                print("SB", name, shape, dt, "remaining", nc.sbuf_bytes_remaining, flush=True)
            return stk.enter_context(nc.sbuf_tensor("t%d_%s" % (uid[0], name), list(shape), dt))

        pcs = sb(top, "pcs", [128, cfg.NPC])
        cst = sb(top, "cst", [128, cfg.NCC])
        Bpar = sy.buf("params", dma=True)
        sy.dma(sy.sp, pcs[:], pcols_d[:, :], wbuf=Bpar)
        sy.dma(sy.sp, cst[:], consts_d[:, :], wbuf=Bpar)
        ident = cst[:, 0:128]
        bdmean = cst[:, 128:256]
        bdsum = cst[:, 256:384]
        onesln = cst[:, 384:512]
        masks = cst[0:64, 512:832].rearrange("p (q c) -> p q c", c=64)
        onescol = cst[:, 832:833]

        def pcol(name, i=0, n=1):
            return pcs[:, PC[name] + i:PC[name] + i + n]

        def rsqrt_inplace(ap, Bap):
            A([Bap], [Bap], lambda h: h.activation(out=ap, in_=ap, func=AF.Ln))
            A([Bap], [Bap], lambda h: h.activation(out=ap, in_=ap, func=AF.Exp, scale=-0.5))

        HsAll = sb(top, "HsAll", [64, NHP, 2, 64])
        BHs = sy.buf("HsAll")
        small = sb(top, "small", [128, 16])
        Bsmall = sy.buf("small")

        banks = [top.enter_context(nc.psum_tensor("bank%d" % i, [128, 512], F32)) for i in range(8)]
        Bbank = [sy.buf("bank%d" % i) for i in range(8)]
        misc_rr = [0]

        def misc():
            i = 6 + (misc_rr[0] % 2)
            misc_rr[0] += 1
            return banks[i], Bbank[i]

        def norm_transpose(src, Bsrc, n, gname, dst, Bdst, tok, xn, Bxn, junk, Bjunk, extra_w=[]):
            ss = small[:, 0:1]
            rs = small[:, 1:2]
            V([], [Bsmall], lambda h: h.memset(ss[:n], 0.0))
            A([Bsrc, Bsmall], [Bjunk, Bsmall],
              lambda h: h.activation(out=junk[:n, :], in_=src, func=AF.Square, accum_out=ss[:n]))
            V([Bsmall], [Bsmall], lambda h: h.tensor_scalar(out=rs[:n], in0=ss[:n], scalar1=1.0 / D, scalar2=RMS_EPS,
                                                           op0=ALU.mult, op1=ALU.add))
            rsqrt_inplace(rs[:n], Bsmall)
            A([Bsrc, Bsmall], [Bxn], lambda h: h.activation(out=xn[:n, :], in_=src, func=AF.Identity, scale=rs[:n]))
            gq = min(4, KD)
            for dg in range(KD // gq):
                bk, Bbk = misc()
                for q in range(gq):
                    dc = dg * gq + q
                    T([Bxn, Bpar], [Bbk], lambda h, q=q, dc=dc: h.transpose(
                        bk[:, q * 128:q * 128 + n], xn[:n, dc * 128:(dc + 1) * 128], ident[:n, :n]))
                gcol = PC[gname] + dg * gq
                V([Bbk, Bpar], [Bdst] + extra_w, lambda h, dg=dg, gcol=gcol: h.tensor_tensor(
                    out=dst[:, dg * gq:(dg + 1) * gq, tok:tok + n],
                    in0=bk[:, 0:gq * 128].rearrange("p (q t) -> p q t", t=128)[:, :, :n],
                    in1=pcs[:, gcol:gcol + gq].unsqueeze(2).to_broadcast([128, gq, n]), op=ALU.mult))

        def load_norm_transpose(stk, rows, gname, dst, Bdst):
            xin = [sb(stk, "xin%d" % i, [128, D]) for i in range(2)]
            Bxin = [sy.buf("xin%d" % i, dma=True) for i in range(2)]
            xn = sb(stk, "xn", [128, D])
            Bxn = sy.buf("xn")
            junk, Bjunk = xn, Bxn
            for i, (ap, n, tok) in enumerate(rows):
                sl = i % 2
                sy.dma(sy.sp, xin[sl][:n, :], ap, wbuf=Bxin[sl])
                norm_transpose(xin[sl][:n, :], Bxin[sl], n, gname, dst, Bdst, tok, xn, Bxn, junk, Bjunk)

        def mm_pass(Bslab, slab, kcn, rhs_fn, Brhs, blocks, bankset):
            for kc in range(kcn):
                for bi, (t0, n) in enumerate(blocks):
                    bk, Bbk = banks[bankset[bi]], Bbank[bankset[bi]]
                    T([Bslab] + Brhs, [Bbk], lambda h, kc=kc, t0=t0, n=n, bk=bk: h.matmul(
                        bk[:, 0:n], lhsT=slab[:, kc, :], rhs=rhs_fn(kc, t0, n), start=(kc == 0), stop=(kc == kcn - 1)))

        def rwkv_proj(mode, hp, W, hT, BhT, t_lo, n, has_prev, wk, Bw, stg, pset, cbase=0, nch=0, part="both",
                      sel=None, kcr=None):
            o = 1 if has_prev else 0
            n1 = n + o
            names = ["k", "v"] if mode == "pre" else ["r", "k", "v"]
            tix = {"r": hp, "k": NHP + hp, "v": 2 * NHP + hp}
            TS = cfg.TS
            for xi, x in enumerate(names):
                Bsl, sl = W[x]
                pt, Bpt = wk["p" + x + pset], Bw["p" + x + pset]
                if sel is not None and x not in sel:
                    continue
                if part in ("both", "mm"):
                    k0_, k1_ = kcr if kcr is not None else (0, KD)
                    for kc in range(k0_, k1_):
                        T([Bsl, BhT], [Bbank[xi]], lambda h, kc=kc, xi=xi, sl=sl: h.matmul(
                            banks[xi][:, 0:n1], lhsT=sl[:, kc, :], rhs=hT[:, kc, t_lo - o:t_lo - o + n1],
                            start=(kc == 0), stop=(kc == KD - 1)))
                if part in ("both", "ev"):
                    A([Bbank[xi]], [Bpt], lambda h, xi=xi, pt=pt: h.activation(
                        out=R(pt[:, 0:n1]), in_=banks[xi][:, 0:n1], func=AF.Identity))
                    if mode == "own" and t_lo + n == SMP0:
                        G([Bpt], [stg["Bshp"]], lambda h, x=x, pt=pt: h.tensor_copy(
                            out=stg["shp"][:, tix[x]:tix[x] + 1], in_=pt[:, n1 - 1:n1]))
                    if mode == "smp":
                        G([Bpt], [stg["Bshs"]], lambda h, x=x, pt=pt: h.tensor_copy(
                            out=stg["shs"][:, tix[x], cbase:cbase + nch],
                            in_=pt[:, 0:n1].rearrange("p (b s) -> p b s", s=TS)[:, :, TS - 1]))

        def rwkv_group(mode, hp, lor, Blor, t_lo, n, C, nch, has_prev, wk, Bw0, shT, BshT,
                       pset, Sin=None, BSin=None, Sout=None, BSout=None, ytok=0, cbase=0, mid_hook=None, lev_hook=None):
            o = 1 if has_prev else 0
            n1 = n + o
            names = ["k", "v"] if mode == "pre" else ["r", "k", "v"]
            tix = {"r": hp, "k": NHP + hp, "v": 2 * NHP + hp}
            TS = cfg.TS
            amap = {"pr": "pr" + pset, "pk": "pk" + pset, "pv": "pv" + pset,
                    "cum": "pr" + pset, "ep": "pk" + pset, "en": "pv" + pset}

            def t(name):
                return wk[amap.get(name, name)]

            class _B(dict):
                def __getitem__(self_, k):
                    return Bw0[amap.get(k, k)]
            Bw = _B()

            stop_here("g1")
            for x in names:
                px = t("p" + x)
                d = t("d")
                mu = pcol("mu", tix[x])
                if mode == "smp":
                    p3 = px[:, 0:n].rearrange("p (b s) -> p b s", s=TS)
                    d3 = d[:, 0:n].rearrange("p (b s) -> p b s", s=TS)
                    V([Bw["p" + x]], [Bw["d"]], lambda h, p3=p3, d3=d3: h.tensor_tensor(
                        out=d3[:, :, 1:TS], in0=p3[:, :, 0:TS - 1], in1=p3[:, :, 1:TS], op=ALU.subtract))
                    V([Bw["p" + x], BshT], [Bw["d"]], lambda h, p3=p3, d3=d3, x=x: h.tensor_tensor(
                        out=d3[:, :, 0], in0=shT[:, tix[x], cbase:cbase + nch], in1=p3[:, :, 0], op=ALU.subtract))
                else:
                    V([Bw["p" + x]], [Bw["d"]], lambda h, px=px, d=d: h.tensor_tensor(
                        out=d[:, 1:n1], in0=px[:, 0:n1 - 1], in1=px[:, 1:n1], op=ALU.subtract))
                    if not has_prev:
                        V([Bw["p" + x]], [Bw["d"]], lambda h, px=px, d=d: h.tensor_scalar(
                            out=d[:, 0:1], in0=px[:, 0:1], scalar1=-1.0, scalar2=None, op0=ALU.mult))
                V([Bw["p" + x], Bw["d"], Bpar], [Bw["m" + x]], lambda h, px=px, d=d, mu=mu, x=x: h.scalar_tensor_tensor(
                    out=R(t("m" + x)[:, 0:n]), in0=d[:, o:n1], scalar=mu, in1=px[:, o:n1], op0=ALU.mult, op1=ALU.add))
            mk, mv = t("mk"), t("mv")
            stop_here("g2")
            T([Bw["lw_"], Blor], [Bbank[0]], lambda h: h.matmul(banks[0][:, 0:n], lhsT=wk["w2a2"][0:64, :],
                                                               rhs=lor[0:64, 0, t_lo:t_lo + n], start=True, stop=True))
            A([Bbank[0], Bpar], [Bw["lw"]], lambda h: h.activation(out=t("lw")[:, 0:n], in_=banks[0][:, 0:n],
                                                                 func=AF.Sigmoid, bias=pcol("w0", hp)))
            V([Bw["lw"]], [Bw["lw"]], lambda h: h.tensor_scalar(out=t("lw")[:, 0:n], in0=t("lw")[:, 0:n],
                                                              scalar1=-0.6065306597126334, scalar2=None, op0=ALU.mult))
            T([Bw["lw_"], Blor], [Bbank[1]], lambda h: h.matmul(banks[1][:, 0:n], lhsT=wk["w2a2"][64:128, :],
                                                               rhs=lor[64:128, 0, t_lo:t_lo + n], start=True, stop=True))
            A([Bbank[1], Bpar], [Bw["ic"]], lambda h: h.activation(out=R(t("ic")[:, 0:n]), in_=banks[1][:, 0:n],
                                                                 func=AF.Sigmoid, bias=pcol("a0", hp)))
            if mode != "pre":
                T([Bw["lw_"], Blor], [Bbank[2]], lambda h: h.matmul(banks[2][:, 0:n], lhsT=wk["g2"][:, :],
                                                                   rhs=lor[:, 1, t_lo:t_lo + n], start=True, stop=True))
                A([Bbank[2]], [Bw["gt"]], lambda h: h.activation(out=t("gt")[:, 0:n], in_=banks[2][:, 0:n],
                                                               func=AF.Identity))
            stop_here("g3")
            V([Bw["mk"], Bpar], [Bw["kk"]], lambda h: h.tensor_scalar(out=t("kk")[:, 0:n], in0=mk[:, 0:n],
                                                                    scalar1=pcol("kk", hp), scalar2=None, op0=ALU.mult))
            A([Bw["mk"], Bpar], [Bw["e1"]], lambda h: h.activation(out=t("e1")[:, 0:n], in_=mk[:, 0:n], func=AF.Square,
                                                                 scale=pcol("kk", hp)))
            T([Bpar, Bw["e1"]], [Bbank[0]], lambda h: h.matmul(banks[0][:, 0:n], lhsT=bdsum, rhs=t("e1")[:, 0:n],
                                                              start=True, stop=True))
            V([Bbank[0]], [Bw["e1"]], lambda h: h.tensor_scalar(out=t("e1")[:, 0:n], in0=banks[0][:, 0:n],
                                                              scalar1=1e-24, scalar2=None, op0=ALU.max))
            rsqrt_inplace(t("e1")[:, 0:n], Bw["e1"])
            V([Bw["kk"], Bw["e1"]], [Bw["kk"]], lambda h: h.tensor_tensor(out=t("kk")[:, 0:n], in0=t("kk")[:, 0:n],
                                                                         in1=t("e1")[:, 0:n], op=ALU.mult))
            if mid_hook is not None:
                mid_hook()
            stop_here("g4")
            V([Bw["ic"], Bpar], [Bw["e2"]], lambda h: h.tensor_scalar(out=t("e2")[:, 0:n], in0=t("ic")[:, 0:n],
                                                                    scalar1=-1.0, scalar2=pcol("ka", hp),
                                                                    op0=ALU.add, op1=ALU.mult))
            V([Bw["e2"], Bw["mk"]], [Bw["km"]], lambda h: h.scalar_tensor_tensor(
                out=t("km")[:, 0:n], in0=t("e2")[:, 0:n], scalar=1.0, in1=mk[:, 0:n], op0=ALU.add, op1=ALU.mult))
            G([Bw["kk"], Bw["ic"]], [Bw["b"]], lambda h: h.tensor_tensor(out=R(t("b")[:, 0:n]), in0=t("kk")[:, 0:n],
                                                                        in1=t("ic")[:, 0:n], op=ALU.mult))
            stop_here("g5")
            V([Bw["lw"], Bw["ones"]], [Bw["cum"]], lambda h: h.tensor_tensor_scan(
                out=t("cum")[:, 0:n], data0=wk["ones"][:, 0:n], data1=t("lw")[:, 0:n],
                initial=0.0, op0=ALU.mult, op1=ALU.add))
            cum3 = t("cum")[:, 0:n].rearrange("p (c s) -> p c s", s=C)
            lw3 = t("lw")[:, 0:n].rearrange("p (c s) -> p c s", s=C)
            off = t("off")
            V([Bw["cum"], Bw["lw"]], [Bw["off"]], lambda h: h.tensor_tensor(
                out=off[:, 0:nch], in0=cum3[:, :, 0], in1=lw3[:, :, 0], op=ALU.subtract))
            V([Bw["cum"], Bw["off"]], [Bw["cum"]], lambda h: h.tensor_tensor(
                out=cum3, in0=cum3, in1=off[:, 0:nch].unsqueeze(2).to_broadcast([128, nch, C]), op=ALU.subtract))
            stop_here("g6")
            A([Bw["cum"]], [Bw["ep"]], lambda h: h.activation(out=R(t("ep")[:, 0:n]), in_=t("cum")[:, 0:n], func=AF.Exp))
            A([Bw["cum"]], [Bw["en"]], lambda h: h.activation(out=t("en")[:, 0:n], in_=t("cum")[:, 0:n], func=AF.Exp,
                                                            scale=-1.0))
            ep3_ = t("ep")[:, 0:n].rearrange("p (c s) -> p c s", s=C)
            kk3_ = t("kk")[:, 0:n].rearrange("p (c s) -> p c s", s=C)
            at3_ = t("at")[:, 0:n].rearrange("p (c s) -> p c s", s=C)
            V([Bw["kk"], Bw["ep"]], [Bw["at"]], lambda h: h.scalar_tensor_tensor(
                out=R(at3_[:, :, 1:C]), in0=kk3_[:, :, 1:C], scalar=-1.0, in1=ep3_[:, :, 0:C - 1],
                op0=ALU.mult, op1=ALU.mult))
            V([Bw["kk"]], [Bw["at"]], lambda h: h.tensor_scalar(
                out=R(at3_[:, :, 0]), in0=kk3_[:, :, 0], scalar1=-1.0, scalar2=None, op0=ALU.mult))
            G([Bw["b"], Bw["en"]], [Bw["bt"]], lambda h: h.tensor_tensor(out=R(t("bt")[:, 0:n]), in0=t("b")[:, 0:n],
                                                                        in1=t("en")[:, 0:n], op=ALU.mult))
            G([Bw["km"], Bw["en"]], [Bw["kt"]], lambda h: h.tensor_tensor(out=R(t("kt")[:, 0:n]), in0=t("km")[:, 0:n],
                                                                         in1=t("en")[:, 0:n], op=ALU.mult))
            if mode != "pre":
                V([Bw["mr"], Bw["ep"]], [Bw["rt"]], lambda h: h.tensor_tensor(
                    out=R(t("rt")[:, 0:n]), in0=t("mr")[:, 0:n], in1=t("ep")[:, 0:n], op=ALU.mult))
            ep3 = t("ep")[:, 0:n].rearrange("p (c s) -> p c s", s=C)
            stop_here("g7")
            rr5a = [0]

            def misc5a():
                i = [6, 7, 3, 4, 5][rr5a[0] % 5]
                rr5a[0] += 1
                return banks[i], Bbank[i]
            TM = wk["TM" + str(C)]
            AM = wk["AM" + str(C)]
            for c in range(nch):
                cs = slice(c * C, (c + 1) * C)
                bk, Bbk = misc5a()
                for qi, srcn in enumerate(["mv", "bt", "kt"]):
                    T([Bw[srcn], Bpar], [Bbk], lambda h, qi=qi, srcn=srcn, bk=bk, cs=cs: h.transpose(
                        bk[0:C, qi * 128:(qi + 1) * 128], t(srcn)[:, cs], ident))
                A([Bbk], [Bw["TM"]], lambda h, bk=bk, c=c: h.activation(
                    out=R(TM[:, c, :, :]), in_=bk[0:C, 0:384].rearrange("p (q f) -> p q f", f=128), func=AF.Identity))
                for hd in range(2):
                    ps = slice(hd * 64, (hd + 1) * 64)
                    bk, Bbk = misc5a()
                    pairs = [("bt", "at"), ("kt", "at"), ("bt", "rt"), ("kt", "rt"), ("at", "bt")]
                    for qi, (l, r) in enumerate(pairs):
                        if mode == "pre" and qi in (2, 3):
                            continue
                        T([Bw[l], Bw[r]], [Bbk], lambda h, qi=qi, l=l, r=r, bk=bk, ps=ps, cs=cs: h.matmul(
                            bk[0:C, qi * C:(qi + 1) * C], lhsT=R(t(l)[ps, cs]), rhs=R(t(r)[ps, cs]), start=True, stop=True))
                    rng = ((0, 2), (4, 5)) if mode == "pre" else ((0, 5),)
                    for q0, q1 in rng:
                        V([Bbk, Bpar], [Bw["AM"]], lambda h, bk=bk, c=c, hd=hd, q0=q0, q1=q1: h.tensor_tensor(
                            out=R(AM[:, c, hd, q0:q1, :]),
                            in0=bk[0:C, q0 * C:q1 * C].rearrange("p (q s) -> p q s", s=C),
                            in1=masks[0:C, q0:q1, 0:C], op=ALU.mult))
            stop_here("g9")
            PM = wk["PM" + str(C)]
            NN = wk["NN" + str(C)]
            nlev = {64: 5, 4: 1}[C]
            ng = nch * 2
            V([Bw["AM"], Bpar], [Bw["PM"]], lambda h: h.tensor_tensor(
                out=R(PM[:, :, :, :].rearrange("p c h s -> p (c h) s")),
                in0=AM[:, :, :, 0, :].rearrange("p c h s -> p (c h) s"),
                in1=ident[0:C, 0:C].unsqueeze(1).to_broadcast([C, ng, C]), op=ALU.add))

            def Ncur(lev, which, c, hd):
                if lev % 2 == 0:
                    return AM[:, c, hd, 0 if which == 0 else 4, :]
                return NN[:, which, c, hd, :]

            def Nall(par, which):
                if par == 0:
                    return AM[:, :, :, 0 if which == 0 else 4, :].rearrange("p c h s -> p (c h) s")
                return NN[:, which, :, :, :].rearrange("p c h s -> p (c h) s")
            for lev in range(nlev):
                last = (lev == nlev - 1)
                for c in range(nch):
                    for hd in range(2):
                        idx = (c * 2 + hd) * C
                        if not last:
                            T([Bw["AM"], Bw["NN"]], [Bbank[3]], lambda h, lev=lev, c=c, hd=hd, idx=idx: h.matmul(
                                banks[3][0:C, idx:idx + C], lhsT=R(Ncur(lev, 1, c, hd)), rhs=R(Ncur(lev, 0, c, hd)),
                                start=True, stop=True))
                        T([Bw["AM"], Bw["NN"]], [Bbank[4]], lambda h, lev=lev, c=c, hd=hd, idx=idx: h.matmul(
                            banks[4][0:C, idx:idx + C], lhsT=R(Ncur(lev, 0, c, hd)), rhs=R(Ncur(lev, 1, c, hd)),
                            start=True, stop=True))
                nx = (lev + 1) % 2
                if not last:
                    A([Bbank[3]], [Bw["NN"], Bw["AM"]], lambda h, nx=nx: h.activation(
                        out=R(Nall(nx, 0)), in_=banks[3][0:C, 0:ng * C].rearrange("p (g s) -> p g s", s=C),
                        func=AF.Identity))
                V([Bbank[4]], [Bw["NN"], Bw["AM"]], lambda h, nx=nx: h.tensor_copy(
                    out=R(Nall(nx, 1)), in_=banks[4][0:C, 0:ng * C].rearrange("p (g s) -> p g s", s=C)))
                if lev_hook is not None:
                    lev_hook(lev, nlev)
                for c in range(nch):
                    for hd in range(2):
                        idx = (c * 2 + hd) * C
                        T([Bw["NN"], Bw["AM"], Bw["PM"]], [Bbank[5]], lambda h, lev=lev, c=c, hd=hd, idx=idx: h.matmul(
                            banks[5][0:C, idx:idx + C], lhsT=R(Ncur(lev + 1, 1, c, hd)), rhs=R(PM[:, c, hd, :]),
                            start=True, stop=True))
                V([Bbank[5], Bw["PM"]], [Bw["PM"]], lambda h: h.tensor_tensor(
                    out=R(PM[:, :, :, :].rearrange("p c h s -> p (c h s)")),
                    in0=PM[:, :, :, :].rearrange("p c h s -> p (c h s)"), in1=banks[5][0:C, 0:ng * C], op=ALU.add))
            stop_here("g10")
            rr5 = [0]

            def misc5():
                i = [3, 4, 5, 6, 7][rr5[0] % 5]
                rr5[0] += 1
                return banks[i], Bbank[i]
            sh_src = ["at", "ep"] if mode == "pre" else ["at", "ep", "rt"]
            for si, nm in enumerate(sh_src):
                T([Bw[nm], Bpar], [Bbank[3 + si]], lambda h, si=si, nm=nm: h.matmul(
                    banks[3 + si][0:64, 0:n], lhsT=R(identr[:, 64:128]), rhs=R(t(nm)[:, 0:n]), start=True, stop=True))
                A([Bbank[3 + si]], [Bw[nm + "1"]], lambda h, si=si, nm=nm: h.activation(
                    out=R(wk[nm + "1"][:, 0:n]), in_=banks[3 + si][0:64, 0:n], func=AF.Identity))
            ep13 = wk["ep1"][:, 0:n].rearrange("p (c s) -> p c s", s=C)

            def RY(hd, ap):
                return R(ap) if hd == 0 else ap

            def hsel(nm, hd):
                return t(nm)[0:64, :] if hd == 0 else wk[nm + "1"]
            for c in range(nch):
                cs = slice(c * C, (c + 1) * C)
                par = (c % 2) if mode == "smp" else 0
                if mode == "smp":
                    Hs, BHs_ = wk["HsS"][:, par, :, :], Bw["HsS%d" % par]
                else:
                    Hs, BHs_ = HsAll[:, hp, :, :], BHs
                Wsb, Usb, HsG = wk["Wsb"][:, par, :], wk["Usb"][:, par, :], wk["HsG"][:, par, :, :]
                BWsb, BUsb, BHsG = Bw["Wsb%d" % par], Bw["Usb%d" % par], Bw["HsG%d" % par]
                if mode == "smp":
                    bk, Bbk = misc5()
                    for hd in range(2):
                        T([BSin, Bpar], [Bbk], lambda h, bk=bk, c=c, hd=hd: h.transpose(
                            bk[0:64, hd * 64:(hd + 1) * 64], Sin[:, cbase + c, hd, :], ident[0:64, 0:64]))
                    A([Bbk], [BHs_], lambda h, bk=bk: h.activation(
                        out=R(Hs), in_=bk[0:64, 0:128].rearrange("p (h v) -> p h v", v=64), func=AF.Identity))
                gams = [ep3[0:64, c, C - 1:C], ep13[:, c, C - 1:C]]
                for hd in range(2):
                    A([BHs_, Bw["ep"], Bw["ep1"]], [BHsG], lambda h, hd=hd: h.activation(
                        out=HsG[:, hd, :], in_=Hs[:, hd, :], func=AF.Identity, scale=gams[hd]))
                bk, Bbk = misc5()
                for hd in range(2):
                    T([Bw["at"], Bw["at1"], BHs_], [Bbk], lambda h, bk=bk, hd=hd, cs=cs: h.matmul(
                        bk[0:C, hd * 64:(hd + 1) * 64], lhsT=R(hsel("at", hd)[:, cs]), rhs=R(Hs[:, hd, :]), start=True, stop=False))
                    T([Bw["AM"], Bw["TM"]], [Bbk], lambda h, bk=bk, hd=hd, c=c: h.matmul(
                        bk[0:C, hd * 64:(hd + 1) * 64], lhsT=R(AM[:, c, hd, 1, :]), rhs=R(TM[:, c, 0, hd * 64:(hd + 1) * 64]),
                        start=False, stop=True))
                A([Bbk], [BWsb], lambda h, bk=bk: h.activation(out=R(Wsb[0:C, :]), in_=bk[0:C, 0:128], func=AF.Identity))
                bk2, Bbk2 = misc5()
                for hd in range(2):
                    T([Bw["PM"], BWsb], [Bbk2], lambda h, bk2=bk2, hd=hd, c=c: h.matmul(
                        bk2[0:C, hd * 64:(hd + 1) * 64], lhsT=R(PM[:, c, hd, :]), rhs=R(Wsb[0:C, hd * 64:(hd + 1) * 64]),
                        start=True, stop=True))
                V([Bbk2], [BUsb], lambda h, bk2=bk2: h.tensor_copy(out=R(Usb[0:C, :]), in_=bk2[0:C, 0:128]))
                bk4, Bbk4 = misc5()
                for hd in range(2):
                    T([Bw["TM"], BUsb], [Bbk4], lambda h, bk4=bk4, hd=hd, c=c: h.matmul(
                        bk4[0:64, hd * 64:(hd + 1) * 64], lhsT=R(TM[:, c, 1, hd * 64:(hd + 1) * 64]),
                        rhs=R(Usb[0:C, hd * 64:(hd + 1) * 64]), start=True, stop=False))
                    T([Bw["TM"]], [Bbk4], lambda h, bk4=bk4, hd=hd, c=c: h.matmul(
                        bk4[0:64, hd * 64:(hd + 1) * 64], lhsT=R(TM[:, c, 2, hd * 64:(hd + 1) * 64]),
                        rhs=R(TM[:, c, 0, hd * 64:(hd + 1) * 64]), start=False, stop=True))
                if mode != "pre":
                    bk3, Bbk3 = misc5()
                    for hd in range(2):
                        ps = slice(hd * 64, (hd + 1) * 64)
                        T([BHs_, Bw["rt"], Bw["rt1"]], [Bbk3], lambda h, bk3=bk3, ps=ps, hd=hd, cs=cs: h.matmul(
                            bk3[ps, 0:C], lhsT=RY(hd, Hs[:, hd, :]), rhs=RY(hd, hsel("rt", hd)[:, cs]), start=True, stop=False))
                        T([BUsb, Bw["AM"]], [Bbk3], lambda h, bk3=bk3, ps=ps, hd=hd, c=c: h.matmul(
                            bk3[ps, 0:C], lhsT=RY(hd, Usb[0:C, hd * 64:(hd + 1) * 64]), rhs=RY(hd, AM[:, c, hd, 2, :]),
                            start=False, stop=False))
                        T([Bw["TM"], Bw["AM"]], [Bbk3], lambda h, bk3=bk3, ps=ps, hd=hd, c=c: h.matmul(
                            bk3[ps, 0:C], lhsT=RY(hd, TM[:, c, 0, hd * 64:(hd + 1) * 64]), rhs=RY(hd, AM[:, c, hd, 3, :]),
                            start=False, stop=True))
                    A([Bbk3], [Bw["yr"]], lambda h, bk3=bk3, cs=cs: h.activation(out=t("yr")[:, cs], in_=bk3[:, 0:C],
                                                                              func=AF.Identity))
                for hd in range(2):
                    V([Bbk4, BHsG, Bw["ep"], Bw["ep1"]], [BHs_], lambda h, bk4=bk4, hd=hd: h.scalar_tensor_tensor(
                        out=R(Hs[:, hd, :]), in0=bk4[0:64, hd * 64:(hd + 1) * 64], scalar=gams[hd], in1=HsG[:, hd, :],
                        op0=ALU.mult, op1=ALU.add))
                if mode == "smp":
                    bk5, Bbk5 = misc5()
                    for hd in range(2):
                        T([BHs_, Bpar], [Bbk5], lambda h, bk5=bk5, hd=hd: h.transpose(
                            bk5[0:64, hd * 64:(hd + 1) * 64], Hs[:, hd, :], ident[0:64, 0:64]))
                    A([Bbk5], [BSout], lambda h, bk5=bk5, c=c: h.activation(
                        out=Sout[:, cbase + c, :, :].rearrange("p h k -> p (h k)"), in_=bk5[0:64, 0:128], func=AF.Identity))
            stop_here("g11")
            if mode == "pre":
                return
            yr = t("yr")
            bkA, BbkA = misc()
            T([Bpar, Bw["yr"]], [BbkA], lambda h: h.matmul(bkA[:, 0:n], lhsT=bdmean, rhs=yr[:, 0:n],
                                                              start=True, stop=True))
            V([Bw["yr"], BbkA], [Bw["yr"]], lambda h: h.tensor_tensor(out=yr[:, 0:n], in0=yr[:, 0:n],
                                                                         in1=bkA[:, 0:n], op=ALU.subtract))
            bkB, BbkB = misc()
            G([Bw["yr"]], [Bw["e1"]], lambda h: h.tensor_tensor(out=t("e1")[:, 0:n], in0=yr[:, 0:n], in1=yr[:, 0:n],
                                                              op=ALU.mult))
            T([Bpar, Bw["e1"]], [BbkB], lambda h: h.matmul(bkB[:, 0:n], lhsT=bdmean, rhs=t("e1")[:, 0:n],
                                                              start=True, stop=True))
            V([BbkB], [Bw["e1"]], lambda h: h.tensor_scalar(out=t("e1")[:, 0:n], in0=bkB[:, 0:n],
                                                              scalar1=GN_EPS, scalar2=None, op0=ALU.add))
            rsqrt_inplace(t("e1")[:, 0:n], Bw["e1"])
            V([Bw["yr"], Bw["e1"]], [Bw["yr"]], lambda h: h.tensor_tensor(out=yr[:, 0:n], in0=yr[:, 0:n],
                                                                         in1=t("e1")[:, 0:n], op=ALU.mult))
            V([Bw["yr"], Bpar], [Bw["yr"]], lambda h: h.tensor_scalar(out=yr[:, 0:n], in0=yr[:, 0:n],
                                                                    scalar1=pcol("gg", hp), scalar2=pcol("gb", hp),
                                                                    op0=ALU.mult, op1=ALU.add))
            V([Bw["mr"], Bw["km"], Bpar], [Bw["e2"]], lambda h: h.scalar_tensor_tensor(
                out=t("e2")[:, 0:n], in0=t("mr")[:, 0:n], scalar=pcol("rk", hp), in1=t("km")[:, 0:n],
                op0=ALU.mult, op1=ALU.mult))
            bkC, BbkC = misc()
            T([Bpar, Bw["e2"]], [BbkC], lambda h: h.matmul(bkC[:, 0:n], lhsT=bdsum, rhs=t("e2")[:, 0:n],
                                                              start=True, stop=True))
            V([BbkC, Bw["mv"]], [Bw["e2"]], lambda h: h.tensor_tensor(out=t("e2")[:, 0:n], in0=bkC[:, 0:n],
                                                                         in1=mv[:, 0:n], op=ALU.mult))
            G([Bw["yr"], Bw["e2"]], [Bw["yr"]], lambda h: h.tensor_tensor(out=yr[:, 0:n], in0=yr[:, 0:n],
                                                                         in1=t("e2")[:, 0:n], op=ALU.add))
            e = wk["yst_rr"][0] % 2
            wk["yst_rr"][0] += 1
            yst, Byst = wk["yst"][e], Bw["yst%d" % e]
            V([Bw["yr"], Bw["gt"]], [Byst], lambda h: h.tensor_tensor(out=yst[:, 0:n], in0=yr[:, 0:n],
                                                                     in1=t("gt")[:, 0:n], op=ALU.mult))
            sy.dma(sy.sp, ysp[hp, :, ytok:ytok + n], yst[:, 0:n], rbuf=Byst)

        def alloc_work(stk, GW, nchp, with_smp):
            wk = {}
            Bw = {}
            for nm in ["pr0", "pk0", "pv0", "pr1", "pk1", "pv1", "d", "mr", "mk", "mv", "lw", "ic", "gt", "kk", "e1", "e2",
                       "km", "b", "rt", "ones"]:
                wk[nm] = sb(stk, "wk_" + nm, [128, GW + 1])
                Bw[nm] = sy.buf("wk_" + nm)
            for a_, b_ in [("yr", "d"), ("at", "mk"), ("bt", "ic"), ("kt", "b")]:
                wk[a_] = wk[b_]
                Bw[a_] = Bw[b_]
            wk["off"] = sb(stk, "wk_off", [128, max(nchp, NSQ)])
            Bw["off"] = sy.buf("wk_off")
            V([], [Bw["ones"]], lambda h: h.memset(wk["ones"][:], 1.0))
            wk["TM64"] = sb(stk, "TM64", [64, nchp, 3, 128])
            wk["AM64"] = sb(stk, "AM64", [64, nchp, 2, 5, 64])
            wk["PM64"] = sb(stk, "PM64", [64, nchp, 2, 64])
            wk["NN64"] = sb(stk, "NN64", [64, 2, nchp, 2, 64])
            if with_smp:
                wk["TM4"] = sb(stk, "TM4", [4, NSH, 3, 128])
                wk["AM4"] = sb(stk, "AM4", [4, NSH, 2, 5, 4])
                wk["PM4"] = sb(stk, "PM4", [4, NSH, 2, 4])
                wk["NN4"] = sb(stk, "NN4", [4, 2, NSH, 2, 4])
                wk["yst"] = [sb(stk, "yst%d" % i, [128, GW + 1], BF16) for i in range(2)]
                wk["yst_rr"] = [0]
                for i in range(2):
                    Bw["yst%d" % i] = sy.buf("yst%d" % i, dma=True)
            for nm in ["TM", "AM", "PM", "NN"]:
                Bw[nm] = sy.buf("wk_" + nm)
            for nm in ["Wsb", "Usb", "HsG", "HsS"]:
                for i in range(2):
                    Bw[nm + str(i)] = sy.buf("wk_%s%d" % (nm, i))
            wk["Wsb"] = sb(stk, "Wsb", [64, 2, 128])
            wk["Usb"] = sb(stk, "Usb", [64, 2, 128])
            wk["HsG"] = sb(stk, "HsG", [64, 2, 2, 64])
            wk["HsS"] = sb(stk, "HsS", [64, 2, 2, 64])
            for nm in ["at1", "ep1", "rt1"]:
                wk[nm] = sb(stk, "wk_" + nm, [64, GW + 1])
                Bw[nm] = sy.buf("wk_" + nm)
            wk["w2a2"] = sb(stk, "w2a2", [128, 128])
            wk["g2"] = sb(stk, "g2", [128, 128])
            Bw["lw_"] = sy.buf("lora_w", dma=True)
            return wk, Bw

        def load_lora_w(wk, Bw, hp):
            sy.dma(sy.sp, wk["w2a2"][:, :], w2a2_d[:, hp * 128:(hp + 1) * 128], wbuf=Bw["lw_"])
            sy.dma(sy.sp, wk["g2"][:, :], g2_d[:, hp * 128:(hp + 1) * 128], wbuf=Bw["lw_"])

        def lora_inputs(stk, ws, hT, BhT, NT, ntiles, prompt_n, has_smp, shT, BshT, stg):
            TS = cfg.TS
            lor = sb(stk, "lor", [128, ntiles, NT])
            Blor = sy.buf("lor")
            with ExitStack() as tmpstk:
                ptmp = sb(tmpstk, "lor_p", [128, NT])
                dtmp = sb(tmpstk, "lor_d", [128, NT])
                Bp, Bd = sy.buf("lor_p"), sy.buf("lor_d")
                blocks = split_blocks(0, NT)
                for i in range(ntiles):
                    Bsl, sl = ws.get()
                    mm_pass(Bsl, sl, KD, lambda kc, t0, nn: hT[:, kc, t0:t0 + nn], [BhT], blocks,
                            list(range(len(blocks))))
                    for bi, (t0, nn) in enumerate(blocks):
                        A([Bbank[bi]], [Bp], lambda h, bi=bi, t0=t0, nn=nn: h.activation(
                            out=ptmp[:, t0:t0 + nn], in_=banks[bi][:, 0:nn], func=AF.Identity))
                    stop_here("lmm")
                    ti = 3 * NHP + i
                    if stg is not None:
                        G([Bp], [stg["Bshp"]], lambda h, ti=ti: h.tensor_copy(out=stg["shp"][:, ti:ti + 1],
                                                                            in_=ptmp[:, SMP0 - 1:SMP0]))
                        G([Bp], [stg["Bshs"]], lambda h, ti=ti: h.tensor_copy(
                            out=stg["shs"][:, ti, :],
                            in_=ptmp[:, SMP0:NT].rearrange("p (b s) -> p b s", s=TS)[:, :, TS - 1]))
                    pn = prompt_n
                    V([Bp], [Bd], lambda h: h.tensor_tensor(out=dtmp[:, 1:pn], in0=ptmp[:, 0:pn - 1], in1=ptmp[:, 1:pn],
                                                           op=ALU.subtract))
                    V([Bp], [Bd], lambda h: h.tensor_scalar(out=dtmp[:, 0:1], in0=ptmp[:, 0:1], scalar1=-1.0,
                                                           scalar2=None, op0=ALU.mult))
                    if has_smp:
                        p3 = ptmp[:, pn:NT].rearrange("p (b s) -> p b s", s=TS)
                        d3 = dtmp[:, pn:NT].rearrange("p (b s) -> p b s", s=TS)
                        V([Bp], [Bd], lambda h, p3=p3, d3=d3: h.tensor_tensor(
                            out=d3[:, :, 1:TS], in0=p3[:, :, 0:TS - 1], in1=p3[:, :, 1:TS], op=ALU.subtract))
                        V([Bp, BshT], [Bd], lambda h, p3=p3, d3=d3, ti=ti: h.tensor_tensor(
                            out=d3[:, :, 0], in0=shT[:, ti, :], in1=p3[:, :, 0], op=ALU.subtract))
                    stop_here("l1")
                    V([Bp, Bd, Bpar], [Blor], lambda h, i=i, ti=ti: h.scalar_tensor_tensor(
                        out=lor[:, i, :], in0=dtmp[:, :], scalar=pcol("mu", ti), in1=ptmp[:, :],
                        op0=ALU.mult, op1=ALU.add))
                    stop_here("l2")
                    if i == 0:
                        A([Blor], [Blor], lambda h: h.activation(out=lor[0:64, 0, :], in_=lor[0:64, 0, :], func=AF.Tanh))
                    else:
                        A([Blor], [Blor], lambda h: h.activation(out=lor[:, 1, :], in_=lor[:, 1, :], func=AF.Sigmoid))
                sy.barrier()
            return lor, Blor

        SH = 2 if (NSQ % 2 == 0 and NSQ >= 4) else 1
        NSH = NSQ // SH
        NCHG = min(4, TP // 64)
        GTOK = NCHG * 64
        NGRP = TP // GTOK
        rcol = 2 * CW
        lcol = 2 * CW + 3 * RW

        def hp_specs(hp, names):
            off = {"r": 0, "k": RW, "v": 2 * RW}
            return [(w_in, 0, KD, rcol + off[x] + hp * 128) for x in names]

        for hp_ in range(NHP):
            V([Bpar], [BHs], lambda h, hp_=hp_: h.tensor_scalar(
                out=R(HsAll[:, hp_, :, :].rearrange("p h v -> p (h v)")), in0=ident[0:64, 0:128], scalar1=0.0,
                scalar2=None, op0=ALU.mult))
        identr = sb(top, "identr", [128, 128])
        V([Bpar], [Bpar], lambda h: h.tensor_copy(out=R(identr[:, :]), in_=ident))

        Bconv = sy.buf("wconv", dma=True)
        conv_jobs = []
        for j in range(KD):
            conv_jobs.append((wsc_o[j, :, :], wo, KD, j))
        for f in range(KF):
            conv_jobs.append((wsc_g[f, :, :], wg, KD, f))
            conv_jobs.append((wsc_u[f, :, :], wu, KD, f))
        for j in range(KD):
            conv_jobs.append((wsc_d[j, :, :], wd, KF, j))
        conv_next = [0]

        def bg_convert(n=2):
            for _ in range(n):
                if conv_next[0] >= len(conv_jobs):
                    return
                dst, W, kc, ct = conv_jobs[conv_next[0]]
                conv_next[0] += 1
                sy.dma(sy.pool, dst.rearrange("p (k c) -> p k c", c=128),
                       W[:, ct * 128:(ct + 1) * 128].rearrange("(k p) c -> p k c", p=128), wbuf=Bconv)

        def stop_here(tag):
            if getattr(cfg, "stop", None) == tag:
                sy.barrier()
                sy.finish()
                sy.dead = True

        try:
            with ExitStack() as ph:
                hTp = sb(ph, "hTp", [128, KD, TP], BF16)
                BhTp = sy.buf("hTp")
                with ExitStack() as pa:
                    rows = [(xp[i:i + n, :], n, i) for (i, n) in tok_tiles(TP)]
                    load_norm_transpose(pa, rows, "g1", hTp, BhTp)
                    sy.barrier()
                    stop_here("p1a")
                ws = WStream(sy, nc, ph, 4, "p1", 2)
                ws.bg = bg_convert
                ws.plan([(w_in, 0, KD, lcol)])
                for hp in range(NHP):
                    ws.plan(hp_specs(hp, ["k", "v"]))
                lor, Blor = lora_inputs(ph, ws, hTp, BhTp, TP, 1, TP, False, None, None, None)
                wk, Bw = alloc_work(ph, GTOK, NCHG, False)
                stop_here("p1w")
                stop_here("p1l")
                for hp in range(NHP):
                    load_lora_w(wk, Bw, hp)
                    W = {}
                    for x in ["k", "v"]:
                        W[x] = ws.get()
                    rwkv_proj("pre", hp, W, hTp, BhTp, 0, GTOK, False, wk, Bw, None, "0")
                    for g in range(NGRP):
                        hook = None
                        lhook = None
                        if g + 1 < NGRP:
                            def hook(g=g, W=W, hp=hp):
                                rwkv_proj("pre", hp, W, hTp, BhTp, (g + 1) * GTOK, GTOK, True, wk, Bw, None,
                                          str((g + 1) % 2), part="mm", sel=["k"])

                            def lhook(lev, nlev, g=g, W=W, hp=hp):
                                k0_ = KD * lev // nlev
                                k1_ = KD * (lev + 1) // nlev
                                rwkv_proj("pre", hp, W, hTp, BhTp, (g + 1) * GTOK, GTOK, True, wk, Bw, None,
                                          str((g + 1) % 2), part="mm", sel=["v"], kcr=(k0_, k1_))
                        rwkv_group("pre", hp, lor, Blor, g * GTOK, GTOK, 64, NCHG, g > 0, wk, Bw,
                                   None, None, str(g % 2), mid_hook=hook, lev_hook=lhook)
                        if g + 1 < NGRP:
                            rwkv_proj("pre", hp, W, hTp, BhTp, (g + 1) * GTOK, GTOK, True, wk, Bw, None,
                                      str((g + 1) % 2), part="ev")
                sy.barrier()
                stop_here("p1")

            with ExitStack() as ph:
                TS = cfg.TS
                hT = sb(ph, "hT", [128, KD, NT2], BF16)
                BhT = sy.buf("hT")
                with ExitStack() as pa:
                    rows = [(xp[TP - HALO + i:TP - HALO + i + n, :], n, i) for (i, n) in tok_tiles(HALO + TP)]
                    rows += [(xs[i:i + n, :], n, SMP0 + i) for (i, n) in tok_tiles(NTS)]
                    load_norm_transpose(pa, rows, "g1", hT, BhT)
                    sy.barrier()
                    stop_here("p2a")
                shT = sb(ph, "shT", [128, NST, NSQ])
                BshT = sy.buf("shT")
                stg = {"shp": sb(ph, "shp", [128, NST]), "Bshp": sy.buf("shp"),
                       "shs": sb(ph, "shs", [128, NST, NSQ]), "Bshs": sy.buf("shs")}
                with ExitStack() as pa:
                    ssh = sb(pa, "ssh", [NSQ, SHW])
                    Bssh = sy.buf("ssh", dma=True)
                    sy.dma(sy.sp, ssh[:], sshift[:, :], wbuf=Bssh)
                    for i0 in range(0, NST, 4):
                        nq = min(4, NST - i0)
                        bk, Bbk = misc()
                        for q in range(nq):
                            T([Bssh, Bpar], [Bbk], lambda h, q=q, i0=i0, bk=bk: h.transpose(
                                bk[:, q * NSQ:(q + 1) * NSQ], ssh[:, (i0 + q) * 128:(i0 + q + 1) * 128], ident[0:NSQ, 0:NSQ]))
                        A([Bbk], [BshT], lambda h, i0=i0, nq=nq, bk=bk: h.activation(
                            out=shT[:, i0:i0 + nq, :], in_=bk[:, 0:nq * NSQ].rearrange("p (q b) -> p q b", b=NSQ),
                            func=AF.Identity))
                    sy.barrier()
                with ExitStack() as pc_:
                    ws = WStream(sy, nc, pc_, 3, "p2c", 2)
                    ws.bg = bg_convert
                    for ct in range(KC):
                        ws.plan([(w_in, 0, KD, ct * 128), (w_in, 0, KD, CW + ct * 128)])
                    cwt = sb(pc_, "cwt", [128, KC * CONV_K])
                    Bcwt = sy.buf("cwt", dma=True)
                    sy.dma(sy.sp, cwt[:, :], cwc_d[:, :], wbuf=Bcwt)
                    extp = [sb(pc_, "extp%d" % i, [128, SMP0], BF16) for i in range(2)]
                    exts = [sb(pc_, "exts%d" % i, [128, NSQ, HALO + TS], BF16) for i in range(2)]
                    Bext = [sy.buf("ext%d" % i) for i in range(2)]
                    dgw = [sb(pc_, "dgw%d" % i, [128, CONV_K, 128], BF16) for i in range(2)]
                    Bdg = [sy.buf("dgw%d" % i) for i in range(2)]
                    sig = sb(pc_, "sig", [128, 512])
                    Bsig = sy.buf("sig")
                    cps = sb(pc_, "cps", [128, KC, HALO])
                    Bcps = sy.buf("cps")
                    css = sb(pc_, "css", [128, KC, NTS])
                    Bcss = sy.buf("css")
                    NG4 = -(-NSQ // 4)
                    sct = sb(pc_, "sct", [4 * HALO, NG4, 128])
                    Bsct = sy.buf("sct", dma=True)
                    zst = [sb(pc_, "zst%d" % i, [128, 512]) for i in range(2)]
                    Bzst = [sy.buf("zst%d" % i, dma=True) for i in range(2)]
                    zk = 0
                    gblocks = split_blocks(0, NT2)
                    nb_ = len(gblocks)
                    assert 2 * nb_ <= 6, nb_
                    cblocks = split_blocks(0, TP)
                    for ct in range(KC):
                        e = ct % 2
                        for g4 in range(NG4):
                            nb = min(4, NSQ - g4 * 4)
                            sy.dma(sy.sp, sct[0:nb * HALO, g4, :],
                                   sconv[g4 * 4 * HALO:(g4 * 4 + nb) * HALO, ct * 128:(ct + 1) * 128], wbuf=Bsct)
                        bk, Bbk = misc()
                        for g4 in range(NG4):
                            nb = min(4, NSQ - g4 * 4)
                            T([Bsct, Bpar], [Bbk], lambda h, g4=g4, nb=nb, bk=bk: h.transpose(
                                bk[:, g4 * 4 * HALO:(g4 * 4 + nb) * HALO], sct[0:nb * HALO, g4, :],
                                ident[0:nb * HALO, 0:nb * HALO]))
                        V([Bbk], [Bext[e]], lambda h, bk=bk, e=e: h.tensor_copy(
                            out=exts[e][:, :, 0:HALO], in_=bk[:, 0:NSQ * HALO].rearrange("p (b r) -> p b r", r=HALO)))
                        V([Bpar, Bcwt], [Bdg[e]], lambda h, e=e, ct=ct: h.tensor_tensor(
                            out=dgw[e][:, :, :], in0=ident.unsqueeze(1).to_broadcast([128, CONV_K, 128]),
                            in1=cwt[:, ct * CONV_K:(ct + 1) * CONV_K].unsqueeze(2).to_broadcast(
                                [128, CONV_K, 128]), op=ALU.mult))
                        Bsa, sla = ws.get()
                        Bsb, slb = ws.get()
                        mm_pass(Bsa, sla, KD, lambda kc, t0, nn: hT[:, kc, t0:t0 + nn], [BhT], gblocks, list(range(nb_)))
                        mm_pass(Bsb, slb, KD, lambda kc, t0, nn: hT[:, kc, t0:t0 + nn], [BhT], gblocks,
                                list(range(nb_, 2 * nb_)))
                        for bi, (t0, nn) in enumerate(gblocks):
                            A([Bbank[nb_ + bi]], [Bsig], lambda h, bi=bi, nn=nn: h.activation(
                                out=sig[:, 0:nn], in_=banks[nb_ + bi][:, 0:nn], func=AF.Sigmoid))
                            pe_ = min(t0 + nn, SMP0)
                            if pe_ > t0:
                                pn_ = pe_ - t0
                                V([Bbank[bi], Bsig], [Bext[e]], lambda h, bi=bi, t0=t0, pn_=pn_, e=e: h.tensor_tensor(
                                    out=extp[e][:, t0:t0 + pn_], in0=banks[bi][:, 0:pn_], in1=sig[:, 0:pn_], op=ALU.mult))
                                lo = max(t0, SMP0 - HALO)
                                if lo < pe_:
                                    V([Bbank[bi], Bsig], [Bcps], lambda h, bi=bi, t0=t0, lo=lo, ct=ct, pe_=pe_: h.tensor_tensor(
                                        out=cps[:, ct, lo - (SMP0 - HALO):pe_ - (SMP0 - HALO)],
                                        in0=banks[bi][:, lo - t0:pe_ - t0], in1=sig[:, lo - t0:pe_ - t0], op=ALU.mult))
                            if t0 + nn > SMP0:
                                so = max(t0, SMP0) - t0
                                assert max(t0, SMP0) == SMP0 and t0 + nn == NT2
                                V([Bbank[bi], Bsig], [Bext[e]], lambda h, bi=bi, so=so, nn=nn, e=e: h.tensor_tensor(
                                    out=exts[e][:, :, HALO:HALO + TS],
                                    in0=banks[bi][:, so:nn].rearrange("p (b s) -> p b s", s=TS),
                                    in1=sig[:, so:nn].rearrange("p (b s) -> p b s", s=TS), op=ALU.mult))
                                V([Bbank[bi], Bsig], [Bcss], lambda h, bi=bi, so=so, nn=nn, ct=ct: h.tensor_tensor(
                                    out=css[:, ct, :], in0=banks[bi][:, so:nn], in1=sig[:, so:nn], op=ALU.mult))
                        for bi, (t0, nn) in enumerate(cblocks):
                            bk, Bbk = misc()
                            for j in range(CONV_K):
                                T([Bdg[e], Bext[e]], [Bbk], lambda h, j=j, t0=t0, nn=nn, e=e, bk=bk: h.matmul(
                                    bk[:, 0:nn], lhsT=dgw[e][:, j, :], rhs=extp[e][:, t0 + j:t0 + j + nn],
                                    start=(j == 0), stop=(j == CONV_K - 1)))
                            q = zk % 2
                            zk += 1
                            A([Bbk, Bpar], [Bzst[q]], lambda h, nn=nn, ct=ct, bk=bk, q=q: h.activation(
                                out=zst[q][:, 0:nn], in_=bk[:, 0:nn], func=AF.Identity, bias=pcol("cb", ct)))
                            sy.dma(sy.sp, zfsp[ct, :, t0:t0 + nn], zst[q][:, 0:nn], rbuf=Bzst[q])
                        bk, Bbk = misc()
                        for j in range(CONV_K):
                            T([Bdg[e], Bext[e]], [Bbk], lambda h, j=j, e=e, bk=bk: h.matmul(
                                bk[:, 0:NTS].rearrange("p (b s) -> p b s", s=TS), lhsT=dgw[e][:, j, :],
                                rhs=exts[e][:, :, j:j + TS], start=(j == 0), stop=(j == CONV_K - 1)))
                        q = zk % 2
                        zk += 1
                        A([Bbk, Bpar], [Bzst[q]], lambda h, ct=ct, bk=bk, q=q: h.activation(
                            out=zst[q][:, 0:NTS], in_=bk[:, 0:NTS], func=AF.Identity, bias=pcol("cb", ct)))
                        sy.dma(sy.sp, zfsp[ct, :, TP:TP + NTS], zst[q][:, 0:NTS], rbuf=Bzst[q])
                    cvo = sb(pc_, "cvo", [HALO, CW])
                    Bcvo = sy.buf("cvo", dma=True)
                    cso = sb(pc_, "cso", [NTS, CW])
                    Bcso = sy.buf("cso", dma=True)
                    for ct in range(KC):
                        bk, Bbk = misc()
                        T([Bcps, Bpar], [Bbk], lambda h, ct=ct, bk=bk: h.transpose(bk[0:HALO, 0:128], cps[:, ct, :], ident))
                        T([Bcss, Bpar], [Bbk], lambda h, ct=ct, bk=bk: h.transpose(bk[0:NTS, 128:256], css[:, ct, :], ident))
                        V([Bbk], [Bcvo], lambda h, ct=ct, bk=bk: h.tensor_copy(out=cvo[:, ct * 128:(ct + 1) * 128],
                                                                              in_=bk[0:HALO, 0:128]))
                        V([Bbk], [Bcso], lambda h, ct=ct, bk=bk: h.tensor_copy(out=cso[:, ct * 128:(ct + 1) * 128],
                                                                              in_=bk[0:NTS, 128:256]))
                    sy.dma(sy.sp, convp_d[:, :], cvo[:], rbuf=Bcvo)
                    for b in range(NSQ):
                        sy.dma(sy.sp, convs_d[b * HALO + HALO - TS:(b + 1) * HALO, :], cso[b * TS:(b + 1) * TS, :], rbuf=Bcso)
                        sy.dma(sy.sp, convs_d[b * HALO:b * HALO + HALO - TS, :], sconv[b * HALO + TS:(b + 1) * HALO, :],
                               rbuf=Bcso)
                    Bzfsp_done = Bzst
                    sy.barrier()
                    stop_here("p2c")
                with ExitStack() as pl:
                    zfb = sb(pl, "zfb", [128, KC, 512])
                    Bzfb = sy.buf("zfb", dma=True)
                    sq = sb(pl, "sq", [128, 512])
                    Bsq = sy.buf("sq")
                    mean = sb(pl, "mean", [128, 512])
                    rstd = sb(pl, "rstd", [128, 512])
                    Bmr = sy.buf("meanrstd")
                    zt = [sb(pl, "zt%d" % i, [128, 512]) for i in range(2)]
                    Bzt = [sy.buf("zt%d" % i) for i in range(2)]
                    zo = [sb(pl, "zo%d" % i, [128, 512], BF16) for i in range(2)]
                    Bzo = [sy.buf("zo%d" % i, dma=True) for i in range(2)]
                    k = 0
                    for (t0, nn) in split_blocks(0, NB2):
                        for bz in Bzfsp_done:
                            sy._wait(sy.sp, id(bz), bz.dsem, bz.dcnt)
                        sy.dma(sy.sp, zfb[:, :, 0:nn], zfsp[:, :, t0:t0 + nn].rearrange("k p t -> p k t"), wbuf=Bzfb)
                        for ct in range(KC):
                            T([Bpar, Bzfb], [Bbank[0]], lambda h, ct=ct, nn=nn: h.matmul(
                                banks[0][:, 0:nn], lhsT=onesln, rhs=zfb[:, ct, 0:nn], start=(ct == 0), stop=(ct == KC - 1)))
                        for ct in range(KC):
                            A([Bzfb], [Bsq], lambda h, ct=ct, nn=nn: h.activation(
                                out=sq[:, 0:nn], in_=zfb[:, ct, 0:nn], func=AF.Square))
                            T([Bpar, Bsq], [Bbank[1]], lambda h, ct=ct, nn=nn: h.matmul(
                                banks[1][:, 0:nn], lhsT=onesln, rhs=sq[:, 0:nn], start=(ct == 0), stop=(ct == KC - 1)))
                        V([Bbank[0]], [Bmr], lambda h, nn=nn: h.tensor_copy(out=mean[:, 0:nn], in_=banks[0][:, 0:nn]))
                        V([Bmr], [Bsq], lambda h, nn=nn: h.tensor_tensor(out=sq[:, 0:nn], in0=mean[:, 0:nn],
                                                                       in1=mean[:, 0:nn], op=ALU.mult))
                        V([Bbank[1], Bsq], [Bmr], lambda h, nn=nn: h.tensor_tensor(out=rstd[:, 0:nn], in0=banks[1][:, 0:nn],
                                                                                 in1=sq[:, 0:nn], op=ALU.subtract))
                        V([Bmr], [Bmr], lambda h, nn=nn: h.tensor_scalar(out=rstd[:, 0:nn], in0=rstd[:, 0:nn], scalar1=LN_EPS,
                                                                       scalar2=None, op0=ALU.add))
                        rsqrt_inplace(rstd[:, 0:nn], Bmr)
                        for ct in range(KC):
                            e = k % 2
                            k += 1
                            V([Bzfb, Bmr], [Bzt[e]], lambda h, ct=ct, nn=nn, e=e: h.tensor_tensor(
                                out=zt[e][:, 0:nn], in0=zfb[:, ct, 0:nn], in1=mean[:, 0:nn], op=ALU.subtract))
                            G([Bzt[e], Bmr], [Bzt[e]], lambda h, nn=nn, e=e: h.tensor_tensor(
                                out=zt[e][:, 0:nn], in0=zt[e][:, 0:nn], in1=rstd[:, 0:nn], op=ALU.mult))
                            A([Bzt[e], Bpar], [Bzo[e]], lambda h, ct=ct, nn=nn, e=e: h.activation(
                                out=zo[e][:, 0:nn], in_=zt[e][:, 0:nn], func=AF.Silu, bias=pcol("cbeta", ct),
                                scale=pcol("cg", ct)))
                            sy.dma(sy.sp, zsp[ct, :, t0:t0 + nn], zo[e][:, 0:nn], rbuf=Bzo[e])
                    Bzsp_done = Bzo
                    sy.barrier()
                    stop_here("p2l")
                with ExitStack() as pr_:
                    ws = WStream(sy, nc, pr_, 4, "p2r", 3)
                    ws.bg = bg_convert
                    ws.plan([(w_in, 0, KD, lcol), (w_in, 0, KD, lcol + 128)])
                    for hp in range(NHP):
                        ws.plan(hp_specs(hp, ["r", "k", "v"]))
                    lor, Blor = lora_inputs(pr_, ws, hT, BhT, NT2, 2, SMP0, True, shT, BshT, stg)
                    wk, Bw = alloc_work(pr_, max(GTOK, NTS), NCHG, True)
                    Sin = sb(pr_, "Sin", [64, NSQ, 2, 64])
                    BSin = sy.buf("Sin", dma=True)
                    Sout, BSout = Sin, BSin
                    Hfin = sb(pr_, "Hfin", [64, 128])
                    BHfin = sy.buf("Hfin", dma=True)
                    for hp in range(NHP):
                        for hd in range(2):
                            sy.dma(sy.sp, Sin[:, :, hd, :], swkv[:, 2 * hp + hd, :, :].rearrange("b v k -> v b k"),
                                   wbuf=BSin)
                        load_lora_w(wk, Bw, hp)
                        W = {}
                        for x in ["r", "k", "v"]:
                            W[x] = ws.get()
                        GD = [("own", HALO + g * GTOK, GTOK, 64, NCHG, True, g * GTOK, 0) for g in range(NGRP)]
                        GD += [("smp", SMP0 + h_ * NSH * TS, NSH * TS, TS, NSH, False, TP + h_ * NSH * TS, h_ * NSH)
                               for h_ in range(SH)]

                        def do_proj(i, part="both", W=W, hp=hp, GD=GD, sel=None, kcr=None):
                            m_, tl_, n_, C_, nch_, hpv_, yt_, cb_ = GD[i]
                            rwkv_proj(m_, hp, W, hT, BhT, tl_, n_, hpv_, wk, Bw, stg, str(i % 2), cbase=cb_, nch=nch_,
                                      part=part, sel=sel, kcr=kcr)
                        do_proj(0)
                        for i in range(len(GD)):
                            hook = (lambda i=i: do_proj(i + 1, "mm", sel=["r", "k"])) if i + 1 < len(GD) else None
                            lhook = None
                            if i + 1 < len(GD):
                                def lhook(lev, nlev, i=i):
                                    do_proj(i + 1, "mm", sel=["v"], kcr=(KD * lev // nlev, KD * (lev + 1) // nlev))
                            m_, tl_, n_, C_, nch_, hpv_, yt_, cb_ = GD[i]
                            if m_ == "smp" and cb_ == 0:
                                bk, Bbk = misc()
                                for hd in range(2):
                                    T([BHs, Bpar], [Bbk], lambda h, bk=bk, hp=hp, hd=hd: h.transpose(
                                        bk[0:64, hd * 64:(hd + 1) * 64], HsAll[:, hp, hd, :], ident[0:64, 0:64]))
                                V([Bbk], [BHfin], lambda h, bk=bk: h.tensor_copy(out=Hfin[:, :], in_=bk[0:64, 0:128]))
                                sy.dma(sy.sp, wkvp_d[2 * hp:2 * hp + 2, :, :].rearrange("h v k -> v h k"),
                                       Hfin[:, :].rearrange("p (h k) -> p h k", k=64), rbuf=BHfin)
                            rwkv_group(m_, hp, lor, Blor, tl_, n_, C_, nch_, hpv_, wk, Bw, shT, BshT, str(i % 2),
                                       Sin=Sin, BSin=BSin, Sout=Sout, BSout=BSout, ytok=yt_, cbase=cb_, mid_hook=hook,
                                       lev_hook=lhook)
                            if i + 1 < len(GD):
                                do_proj(i + 1, "ev")
                        for hd in range(2):
                            sy.dma(sy.sp, wkvs_d[:, 2 * hp + hd, :, :].rearrange("b v k -> v b k"),
                                   Sout[:, :, hd, :], rbuf=BSout)
                    Bysp_done = [Bw["yst0"], Bw["yst1"]]
                    sy.barrier()
                    stop_here("p2r")
                with ExitStack() as pa:
                    sho = sb(pa, "sho", [NST, 128])
                    Bsho = sy.buf("sho", dma=True)
                    shso = sb(pa, "shso", [NSQ, SHW])
                    Bshso = sy.buf("shso", dma=True)
                    bk, Bbk = misc()
                    T([stg["Bshp"], Bpar], [Bbk], lambda h, bk=bk: h.transpose(bk[0:NST, 0:128], stg["shp"][:, :], ident))
                    V([Bbk], [Bsho], lambda h, bk=bk: h.tensor_copy(out=sho[:, :], in_=bk[0:NST, 0:128]))
                    sy.dma(sy.sp, shiftp_d[:, :], sho[:, :], rbuf=Bsho)
                    for i0 in range(0, NST, 4):
                        nq = min(4, NST - i0)
                        bk, Bbk = misc()
                        for q in range(nq):
                            T([stg["Bshs"], Bpar], [Bbk], lambda h, q=q, i0=i0, bk=bk: h.transpose(
                                bk[0:NSQ, q * 128:(q + 1) * 128], stg["shs"][:, i0 + q, :], ident))
                        V([Bbk], [Bshso], lambda h, i0=i0, nq=nq, bk=bk: h.tensor_copy(
                            out=shso[:, i0 * 128:(i0 + nq) * 128], in_=bk[0:NSQ, 0:nq * 128]))
                    sy.dma(sy.sp, shifts_d[:, :], shso[:, :], rbuf=Bshso)
                    sy.barrier()
                with ExitStack() as pm:
                    ws = WStream(sy, nc, pm, 4, "p2m", 2)
                    ws.bg = bg_convert
                    gcol = 2 * CW + SHW
                    for j in range(KD):
                        ws.plan([(wco, 0, KC, j * 128), (w_in, 0, KD, gcol + j * 128), (wro, 0, NHP, j * 128),
                                 (w_in, 0, KD, gcol + D + j * 128)])
                    zT = sb(pm, "zT", [128, KC, NB2], BF16)
                    BzT = sy.buf("zT", dma=True)
                    yT = sb(pm, "yT", [128, NHP, NB2], BF16)
                    ByT = sy.buf("yT", dma=True)
                    for bz in Bzsp_done + Bysp_done:
                        sy._wait(sy.sp, id(bz), bz.dsem, bz.dcnt)
                    sy.dma(sy.sp, zT[:, :, :], zsp[:, :, :].rearrange("k p t -> p k t"), wbuf=BzT)
                    sy.dma(sy.sp, yT[:, :, :], ysp[:, :, :].rearrange("k p t -> p k t"), wbuf=ByT)
                    mblocks = split_blocks(0, NB2)
                    nbm = len(mblocks)
                    assert nbm <= 3
                    gc = [sb(pm, "gc%d" % i, [128, 512]) for i in range(2)]
                    Bgc = [sy.buf("gc%d" % i) for i in range(2)]
                    mtmp = sb(pm, "mtmp", [128, NB2])
                    Bmt = sy.buf("mtmp")
                    m2 = [sb(pm, "m2_%d" % i, [128, 512]) for i in range(2)]
                    Bm2 = [sy.buf("m2_%d" % i) for i in range(2)]
                    mo = [sb(pm, "mo%d" % i, [128, NB2], BF16) for i in range(2)]
                    Bmo = [sy.buf("mo%d" % i, dma=True) for i in range(2)]
                    k = 0
                    for j in range(KD):
                        e = j % 2
                        for br in range(2):
                            Bs1, s1 = ws.get()
                            Bs3, s3 = ws.get()
                            if br == 0:
                                mm_pass(Bs1, s1, KC, lambda kc, t0, nn: zT[:, kc, t0:t0 + nn], [BzT], mblocks,
                                        list(range(nbm)))
                            else:
                                mm_pass(Bs1, s1, NHP, lambda kc, t0, nn: yT[:, kc, t0:t0 + nn], [ByT], mblocks,
                                        list(range(nbm)))
                            mm_pass(Bs3, s3, KD, lambda kc, t0, nn: hT[:, kc, HALO + t0:HALO + t0 + nn], [BhT], mblocks,
                                    list(range(3, 3 + nbm)))
                            for bi, (t0, nn) in enumerate(mblocks):
                                q = k % 2
                                k += 1
                                A([Bbank[3 + bi], Bpar], [Bgc[q]], lambda h, bi=bi, nn=nn, q=q, j=j, br=br: h.activation(
                                    out=gc[q][:, 0:nn], in_=banks[3 + bi][:, 0:nn], func=AF.Sigmoid,
                                    bias=pcol("bg", br * KD + j)))
                                if br == 0:
                                    V([Bbank[bi], Bgc[q]], [Bmt], lambda h, bi=bi, nn=nn, q=q, t0=t0: h.tensor_tensor(
                                        out=mtmp[:, t0:t0 + nn], in0=banks[bi][:, 0:nn], in1=gc[q][:, 0:nn], op=ALU.mult))
                                else:
                                    V([Bbank[bi], Bgc[q]], [Bm2[q]], lambda h, bi=bi, nn=nn, q=q: h.tensor_tensor(
                                        out=m2[q][:, 0:nn], in0=banks[bi][:, 0:nn], in1=gc[q][:, 0:nn], op=ALU.mult))
                                    G([Bm2[q], Bmt], [Bmo[e]], lambda h, nn=nn, q=q, t0=t0, e=e: h.tensor_tensor(
                                        out=mo[e][:, t0:t0 + nn], in0=m2[q][:, 0:nn], in1=mtmp[:, t0:t0 + nn], op=ALU.add))
                        sy.dma(sy.sp, mixsp[j, :, :], mo[e][:, :], rbuf=Bmo[e])
                    Bmix_done = Bmo
                    sy.barrier()
                    stop_here("p2m")

            with ExitStack() as ph:
                tiles = tok_tiles(TP) + [(TP + i, n) for (i, n) in tok_tiles(NTS)]
                TB = 3
                blocks = [tiles[i:i + TB] for i in range(0, len(tiles), TB)]
                BTM = TB * 128
                xb = [sb(ph, "xb%d" % i, [128, D]) for i in range(TB)]
                Bxb = [sy.buf("xb%d" % i, dma=True) for i in range(TB)]
                mfraw = sb(ph, "mfraw", [128, TB * D])
                mf = [mfraw[:, i * D:(i + 1) * D] for i in range(TB)]
                Bmf = [sy.buf("mf%d" % i, dma=True) for i in range(TB)]
                assert (TB - 1) * D * 4 >= KD * BTM * 2
                h2T = mfraw[:, D:TB * D].bitcast(BF16)[:, 0:KD * BTM].rearrange("p (k t) -> p k t", t=BTM)
                actraw = sb(ph, "actraw", [128, max(KF, KD) * BTM], BF16)
                Bact = sy.buf("act", dma=True)
                actT = actraw[:, 0:KF * BTM].rearrange("p (k t) -> p k t", t=BTM)
                mixT = actraw[:, 0:KD * BTM].rearrange("p (k t) -> p k t", t=BTM)
                tq = min(2, KD)
                tmpT = sb(ph, "tmpT", [128, tq, BTM])
                BtmpT = sy.buf("tmpT")
                sqT = sb(ph, "sqT", [128, BTM])
                BsqT = sy.buf("sqT")
                sg = [sb(ph, "sg%d" % i, [128, BTM]) for i in range(2)]
                Bsg = [sy.buf("sg%d" % i) for i in range(2)]
                rsm = sb(ph, "rsm", [128, 4])
                Brsm = sy.buf("rsm")
                ws = WStream(sy, nc, ph, 3, "p3", 2)
                kfs = []
                o_ = 0
                while o_ < KF:
                    kfs.append((o_, min(32, KF - o_)))
                    o_ += 32
                nsl = len(kfs)
                base = KF // nsl
                kfs = []
                o_ = 0
                for i in range(nsl):
                    sz = base + (1 if i < KF % nsl else 0)
                    kfs.append((o_, sz))
                    o_ += sz
                bg_convert(len(conv_jobs))
                sy._wait(sy.pool, id(Bconv), Bconv.dsem, Bconv.dcnt)
                for blk in blocks:
                    for j in range(KD):
                        ws.plan([(None, wsc_o[j, :, :], KD, None)])
                    for f in range(KF):
                        ws.plan([(None, wsc_g[f, :, :], KD, None), (None, wsc_u[f, :, :], KD, None)])
                    for j in range(KD):
                        ws.plan([(None, wsc_d[j, :, k0 * 128:(k0 + kn) * 128], kn, None) for (k0, kn) in kfs])
                for bz in Bmix_done:
                    sy._wait(sy.sp, id(bz), bz.dsem, bz.dcnt)
                obank = [0]

                def proj_to_tokmajor(blk, BT, slabs_fn, rhsT, Brhs, gname):
                    for j in range(KD):
                        bi = obank[0] % 3
                        obank[0] += 1
                        sl_list = slabs_fn(j)
                        for si, (k0, kn) in enumerate(sl_list):
                            Bsl, sl = ws.get()
                            for kc in range(kn):
                                T([Bsl] + Brhs, [Bbank[bi]], lambda h, kc=kc, k0=k0, sl=sl, bi=bi, si=si, kn=kn: h.matmul(
                                    banks[bi][:, 0:BT], lhsT=sl[:, kc, :], rhs=rhsT[:, k0 + kc, 0:BT],
                                    start=(si == 0 and kc == 0), stop=(si == len(sl_list) - 1 and kc == kn - 1)))
                        jj = j % tq
                        A([Bbank[bi], Bpar], [BtmpT], lambda h, bi=bi, jj=jj, j=j: h.activation(
                            out=tmpT[:, jj, 0:BT], in_=banks[bi][:, 0:BT], func=AF.Identity, scale=pcol(gname, j)))
                        A([Bbank[bi]], [BsqT], lambda h, bi=bi: h.activation(out=sqT[:, 0:BT], in_=banks[bi][:, 0:BT],
                                                                           func=AF.Square))
                        off = 0
                        for i, (tk0, n) in enumerate(blk):
                            T([BsqT, Bpar], [Bbank[4]], lambda h, i=i, off=off, n=n, j=j: h.matmul(
                                banks[4][0:n, i:i + 1], lhsT=sqT[:, off:off + n], rhs=onescol,
                                start=(j == 0 and i == 0), stop=(j == KD - 1), skip_group_check=True))
                            off += n
                        if jj == tq - 1:
                            j0 = j - (tq - 1)
                            off = 0
                            for i, (tk0, n) in enumerate(blk):
                                bk, Bbk = misc()
                                for q in range(tq):
                                    T([BtmpT, Bpar], [Bbk], lambda h, q=q, off=off, n=n, bk=bk: h.transpose(
                                        bk[0:n, q * 128:(q + 1) * 128], tmpT[:, q, off:off + n], ident))
                                V([Bbk], [Bmf[i]], lambda h, i=i, n=n, j0=j0, bk=bk: h.tensor_copy(
                                    out=mf[i][0:n, j0 * 128:(j0 + tq) * 128], in_=bk[0:n, 0:tq * 128]))
                                off += n

                for blk in blocks:
                    b0 = blk[0][0]
                    BT = sum(n for (_, n) in blk)
                    sy.dma(sy.sp, mixT[:, :, 0:BT], mixsp[:, :, b0:b0 + BT].rearrange("k p t -> p k t"), wbuf=Bact)
                    for i, (tk0, n) in enumerate(blk):
                        src = xp[TP + tk0:TP + tk0 + n, :] if tk0 < TP else xs[tk0 - TP:tk0 - TP + n, :]
                        sy.dma(sy.sp, xb[i][0:n, :], src, wbuf=Bxb[i])
                    proj_to_tokmajor(blk, BT, lambda j: [(0, KD)], mixT, [Bact], "g2")
                    for i, (tk0, n) in enumerate(blk):
                        V([Bbank[4]], [Brsm], lambda h, i=i, n=n: h.tensor_scalar(
                            out=rsm[0:n, i:i + 1], in0=banks[4][0:n, i:i + 1], scalar1=1.0 / D, scalar2=RMS_EPS,
                            op0=ALU.mult, op1=ALU.add))
                        rsqrt_inplace(rsm[0:n, i:i + 1], Brsm)
                        V([Bmf[i], Brsm, Bxb[i]], [Bxb[i]], lambda h, i=i, n=n: h.scalar_tensor_tensor(
                            out=xb[i][0:n, :], in0=mf[i][0:n, :], scalar=rsm[0:n, i:i + 1], in1=xb[i][0:n, :],
                            op0=ALU.mult, op1=ALU.add))
                    off = 0
                    for i, (tk0, n) in enumerate(blk):
                        norm_transpose(xb[i][0:n, :], Bxb[i], n, "g3", h2T, Bmf[1], off, mf[0], Bmf[0], mf[0], Bmf[0],
                                       extra_w=[Bmf[2]] if TB > 2 else [])
                        off += n
                    pk = 0
                    for f in range(KF):
                        Bsg_, slg = ws.get()
                        Bsu_, slu = ws.get()
                        ba, bb = (0, 1) if pk % 2 == 0 else (2, 3)
                        q = pk % 2
                        pk += 1
                        for kc in range(KD):
                            T([Bsg_, Bmf[1], Bmf[2 if TB > 2 else 1]], [Bbank[ba]], lambda h, kc=kc, slg=slg, ba=ba: h.matmul(
                                banks[ba][:, 0:BT], lhsT=slg[:, kc, :], rhs=h2T[:, kc, 0:BT], start=(kc == 0),
                                stop=(kc == KD - 1)))
                        for kc in range(KD):
                            T([Bsu_, Bmf[1], Bmf[2 if TB > 2 else 1]], [Bbank[bb]], lambda h, kc=kc, slu=slu, bb=bb: h.matmul(
                                banks[bb][:, 0:BT], lhsT=slu[:, kc, :], rhs=h2T[:, kc, 0:BT], start=(kc == 0),
                                stop=(kc == KD - 1)))
                        A([Bbank[ba]], [Bsg[q]], lambda h, ba=ba, q=q: h.activation(out=sg[q][:, 0:BT], in_=banks[ba][:, 0:BT],
                                                                                  func=AF.Silu))
                        V([Bbank[bb], Bsg[q]], [Bact], lambda h, bb=bb, q=q, f=f: h.tensor_tensor(
                            out=actT[:, f, 0:BT], in0=banks[bb][:, 0:BT], in1=sg[q][:, 0:BT], op=ALU.mult))
                    proj_to_tokmajor(blk, BT, lambda j: kfs, actT, [Bact], "g4")
                    for i, (tk0, n) in enumerate(blk):
                        V([Bbank[4]], [Brsm], lambda h, i=i, n=n: h.tensor_scalar(
                            out=rsm[0:n, i:i + 1], in0=banks[4][0:n, i:i + 1], scalar1=1.0 / D, scalar2=RMS_EPS,
                            op0=ALU.mult, op1=ALU.add))
                        rsqrt_inplace(rsm[0:n, i:i + 1], Brsm)
                        V([Bmf[i], Brsm, Bxb[i]], [Bmf[i]], lambda h, i=i, n=n: h.scalar_tensor_tensor(
                            out=mf[i][0:n, :], in0=mf[i][0:n, :], scalar=rsm[0:n, i:i + 1], in1=xb[i][0:n, :],
                            op0=ALU.mult, op1=ALU.add))
                        sy.dma(sy.sp, y_d[tk0:tk0 + n, :], mf[i][0:n, :], rbuf=Bmf[i])
        except StopBuild:
            pass
        sy.finish()
        print("kernel built: %d instructions, %d semaphores" % (sy.ninst, sy.nsem), flush=True)
    return nc


_CACHE = {}


def run(cfg, I):
    if "nc" not in _CACHE or _CACHE.get("key") != (cfg.D, cfg.SEQ, cfg.BATCH, cfg.DEC_BATCH):
        _CACHE["nc"] = build(cfg)
        _CACHE["key"] = (cfg.D, cfg.SEQ, cfg.BATCH, cfg.DEC_BATCH)
    nc = _CACHE["nc"]
    f32 = np.float32
    TP, NSQ = cfg.TP, cfg.NSQ
    xpr = np.asarray(I["x_prompt"], f32)
    xsm = np.asarray(I["x_sample"], f32)
    shared = {
        "w_in": np.ascontiguousarray(np.asarray(I["w_in"][0], f32)),
        "wco": np.ascontiguousarray(np.asarray(I["w_conv_out"][0], f32)),
        "wro": np.ascontiguousarray(np.asarray(I["w_rwkv_out"][0], f32)),
        "wo": np.ascontiguousarray(np.asarray(I["w_o"][0], f32)),
        "wg": np.ascontiguousarray(np.asarray(I["w_ffn_gate"][0], f32)),
        "wu": np.ascontiguousarray(np.asarray(I["w_ffn_up"][0], f32)),
        "wd": np.ascontiguousarray(np.asarray(I["w_ffn_down"][0], f32)),
        "pcols": make_pcols(cfg, I),
        "consts": make_consts(cfg),
        "w2a2": np.ascontiguousarray(np.concatenate([np.asarray(I["w2"][0], f32), np.asarray(I["a2"][0], f32)], axis=0)),
        "g2": np.ascontiguousarray(np.asarray(I["g2"][0], f32)),
        "cwc": make_cwcols(cfg, I),
    }
    in_maps = []
    for c in range(cfg.NCORES):
        b, half = c // 2, c % 2
        if half == 0:
            xpc = np.concatenate([np.zeros((TP, cfg.D), f32), xpr[b, :TP]], axis=0)
        else:
            xpc = xpr[b]
        sl = slice(c * NSQ, (c + 1) * NSQ)
        m = dict(shared)
        m["xp"] = np.ascontiguousarray(xpc)
        m["xs"] = np.ascontiguousarray(xsm[sl].reshape(NSQ * cfg.TS, cfg.D))
        m["swkv"] = np.ascontiguousarray(np.asarray(I["state_wkv"][0][sl], f32))
        m["sconv"] = np.ascontiguousarray(np.asarray(I["state_conv"][0][sl], f32).reshape(NSQ * HALO, cfg.CW))
        m["sshift"] = np.ascontiguousarray(np.asarray(I["state_shift"][0][sl], f32))
        in_maps.append(m)
    res = run_bass_kernel_spmd(nc, in_maps, core_ids=list(range(cfg.NCORES)))
    R = res.results
    B = cfg.BATCH
    y_p = np.zeros((B, cfg.SEQ, cfg.D), f32)
    y_s = np.zeros((cfg.DEC_BATCH, cfg.TS, cfg.D), f32)
    wkv_p = np.zeros((1, B, cfg.NH, 64, 64), f32)
    conv_p = np.zeros((1, B, HALO, cfg.CW), f32)
    shift_p = np.zeros((1, B, cfg.SHW), f32)
    wkv_s = np.zeros((1, cfg.DEC_BATCH, cfg.NH, 64, 64), f32)
    conv_s = np.zeros((1, cfg.DEC_BATCH, HALO, cfg.CW), f32)
    shift_s = np.zeros((1, cfg.DEC_BATCH, cfg.SHW), f32)
    for c in range(cfg.NCORES):
        b, half = c // 2, c % 2
        r = R[c]
        y_p[b, half * TP:(half + 1) * TP] = r["y"][:TP]
        sl = slice(c * NSQ, (c + 1) * NSQ)
        y_s[sl] = r["y"][TP:].reshape(NSQ, cfg.TS, cfg.D)
        if half == 1:
            wkv_p[0, b] = r["wkvp"]
            conv_p[0, b] = r["convp"]
            shift_p[0, b] = r["shiftp"].reshape(-1)
        wkv_s[0, sl] = r["wkvs"]
        conv_s[0, sl] = r["convs"].reshape(NSQ, HALO, cfg.CW)
        shift_s[0, sl] = r["shifts"]
    return (y_p, y_s, wkv_p, conv_p, shift_p, wkv_s, conv_s, shift_s)


def kernel(**inputs):
    cfg = Cfg()
    return run(cfg, inputs)
```

```python
import numpy as np
from contextlib import ExitStack
import concourse.bass as bass
import concourse.mybir as mybir
from concourse.bass_utils import run_bass_kernel_spmd

F32 = mybir.dt.float32
BF16 = mybir.dt.bfloat16
F32R = mybir.dt.float32r


def R(ap):
    return ap.bitcast(F32R)
AF = mybir.ActivationFunctionType
ALU = mybir.AluOpType

RMS_EPS = 1e-6
LN_EPS = 1e-5
GN_EPS = 64e-5
CONV_K = 31
HALO = CONV_K - 1
SCRATCH_KIND = "Internal"


class Cfg:
    def __init__(s, D=4096, SEQ=2048, BATCH=4, DEC_BATCH=128, DEC_SEQ=4):
        s.D = D
        s.SEQ = SEQ
        s.BATCH = BATCH
        s.DEC_BATCH = DEC_BATCH
        s.NCORES = 2 * BATCH
        s.CW = D // 2
        s.RW = D // 2
        s.NH = s.RW // 64
        s.NHP = s.RW // 128
        s.SHW = 3 * s.RW + 256
        s.PT = 2 * s.CW + s.SHW + 2 * D
        s.DFF = -(-8 * D // 768) * 256
        s.KD = D // 128
        s.KC = s.CW // 128
        s.KF = s.DFF // 128
        s.NST = s.SHW // 128
        s.TP = SEQ // 2
        s.NSQ = DEC_BATCH // s.NCORES
        s.TS = DEC_SEQ
        s.NTS = s.NSQ * s.TS
        s.NT2 = HALO + s.TP + s.NTS
        s.SMP0 = HALO + s.TP
        s.NB2 = s.TP + s.NTS
        o = 0
        s.pc = {}
        for name, n in [("g1", s.KD), ("g2", s.KD), ("g3", s.KD), ("g4", s.KD), ("bg", 2 * s.KD),
                        ("cb", s.KC), ("cg", s.KC), ("cbeta", s.KC), ("mu", s.NST),
                        ("w0", s.NHP), ("a0", s.NHP), ("kk", s.NHP), ("ka", s.NHP), ("rk", s.NHP),
                        ("gg", s.NHP), ("gb", s.NHP)]:
            s.pc[name] = o
            o += n
        s.NPC = o
        s.cc = {"ident": 0, "bdmean": 128, "bdsum": 256, "onesln": 384, "masks": 512, "onescol": 512 + 320}
        s.NCC = 512 + 320 + 1


def _cols(v):
    v = np.asarray(v, np.float32).reshape(-1, 128)
    return np.ascontiguousarray(v.T)


def make_pcols(cfg, I):
    parts = [_cols(I["ln_mix_pre"][0]), _cols(I["ln_mix_post"][0]), _cols(I["ln_ffn_pre"][0]),
             _cols(I["ln_ffn_post"][0]), _cols(I["b_gate"][0].reshape(-1)),
             _cols(I["conv_b"][0]), _cols(I["conv_ln_g"][0]), _cols(I["conv_ln_b"][0]),
             _cols(I["shift_mu"][0]), _cols(I["w0"][0]), _cols(I["a0"][0]), _cols(I["k_k"][0]),
             _cols(I["k_a"][0]), _cols(I["r_k"][0].reshape(-1)), _cols(I["gn_g"][0]), _cols(I["gn_b"][0])]
    out = np.ascontiguousarray(np.concatenate(parts, axis=1), dtype=np.float32)
    assert out.shape == (128, cfg.NPC), (out.shape, cfg.NPC)
    return out


def make_cwcols(cfg, I):
    cw = np.asarray(I["conv_w"][0], np.float32)
    return np.ascontiguousarray(cw.reshape(CONV_K, cfg.KC, 128).transpose(2, 1, 0).reshape(128, cfg.KC * CONV_K))


def make_consts(cfg):
    c = np.zeros((128, cfg.NCC), np.float32)
    c[:, 0:128] = np.eye(128)
    bd = np.zeros((128, 128), np.float32)
    bd[:64, :64] = 1.0
    bd[64:, 64:] = 1.0
    c[:, 128:256] = bd / 64.0
    c[:, 256:384] = bd
    c[:, 384:512] = 1.0 / cfg.CW
    s = np.arange(64)[:, None]
    t = np.arange(64)[None, :]
    strictT = (s < t).astype(np.float32)
    inclT = (s <= t).astype(np.float32)
    strictL = (t < s).astype(np.float32)
    m = np.stack([strictT, strictT, inclT, inclT, strictL], axis=1)
    c[:64, 512:512 + 320] = m.reshape(64, 320)
    c[:, 512 + 320] = 1.0
    return c


class StopBuild(Exception):
    pass


class Eng:
    def __init__(s, name, h, sem):
        s.name, s.h, s.sem, s.cnt, s.seen = name, h, sem, 0, {}


class Buf:
    def __init__(s, name, dsem=None):
        s.name, s.w, s.rd, s.dsem, s.dcnt = name, {}, {}, dsem, 0


class Sy:
    def __init__(s, nc, stack):
        s.nc, s.stack = nc, stack
        s.nsem = 0
        s.pe = Eng("pe", nc.tensor, s.newsem("pe"))
        s.act = Eng("act", nc.scalar, s.newsem("act"))
        s.dve = Eng("dve", nc.vector, s.newsem("dve"))
        s.pool = Eng("pool", nc.gpsimd, s.newsem("pool"))
        s.sp = Eng("sp", nc.sync, None)
        s.dbufs = []
        s.ninst = 0
        s.dead = False

    def newsem(s, name):
        s.nsem += 1
        return s.stack.enter_context(s.nc.semaphore("s%d_%s" % (s.nsem, name)))

    def buf(s, name, dma=False):
        b = Buf(name, s.newsem("d_" + name) if dma else None)
        if dma:
            s.dbufs.append(b)
        return b

    def _wait(s, e, key, sem, val):
        if s.dead:
            return
        if e.seen.get(key, 0) < val:
            e.h.wait_ge(sem, val)
            e.seen[key] = val

    def _deps(s, e, reads, writes):
        for b in reads:
            for e2, c in b.w.items():
                if not (e2 is e and e is s.pe):
                    s._wait(e, e2.name, e2.sem, c)
            if b.dcnt:
                s._wait(e, id(b), b.dsem, b.dcnt)
        for b in writes:
            for e2, c in b.w.items():
                if not (e2 is e and e is s.pe):
                    s._wait(e, e2.name, e2.sem, c)
            for e2, c in b.rd.items():
                if not (e2 is e and e is s.pe):
                    s._wait(e, e2.name, e2.sem, c)
            if b.dcnt:
                s._wait(e, id(b), b.dsem, b.dcnt)

    def op(s, e, reads, writes, fn):
        if s.dead:
            return None
        s._deps(e, reads, writes)
        ins = fn(e.h)
        e.cnt += 1
        ins.then_inc(e.sem, 1)
        for b in reads:
            b.rd[e] = e.cnt
        for b in writes:
            b.w[e] = e.cnt
        s.ninst += 1
        return ins

    def T(s, r, w, fn):
        return s.op(s.pe, r, w, fn)

    def A(s, r, w, fn):
        return s.op(s.act, r, w, fn)

    def V(s, r, w, fn):
        return s.op(s.dve, r, w, fn)

    def G(s, r, w, fn):
        return s.op(s.pool, r, w, fn)

    def dma(s, q, out, in_, wbuf=None, rbuf=None, **kw):
        if s.dead:
            return None
        b = wbuf if wbuf is not None else rbuf
        s._deps(q, [rbuf] if rbuf is not None else [], [wbuf] if wbuf is not None else [])
        ins = q.h.dma_start(out=out, in_=in_, **kw)
        ins.then_inc(b.dsem, 16)
        b.dcnt += 16
        s.ninst += 1
        return ins

    def barrier(s):
        engs = [s.pe, s.act, s.dve, s.pool]
        for e in engs + [s.sp]:
            for e2 in engs:
                if e2 is not e and e2.cnt:
                    s._wait(e, e2.name, e2.sem, e2.cnt)
            for b in s.dbufs:
                if b.dcnt:
                    s._wait(e, id(b), b.dsem, b.dcnt)

    def finish(s):
        for b in s.dbufs:
            if b.dcnt:
                s._wait(s.sp, id(b), b.dsem, b.dcnt)


class WStream:
    def __init__(s, sy, nc, stack, nslots, name, held, kcmax=32):
        s.sy, s.n, s.ahead = sy, nslots, nslots - held
        assert s.ahead >= 0
        s.tiles = [stack.enter_context(nc.sbuf_tensor("ws_%s_w%d" % (name, i), [128, kcmax * 128], BF16))
                   for i in range(nslots)]
        s.bufs = [sy.buf("%s_w%d" % (name, i), dma=True) for i in range(nslots)]
        s.queue = []
        s.issued = 0
        s.taken = 0
        s.bg = None

    def plan(s, specs):
        s.queue += specs

    def _issue(s):
        i = s.issued
        sl = i % s.n
        if s.queue[i][0] is None:
            _, src, kc, _ = s.queue[i]
            s.sy.dma(s.sy.pool, s.tiles[sl][:, 0:kc * 128], src, wbuf=s.bufs[sl])
        else:
            W, r0, kc, c0 = s.queue[i]
            dst = s.tiles[sl][:, 0:kc * 128].rearrange("p (k c) -> p k c", c=128)
            src = W[r0:r0 + kc * 128, c0:c0 + 128].rearrange("(k p) c -> p k c", p=128)
            s.sy.dma(s.sy.pool, dst, src, wbuf=s.bufs[sl])
        s.issued += 1
        if s.bg is not None:
            s.bg()

    def get(s):
        while s.issued < min(len(s.queue), s.taken + 1 + s.ahead):
            s._issue()
        i = s.taken
        s.taken += 1
        kc = s.queue[i][2]
        sl = i % s.n
        return s.bufs[sl], s.tiles[sl][:, 0:kc * 128].rearrange("p (k c) -> p k c", c=128)


def split_blocks(t0, n, maxn=512):
    nb = -(-n // maxn)
    base = n // nb
    rem = n % nb
    out = []
    o = t0
    for i in range(nb):
        sz = base + (1 if i < rem else 0)
        out.append((o, sz))
        o += sz
    return out


def tok_tiles(n):
    return [(i, min(128, n - i)) for i in range(0, n, 128)]


def build(cfg):
    nc = bass.Bass("TRN2", target_bir_lowering=False)
    D, CW, RW, NHP, NH, KD, KC, KF = cfg.D, cfg.CW, cfg.RW, cfg.NHP, cfg.NH, cfg.KD, cfg.KC, cfg.KF
    TP, NSQ, NTS, NT2, SMP0, NB2, SHW, NST, DFF = (cfg.TP, cfg.NSQ, cfg.NTS, cfg.NT2, cfg.SMP0, cfg.NB2,
                                                  cfg.SHW, cfg.NST, cfg.DFF)
    PC = cfg.pc
    CCo = cfg.cc

    def din(name, shape):
        return nc.dram_tensor(name, list(shape), F32, kind="ExternalInput").ap()

    def dout(name, shape):
        return nc.dram_tensor(name, list(shape), F32, kind="ExternalOutput").ap()

    xp = din("xp", [2 * TP, D])
    xs = din("xs", [NTS, D])
    swkv = din("swkv", [NSQ, NH, 64, 64])
    sconv = din("sconv", [NSQ * HALO, CW])
    sshift = din("sshift", [NSQ, SHW])
    w_in = din("w_in", [D, cfg.PT])
    wco = din("wco", [CW, D])
    wro = din("wro", [RW, D])
    wo = din("wo", [D, D])
    wg = din("wg", [D, DFF])
    wu = din("wu", [D, DFF])
    wd = din("wd", [DFF, D])
    pcols_d = din("pcols", [128, cfg.NPC])
    consts_d = din("consts", [128, cfg.NCC])
    w2a2_d = din("w2a2", [128, RW])
    g2_d = din("g2", [128, RW])
    cwc_d = din("cwc", [128, KC * CONV_K])

    y_d = dout("y", [NB2, D])
    wkvp_d = dout("wkvp", [NH, 64, 64])
    convp_d = dout("convp", [HALO, CW])
    shiftp_d = dout("shiftp", [NST, 128])
    wkvs_d = dout("wkvs", [NSQ, NH, 64, 64])
    convs_d = dout("convs", [NSQ * HALO, CW])
    shifts_d = dout("shifts", [NSQ, SHW])

    zsp = nc.dram_tensor("zsp", [KC, 128, NB2], BF16, kind=SCRATCH_KIND).ap()
    mixsp = nc.dram_tensor("mixsp", [KD, 128, NB2], BF16, kind=SCRATCH_KIND).ap()
    zfsp = nc.dram_tensor("zfsp", [KC, 128, NB2], F32, kind=SCRATCH_KIND).ap()
    ysp = nc.dram_tensor("ysp", [NHP, 128, NB2], BF16, kind=SCRATCH_KIND).ap()
    wsc_o = nc.dram_tensor("wsc_o", [KD, 128, KD * 128], BF16, kind=SCRATCH_KIND).ap()
    wsc_g = nc.dram_tensor("wsc_g", [KF, 128, KD * 128], BF16, kind=SCRATCH_KIND).ap()
    wsc_u = nc.dram_tensor("wsc_u", [KF, 128, KD * 128], BF16, kind=SCRATCH_KIND).ap()
    wsc_d = nc.dram_tensor("wsc_d", [KD, 128, KF * 128], BF16, kind=SCRATCH_KIND).ap()

    with ExitStack() as top:
        sy = Sy(nc, top)
        T, A, V, G = sy.T, sy.A, sy.V, sy.G

        uid = [0]

        def sb(stk, name, shape, dt=F32):
            uid[0] += 1
            if getattr(cfg, "dbg_sb", False):
                print("SB", name, shape, dt, "remaining", nc.sbuf_bytes_remaining, flush=True)
            return stk.enter_context(nc.sbuf_tensor("t%d_%s" % (uid[0], name), list(shape), dt))

        pcs = sb(top, "pcs", [128, cfg.NPC])
        cst = sb(top, "cst", [128, cfg.NCC])
        Bpar = sy.buf("params", dma=True)
        sy.dma(sy.sp, pcs[:], pcols_d[:, :], wbuf=Bpar)
        sy.dma(sy.sp, cst[:], consts_d[:, :], wbuf=Bpar)
        ident = cst[:, 0:128]
        bdmean = cst[:, 128:256]
        bdsum = cst[:, 256:384]
        onesln = cst[:, 384:512]
        masks = cst[0:64, 512:832].rearrange("p (q c) -> p q c", c=64)
        onescol = cst[:, 832:833]

        def pcol(name, i=0, n=1):
            return pcs[:, PC[name] + i:PC[name] + i + n]

        def rsqrt_inplace(ap, Bap):
            A([Bap], [Bap], lambda h: h.activation(out=ap, in_=ap, func=AF.Ln))
            A([Bap], [Bap], lambda h: h.activation(out=ap, in_=ap, func=AF.Exp, scale=-0.5))

        HsAll = sb(top, "HsAll", [64, NHP, 2, 64])
        BHs = sy.buf("HsAll")
        small = sb(top, "small", [128, 16])
        Bsmall = sy.buf("small")

        banks = [top.enter_context(nc.psum_tensor("bank%d" % i, [128, 512], F32)) for i in range(8)]
        Bbank = [sy.buf("bank%d" % i) for i in range(8)]
        misc_rr = [0]

        def misc():
            i = 6 + (misc_rr[0] % 2)
            misc_rr[0] += 1
            return banks[i], Bbank[i]

        def norm_transpose(src, Bsrc, n, gname, dst, Bdst, tok, xn, Bxn, junk, Bjunk, extra_w=[]):
            ss = small[:, 0:1]
            rs = small[:, 1:2]
            V([], [Bsmall], lambda h: h.memset(ss[:n], 0.0))
            A([Bsrc, Bsmall], [Bjunk, Bsmall],
              lambda h: h.activation(out=junk[:n, :], in_=src, func=AF.Square, accum_out=ss[:n]))
            V([Bsmall], [Bsmall], lambda h: h.tensor_scalar(out=rs[:n], in0=ss[:n], scalar1=1.0 / D, scalar2=RMS_EPS,
                                                           op0=ALU.mult, op1=ALU.add))
            rsqrt_inplace(rs[:n], Bsmall)
            A([Bsrc, Bsmall], [Bxn], lambda h: h.activation(out=xn[:n, :], in_=src, func=AF.Identity, scale=rs[:n]))
            gq = min(4, KD)
            for dg in range(KD // gq):
                bk, Bbk = misc()
                for q in range(gq):
                    dc = dg * gq + q
                    T([Bxn, Bpar], [Bbk], lambda h, q=q, dc=dc: h.transpose(
                        bk[:, q * 128:q * 128 + n], xn[:n, dc * 128:(dc + 1) * 128], ident[:n, :n]))
                gcol = PC[gname] + dg * gq
                V([Bbk, Bpar], [Bdst] + extra_w, lambda h, dg=dg, gcol=gcol: h.tensor_tensor(
                    out=dst[:, dg * gq:(dg + 1) * gq, tok:tok + n],
                    in0=bk[:, 0:gq * 128].rearrange("p (q t) -> p q t", t=128)[:, :, :n],
                    in1=pcs[:, gcol:gcol + gq].unsqueeze(2).to_broadcast([128, gq, n]), op=ALU.mult))

        def load_norm_transpose(stk, rows, gname, dst, Bdst):
            xin = [sb(stk, "xin%d" % i, [128, D]) for i in range(2)]
            Bxin = [sy.buf("xin%d" % i, dma=True) for i in range(2)]
            xn = sb(stk, "xn", [128, D])
            Bxn = sy.buf("xn")
            junk, Bjunk = xn, Bxn
            for i, (ap, n, tok) in enumerate(rows):
                sl = i % 2
                sy.dma(sy.sp, xin[sl][:n, :], ap, wbuf=Bxin[sl])
                norm_transpose(xin[sl][:n, :], Bxin[sl], n, gname, dst, Bdst, tok, xn, Bxn, junk, Bjunk)

        def mm_pass(Bslab, slab, kcn, rhs_fn, Brhs, blocks, bankset):
            for kc in range(kcn):
                for bi, (t0, n) in enumerate(blocks):
                    bk, Bbk = banks[bankset[bi]], Bbank[bankset[bi]]
                    T([Bslab] + Brhs, [Bbk], lambda h, kc=kc, t0=t0, n=n, bk=bk: h.matmul(
                        bk[:, 0:n], lhsT=slab[:, kc, :], rhs=rhs_fn(kc, t0, n), start=(kc == 0), stop=(kc == kcn - 1)))

        def rwkv_proj(mode, hp, W, hT, BhT, t_lo, n, has_prev, wk, Bw, stg, pset, cbase=0, nch=0, part="both"):
            o = 1 if has_prev else 0
            n1 = n + o
            names = ["k", "v"] if mode == "pre" else ["r", "k", "v"]
            tix = {"r": hp, "k": NHP + hp, "v": 2 * NHP + hp}
            TS = cfg.TS
            for xi, x in enumerate(names):
                Bsl, sl = W[x]
                pt, Bpt = wk["p" + x + pset], Bw["p" + x + pset]
                if part in ("both", "mm"):
                    mm_pass(Bsl, sl, KD, lambda kc, t0, nn: hT[:, kc, t0:t0 + nn], [BhT], [(t_lo - o, n1)], [xi])
                if part in ("both", "ev"):
                    A([Bbank[xi]], [Bpt], lambda h, xi=xi, pt=pt: h.activation(
                        out=R(pt[:, 0:n1]), in_=banks[xi][:, 0:n1], func=AF.Identity))
                    if mode == "own" and t_lo + n == SMP0:
                        G([Bpt], [stg["Bshp"]], lambda h, x=x, pt=pt: h.tensor_copy(
                            out=stg["shp"][:, tix[x]:tix[x] + 1], in_=pt[:, n1 - 1:n1]))
                    if mode == "smp":
                        G([Bpt], [stg["Bshs"]], lambda h, x=x, pt=pt: h.tensor_copy(
                            out=stg["shs"][:, tix[x], cbase:cbase + nch],
                            in_=pt[:, 0:n1].rearrange("p (b s) -> p b s", s=TS)[:, :, TS - 1]))

        def rwkv_group(mode, hp, lor, Blor, t_lo, n, C, nch, has_prev, wk, Bw0, shT, BshT,
                       pset, Sin=None, BSin=None, Sout=None, BSout=None, ytok=0, cbase=0, mid_hook=None, end_hook=None):
            o = 1 if has_prev else 0
            n1 = n + o
            names = ["k", "v"] if mode == "pre" else ["r", "k", "v"]
            tix = {"r": hp, "k": NHP + hp, "v": 2 * NHP + hp}
            TS = cfg.TS
            amap = {"pr": "pr" + pset, "pk": "pk" + pset, "pv": "pv" + pset,
                    "cum": "pr" + pset, "ep": "pk" + pset, "en": "pv" + pset}

            def t(name):
                return wk[amap.get(name, name)]

            class _B(dict):
                def __getitem__(self_, k):
                    return Bw0[amap.get(k, k)]
            Bw = _B()

            stop_here("g1")
            for x in names:
                px = t("p" + x)
                d = t("d")
                mu = pcol("mu", tix[x])
                if mode == "smp":
                    p3 = px[:, 0:n].rearrange("p (b s) -> p b s", s=TS)
                    d3 = d[:, 0:n].rearrange("p (b s) -> p b s", s=TS)
                    V([Bw["p" + x]], [Bw["d"]], lambda h, p3=p3, d3=d3: h.tensor_tensor(
                        out=d3[:, :, 1:TS], in0=p3[:, :, 0:TS - 1], in1=p3[:, :, 1:TS], op=ALU.subtract))
                    V([Bw["p" + x], BshT], [Bw["d"]], lambda h, p3=p3, d3=d3, x=x: h.tensor_tensor(
                        out=d3[:, :, 0], in0=shT[:, tix[x], cbase:cbase + nch], in1=p3[:, :, 0], op=ALU.subtract))
                else:
                    V([Bw["p" + x]], [Bw["d"]], lambda h, px=px, d=d: h.tensor_tensor(
                        out=d[:, 1:n1], in0=px[:, 0:n1 - 1], in1=px[:, 1:n1], op=ALU.subtract))
                    if not has_prev:
                        V([Bw["p" + x]], [Bw["d"]], lambda h, px=px, d=d: h.tensor_scalar(
                            out=d[:, 0:1], in0=px[:, 0:1], scalar1=-1.0, scalar2=None, op0=ALU.mult))
                V([Bw["p" + x], Bw["d"], Bpar], [Bw["m" + x]], lambda h, px=px, d=d, mu=mu, x=x: h.scalar_tensor_tensor(
                    out=R(t("m" + x)[:, 0:n]), in0=d[:, o:n1], scalar=mu, in1=px[:, o:n1], op0=ALU.mult, op1=ALU.add))
            mk, mv = t("mk"), t("mv")
            stop_here("g2")
            T([Bw["lw_"], Blor], [Bbank[0]], lambda h: h.matmul(banks[0][:, 0:n], lhsT=wk["w2a2"][0:64, :],
                                                               rhs=lor[0:64, 0, t_lo:t_lo + n], start=True, stop=True))
            A([Bbank[0], Bpar], [Bw["lw"]], lambda h: h.activation(out=t("lw")[:, 0:n], in_=banks[0][:, 0:n],
                                                                 func=AF.Sigmoid, bias=pcol("w0", hp)))
            V([Bw["lw"]], [Bw["lw"]], lambda h: h.tensor_scalar(out=t("lw")[:, 0:n], in0=t("lw")[:, 0:n],
                                                              scalar1=-0.6065306597126334, scalar2=None, op0=ALU.mult))
            T([Bw["lw_"], Blor], [Bbank[1]], lambda h: h.matmul(banks[1][:, 0:n], lhsT=wk["w2a2"][64:128, :],
                                                               rhs=lor[64:128, 0, t_lo:t_lo + n], start=True, stop=True))
            A([Bbank[1], Bpar], [Bw["ic"]], lambda h: h.activation(out=R(t("ic")[:, 0:n]), in_=banks[1][:, 0:n],
                                                                 func=AF.Sigmoid, bias=pcol("a0", hp)))
            if mode != "pre":
                T([Bw["lw_"], Blor], [Bbank[2]], lambda h: h.matmul(banks[2][:, 0:n], lhsT=wk["g2"][:, :],
                                                                   rhs=lor[:, 1, t_lo:t_lo + n], start=True, stop=True))
                A([Bbank[2]], [Bw["gt"]], lambda h: h.activation(out=t("gt")[:, 0:n], in_=banks[2][:, 0:n],
                                                               func=AF.Identity))
            stop_here("g3")
            V([Bw["mk"], Bpar], [Bw["kk"]], lambda h: h.tensor_scalar(out=t("kk")[:, 0:n], in0=mk[:, 0:n],
                                                                    scalar1=pcol("kk", hp), scalar2=None, op0=ALU.mult))
            A([Bw["mk"], Bpar], [Bw["e1"]], lambda h: h.activation(out=t("e1")[:, 0:n], in_=mk[:, 0:n], func=AF.Square,
                                                                 scale=pcol("kk", hp)))
            T([Bpar, Bw["e1"]], [Bbank[0]], lambda h: h.matmul(banks[0][:, 0:n], lhsT=bdsum, rhs=t("e1")[:, 0:n],
                                                              start=True, stop=True))
            V([Bbank[0]], [Bw["e1"]], lambda h: h.tensor_scalar(out=t("e1")[:, 0:n], in0=banks[0][:, 0:n],
                                                              scalar1=1e-24, scalar2=None, op0=ALU.max))
            rsqrt_inplace(t("e1")[:, 0:n], Bw["e1"])
            V([Bw["kk"], Bw["e1"]], [Bw["kk"]], lambda h: h.tensor_tensor(out=t("kk")[:, 0:n], in0=t("kk")[:, 0:n],
                                                                         in1=t("e1")[:, 0:n], op=ALU.mult))
            if mid_hook is not None:
                mid_hook()
            stop_here("g4")
            V([Bw["ic"], Bpar], [Bw["e2"]], lambda h: h.tensor_scalar(out=t("e2")[:, 0:n], in0=t("ic")[:, 0:n],
                                                                    scalar1=-1.0, scalar2=pcol("ka", hp),
                                                                    op0=ALU.add, op1=ALU.mult))
            V([Bw["e2"], Bw["mk"]], [Bw["km"]], lambda h: h.scalar_tensor_tensor(
                out=t("km")[:, 0:n], in0=t("e2")[:, 0:n], scalar=1.0, in1=mk[:, 0:n], op0=ALU.add, op1=ALU.mult))
            G([Bw["kk"], Bw["ic"]], [Bw["b"]], lambda h: h.tensor_tensor(out=R(t("b")[:, 0:n]), in0=t("kk")[:, 0:n],
                                                                        in1=t("ic")[:, 0:n], op=ALU.mult))
            stop_here("g5")
            V([Bw["lw"], Bw["ones"]], [Bw["cum"]], lambda h: h.tensor_tensor_scan(
                out=t("cum")[:, 0:n], data0=wk["ones"][:, 0:n], data1=t("lw")[:, 0:n],
                initial=0.0, op0=ALU.mult, op1=ALU.add))
            cum3 = t("cum")[:, 0:n].rearrange("p (c s) -> p c s", s=C)
            lw3 = t("lw")[:, 0:n].rearrange("p (c s) -> p c s", s=C)
            off = t("off")
            V([Bw["cum"], Bw["lw"]], [Bw["off"]], lambda h: h.tensor_tensor(
                out=off[:, 0:nch], in0=cum3[:, :, 0], in1=lw3[:, :, 0], op=ALU.subtract))
            V([Bw["cum"], Bw["off"]], [Bw["cum"]], lambda h: h.tensor_tensor(
                out=cum3, in0=cum3, in1=off[:, 0:nch].unsqueeze(2).to_broadcast([128, nch, C]), op=ALU.subtract))
            stop_here("g6")
            A([Bw["cum"]], [Bw["ep"]], lambda h: h.activation(out=R(t("ep")[:, 0:n]), in_=t("cum")[:, 0:n], func=AF.Exp))
            A([Bw["cum"]], [Bw["en"]], lambda h: h.activation(out=t("en")[:, 0:n], in_=t("cum")[:, 0:n], func=AF.Exp,
                                                            scale=-1.0))
            ep3_ = t("ep")[:, 0:n].rearrange("p (c s) -> p c s", s=C)
            kk3_ = t("kk")[:, 0:n].rearrange("p (c s) -> p c s", s=C)
            at3_ = t("at")[:, 0:n].rearrange("p (c s) -> p c s", s=C)
            V([Bw["kk"], Bw["ep"]], [Bw["at"]], lambda h: h.scalar_tensor_tensor(
                out=R(at3_[:, :, 1:C]), in0=kk3_[:, :, 1:C], scalar=-1.0, in1=ep3_[:, :, 0:C - 1],
                op0=ALU.mult, op1=ALU.mult))
            V([Bw["kk"]], [Bw["at"]], lambda h: h.tensor_scalar(
                out=R(at3_[:, :, 0]), in0=kk3_[:, :, 0], scalar1=-1.0, scalar2=None, op0=ALU.mult))
            G([Bw["b"], Bw["en"]], [Bw["bt"]], lambda h: h.tensor_tensor(out=R(t("bt")[:, 0:n]), in0=t("b")[:, 0:n],
                                                                        in1=t("en")[:, 0:n], op=ALU.mult))
            G([Bw["km"], Bw["en"]], [Bw["kt"]], lambda h: h.tensor_tensor(out=R(t("kt")[:, 0:n]), in0=t("km")[:, 0:n],
                                                                         in1=t("en")[:, 0:n], op=ALU.mult))
            if mode != "pre":
                V([Bw["mr"], Bw["ep"]], [Bw["rt"]], lambda h: h.tensor_tensor(
                    out=R(t("rt")[:, 0:n]), in0=t("mr")[:, 0:n], in1=t("ep")[:, 0:n], op=ALU.mult))
            ep3 = t("ep")[:, 0:n].rearrange("p (c s) -> p c s", s=C)
            stop_here("g7")
            rr5a = [0]

            def misc5a():
                i = [6, 7, 3, 4, 5][rr5a[0] % 5]
                rr5a[0] += 1
                return banks[i], Bbank[i]
            TM = wk["TM" + str(C)]
            AM = wk["AM" + str(C)]
            for c in range(nch):
                cs = slice(c * C, (c + 1) * C)
                bk, Bbk = misc5a()
                for qi, srcn in enumerate(["mv", "bt", "kt"]):
                    T([Bw[srcn], Bpar], [Bbk], lambda h, qi=qi, srcn=srcn, bk=bk, cs=cs: h.transpose(
                        bk[0:C, qi * 128:(qi + 1) * 128], t(srcn)[:, cs], ident))
                A([Bbk], [Bw["TM"]], lambda h, bk=bk, c=c: h.activation(
                    out=R(TM[:, c, :, :]), in_=bk[0:C, 0:384].rearrange("p (q f) -> p q f", f=128), func=AF.Identity))
                for hd in range(2):
                    ps = slice(hd * 64, (hd + 1) * 64)
                    bk, Bbk = misc5a()
                    pairs = [("bt", "at"), ("kt", "at"), ("bt", "rt"), ("kt", "rt"), ("at", "bt")]
                    for qi, (l, r) in enumerate(pairs):
                        if mode == "pre" and qi in (2, 3):
                            continue
                        T([Bw[l], Bw[r]], [Bbk], lambda h, qi=qi, l=l, r=r, bk=bk, ps=ps, cs=cs: h.matmul(
                            bk[0:C, qi * C:(qi + 1) * C], lhsT=R(t(l)[ps, cs]), rhs=R(t(r)[ps, cs]), start=True, stop=True))
                    rng = ((0, 2), (4, 5)) if mode == "pre" else ((0, 5),)
                    for q0, q1 in rng:
                        V([Bbk, Bpar], [Bw["AM"]], lambda h, bk=bk, c=c, hd=hd, q0=q0, q1=q1: h.tensor_tensor(
                            out=R(AM[:, c, hd, q0:q1, :]),
                            in0=bk[0:C, q0 * C:q1 * C].rearrange("p (q s) -> p q s", s=C),
                            in1=masks[0:C, q0:q1, 0:C], op=ALU.mult))
            stop_here("g9")
            PM = wk["PM" + str(C)]
            NN = wk["NN" + str(C)]
            nlev = {64: 5, 4: 1}[C]
            ng = nch * 2
            V([Bw["AM"], Bpar], [Bw["PM"]], lambda h: h.tensor_tensor(
                out=R(PM[:, :, :, :].rearrange("p c h s -> p (c h) s")),
                in0=AM[:, :, :, 0, :].rearrange("p c h s -> p (c h) s"),
                in1=ident[0:C, 0:C].unsqueeze(1).to_broadcast([C, ng, C]), op=ALU.add))

            def Ncur(lev, which, c, hd):
                if lev % 2 == 0:
                    return AM[:, c, hd, 0 if which == 0 else 4, :]
                return NN[:, which, c, hd, :]

            def Nall(par, which):
                if par == 0:
                    return AM[:, :, :, 0 if which == 0 else 4, :].rearrange("p c h s -> p (c h) s")
                return NN[:, which, :, :, :].rearrange("p c h s -> p (c h) s")
            for lev in range(nlev):
                last = (lev == nlev - 1)
                for c in range(nch):
                    for hd in range(2):
                        idx = (c * 2 + hd) * C
                        if not last:
                            T([Bw["AM"], Bw["NN"]], [Bbank[3]], lambda h, lev=lev, c=c, hd=hd, idx=idx: h.matmul(
                                banks[3][0:C, idx:idx + C], lhsT=R(Ncur(lev, 1, c, hd)), rhs=R(Ncur(lev, 0, c, hd)),
                                start=True, stop=True))
                        T([Bw["AM"], Bw["NN"]], [Bbank[4]], lambda h, lev=lev, c=c, hd=hd, idx=idx: h.matmul(
                            banks[4][0:C, idx:idx + C], lhsT=R(Ncur(lev, 0, c, hd)), rhs=R(Ncur(lev, 1, c, hd)),
                            start=True, stop=True))
                nx = (lev + 1) % 2
                if not last:
                    A([Bbank[3]], [Bw["NN"], Bw["AM"]], lambda h, nx=nx: h.activation(
                        out=R(Nall(nx, 0)), in_=banks[3][0:C, 0:ng * C].rearrange("p (g s) -> p g s", s=C),
                        func=AF.Identity))
                V([Bbank[4]], [Bw["NN"], Bw["AM"]], lambda h, nx=nx: h.tensor_copy(
                    out=R(Nall(nx, 1)), in_=banks[4][0:C, 0:ng * C].rearrange("p (g s) -> p g s", s=C)))
                for c in range(nch):
                    for hd in range(2):
                        idx = (c * 2 + hd) * C
                        T([Bw["NN"], Bw["AM"], Bw["PM"]], [Bbank[5]], lambda h, lev=lev, c=c, hd=hd, idx=idx: h.matmul(
                            banks[5][0:C, idx:idx + C], lhsT=R(Ncur(lev + 1, 1, c, hd)), rhs=R(PM[:, c, hd, :]),
                            start=True, stop=True))
                V([Bbank[5], Bw["PM"]], [Bw["PM"]], lambda h: h.tensor_tensor(
                    out=R(PM[:, :, :, :].rearrange("p c h s -> p (c h s)")),
                    in0=PM[:, :, :, :].rearrange("p c h s -> p (c h s)"), in1=banks[5][0:C, 0:ng * C], op=ALU.add))
            stop_here("g10")
            rr5 = [0]

            def misc5():
                i = [3, 4, 5, 6, 7][rr5[0] % 5]
                rr5[0] += 1
                return banks[i], Bbank[i]
            sh_src = ["at", "ep"] if mode == "pre" else ["at", "ep", "rt"]
            for si, nm in enumerate(sh_src):
                T([Bw[nm], Bpar], [Bbank[3 + si]], lambda h, si=si, nm=nm: h.matmul(
                    banks[3 + si][0:64, 0:n], lhsT=R(identr[:, 64:128]), rhs=R(t(nm)[:, 0:n]), start=True, stop=True))
                A([Bbank[3 + si]], [Bw[nm + "1"]], lambda h, si=si, nm=nm: h.activation(
                    out=R(wk[nm + "1"][:, 0:n]), in_=banks[3 + si][0:64, 0:n], func=AF.Identity))
            ep13 = wk["ep1"][:, 0:n].rearrange("p (c s) -> p c s", s=C)

            def RY(hd, ap):
                return R(ap) if hd == 0 else ap

            def hsel(nm, hd):
                return t(nm)[0:64, :] if hd == 0 else wk[nm + "1"]
            for c in range(nch):
                cs = slice(c * C, (c + 1) * C)
                par = (c % 2) if mode == "smp" else 0
                if mode == "smp":
                    Hs, BHs_ = wk["HsS"][:, par, :, :], Bw["HsS%d" % par]
                else:
                    Hs, BHs_ = HsAll[:, hp, :, :], BHs
                Wsb, Usb, HsG = wk["Wsb"][:, par, :], wk["Usb"][:, par, :], wk["HsG"][:, par, :, :]
                BWsb, BUsb, BHsG = Bw["Wsb%d" % par], Bw["Usb%d" % par], Bw["HsG%d" % par]
                if mode == "smp":
                    bk, Bbk = misc5()
                    for hd in range(2):
                        T([BSin, Bpar], [Bbk], lambda h, bk=bk, c=c, hd=hd: h.transpose(
                            bk[0:64, hd * 64:(hd + 1) * 64], Sin[:, cbase + c, hd, :], ident[0:64, 0:64]))
                    A([Bbk], [BHs_], lambda h, bk=bk: h.activation(
                        out=R(Hs), in_=bk[0:64, 0:128].rearrange("p (h v) -> p h v", v=64), func=AF.Identity))
                gams = [ep3[0:64, c, C - 1:C], ep13[:, c, C - 1:C]]
                for hd in range(2):
                    A([BHs_, Bw["ep"], Bw["ep1"]], [BHsG], lambda h, hd=hd: h.activation(
                        out=HsG[:, hd, :], in_=Hs[:, hd, :], func=AF.Identity, scale=gams[hd]))
                bk, Bbk = misc5()
                for hd in range(2):
                    T([Bw["at"], Bw["at1"], BHs_], [Bbk], lambda h, bk=bk, hd=hd, cs=cs: h.matmul(
                        bk[0:C, hd * 64:(hd + 1) * 64], lhsT=R(hsel("at", hd)[:, cs]), rhs=R(Hs[:, hd, :]), start=True, stop=False))
                    T([Bw["AM"], Bw["TM"]], [Bbk], lambda h, bk=bk, hd=hd, c=c: h.matmul(
                        bk[0:C, hd * 64:(hd + 1) * 64], lhsT=R(AM[:, c, hd, 1, :]), rhs=R(TM[:, c, 0, hd * 64:(hd + 1) * 64]),
                        start=False, stop=True))
                A([Bbk], [BWsb], lambda h, bk=bk: h.activation(out=R(Wsb[0:C, :]), in_=bk[0:C, 0:128], func=AF.Identity))
                bk2, Bbk2 = misc5()
                for hd in range(2):
                    T([Bw["PM"], BWsb], [Bbk2], lambda h, bk2=bk2, hd=hd, c=c: h.matmul(
                        bk2[0:C, hd * 64:(hd + 1) * 64], lhsT=R(PM[:, c, hd, :]), rhs=R(Wsb[0:C, hd * 64:(hd + 1) * 64]),
                        start=True, stop=True))
                V([Bbk2], [BUsb], lambda h, bk2=bk2: h.tensor_copy(out=R(Usb[0:C, :]), in_=bk2[0:C, 0:128]))
                bk4, Bbk4 = misc5()
                for hd in range(2):
                    T([Bw["TM"], BUsb], [Bbk4], lambda h, bk4=bk4, hd=hd, c=c: h.matmul(
                        bk4[0:64, hd * 64:(hd + 1) * 64], lhsT=R(TM[:, c, 1, hd * 64:(hd + 1) * 64]),
                        rhs=R(Usb[0:C, hd * 64:(hd + 1) * 64]), start=True, stop=False))
                    T([Bw["TM"]], [Bbk4], lambda h, bk4=bk4, hd=hd, c=c: h.matmul(
                        bk4[0:64, hd * 64:(hd + 1) * 64], lhsT=R(TM[:, c, 2, hd * 64:(hd + 1) * 64]),
                        rhs=R(TM[:, c, 0, hd * 64:(hd + 1) * 64]), start=False, stop=True))
                if mode != "pre":
                    bk3, Bbk3 = misc5()
                    for hd in range(2):
                        ps = slice(hd * 64, (hd + 1) * 64)
                        T([BHs_, Bw["rt"], Bw["rt1"]], [Bbk3], lambda h, bk3=bk3, ps=ps, hd=hd, cs=cs: h.matmul(
                            bk3[ps, 0:C], lhsT=RY(hd, Hs[:, hd, :]), rhs=RY(hd, hsel("rt", hd)[:, cs]), start=True, stop=False))
                        T([BUsb, Bw["AM"]], [Bbk3], lambda h, bk3=bk3, ps=ps, hd=hd, c=c: h.matmul(
                            bk3[ps, 0:C], lhsT=RY(hd, Usb[0:C, hd * 64:(hd + 1) * 64]), rhs=RY(hd, AM[:, c, hd, 2, :]),
                            start=False, stop=False))
                        T([Bw["TM"], Bw["AM"]], [Bbk3], lambda h, bk3=bk3, ps=ps, hd=hd, c=c: h.matmul(
                            bk3[ps, 0:C], lhsT=RY(hd, TM[:, c, 0, hd * 64:(hd + 1) * 64]), rhs=RY(hd, AM[:, c, hd, 3, :]),
                            start=False, stop=True))
                    A([Bbk3], [Bw["yr"]], lambda h, bk3=bk3, cs=cs: h.activation(out=t("yr")[:, cs], in_=bk3[:, 0:C],
                                                                              func=AF.Identity))
                for hd in range(2):
                    V([Bbk4, BHsG, Bw["ep"], Bw["ep1"]], [BHs_], lambda h, bk4=bk4, hd=hd: h.scalar_tensor_tensor(
                        out=R(Hs[:, hd, :]), in0=bk4[0:64, hd * 64:(hd + 1) * 64], scalar=gams[hd], in1=HsG[:, hd, :],
                        op0=ALU.mult, op1=ALU.add))
                if mode == "smp":
                    bk5, Bbk5 = misc5()
                    for hd in range(2):
                        T([BHs_, Bpar], [Bbk5], lambda h, bk5=bk5, hd=hd: h.transpose(
                            bk5[0:64, hd * 64:(hd + 1) * 64], Hs[:, hd, :], ident[0:64, 0:64]))
                    A([Bbk5], [BSout], lambda h, bk5=bk5, c=c: h.activation(
                        out=Sout[:, cbase + c, :, :].rearrange("p h k -> p (h k)"), in_=bk5[0:64, 0:128], func=AF.Identity))
            stop_here("g11")
            if mode == "pre":
                return
            yr = t("yr")
            if end_hook is not None:
                end_hook()
            V([Bw["mr"], Bw["km"], Bpar], [Bw["e2"]], lambda h: h.scalar_tensor_tensor(
                out=t("e2")[:, 0:n], in0=t("mr")[:, 0:n], scalar=pcol("rk", hp), in1=t("km")[:, 0:n],
                op0=ALU.mult, op1=ALU.mult))
            bkC, BbkC = misc()
            T([Bpar, Bw["e2"]], [BbkC], lambda h: h.matmul(bkC[:, 0:n], lhsT=bdsum, rhs=t("e2")[:, 0:n],
                                                              start=True, stop=True))
            V([BbkC, Bw["mv"]], [Bw["e2"]], lambda h: h.tensor_tensor(out=t("e2")[:, 0:n], in0=bkC[:, 0:n],
                                                                         in1=mv[:, 0:n], op=ALU.mult))
            bkA, BbkA = misc()
            T([Bpar, Bw["yr"]], [BbkA], lambda h: h.matmul(bkA[:, 0:n], lhsT=bdmean, rhs=yr[:, 0:n],
                                                              start=True, stop=True))
            V([Bw["yr"], BbkA], [Bw["yr"]], lambda h: h.tensor_tensor(out=yr[:, 0:n], in0=yr[:, 0:n],
                                                                         in1=bkA[:, 0:n], op=ALU.subtract))
            bkB, BbkB = misc()
            G([Bw["yr"]], [Bw["e1"]], lambda h: h.tensor_tensor(out=t("e1")[:, 0:n], in0=yr[:, 0:n], in1=yr[:, 0:n],
                                                              op=ALU.mult))
            T([Bpar, Bw["e1"]], [BbkB], lambda h: h.matmul(bkB[:, 0:n], lhsT=bdmean, rhs=t("e1")[:, 0:n],
                                                              start=True, stop=True))
            V([BbkB], [Bw["e1"]], lambda h: h.tensor_scalar(out=t("e1")[:, 0:n], in0=bkB[:, 0:n],
                                                              scalar1=GN_EPS, scalar2=None, op0=ALU.add))
            rsqrt_inplace(t("e1")[:, 0:n], Bw["e1"])
            V([Bw["yr"], Bw["e1"]], [Bw["yr"]], lambda h: h.tensor_tensor(out=yr[:, 0:n], in0=yr[:, 0:n],
                                                                         in1=t("e1")[:, 0:n], op=ALU.mult))
            V([Bw["yr"], Bpar], [Bw["yr"]], lambda h: h.tensor_scalar(out=yr[:, 0:n], in0=yr[:, 0:n],
                                                                    scalar1=pcol("gg", hp), scalar2=pcol("gb", hp),
                                                                    op0=ALU.mult, op1=ALU.add))
            G([Bw["yr"], Bw["e2"]], [Bw["yr"]], lambda h: h.tensor_tensor(out=yr[:, 0:n], in0=yr[:, 0:n],
                                                                         in1=t("e2")[:, 0:n], op=ALU.add))
            e = wk["yst_rr"][0] % 2
            wk["yst_rr"][0] += 1
            yst, Byst = wk["yst"][e], Bw["yst%d" % e]
            V([Bw["yr"], Bw["gt"]], [Byst], lambda h: h.tensor_tensor(out=yst[:, 0:n], in0=yr[:, 0:n],
                                                                     in1=t("gt")[:, 0:n], op=ALU.mult))
            sy.dma(sy.sp, ysp[hp, :, ytok:ytok + n], yst[:, 0:n], rbuf=Byst)

        def alloc_work(stk, GW, nchp, with_smp):
            wk = {}
            Bw = {}
            for nm in ["pr0", "pk0", "pv0", "pr1", "pk1", "pv1", "d", "mr", "mk", "mv", "lw", "ic", "gt", "kk", "e1", "e2",
                       "km", "b", "rt", "ones", "yr"]:
                wk[nm] = sb(stk, "wk_" + nm, [128, GW + 1])
                Bw[nm] = sy.buf("wk_" + nm)
            for a_, b_ in [("at", "mk"), ("bt", "ic"), ("kt", "b")]:
                wk[a_] = wk[b_]
                Bw[a_] = Bw[b_]
            wk["off"] = sb(stk, "wk_off", [128, max(nchp, NSQ)])
            Bw["off"] = sy.buf("wk_off")
            V([], [Bw["ones"]], lambda h: h.memset(wk["ones"][:], 1.0))
            wk["TM64"] = sb(stk, "TM64", [64, nchp, 3, 128])
            wk["AM64"] = sb(stk, "AM64", [64, nchp, 2, 5, 64])
            wk["PM64"] = sb(stk, "PM64", [64, nchp, 2, 64])
            wk["NN64"] = sb(stk, "NN64", [64, 2, nchp, 2, 64])
            if with_smp:
                wk["TM4"] = sb(stk, "TM4", [4, NSH, 3, 128])
                wk["AM4"] = sb(stk, "AM4", [4, NSH, 2, 5, 4])
                wk["PM4"] = sb(stk, "PM4", [4, NSH, 2, 4])
                wk["NN4"] = sb(stk, "NN4", [4, 2, NSH, 2, 4])
                wk["yst"] = [sb(stk, "yst%d" % i, [128, GW + 1], BF16) for i in range(2)]
                wk["yst_rr"] = [0]
                for i in range(2):
                    Bw["yst%d" % i] = sy.buf("yst%d" % i, dma=True)
            for nm in ["TM", "AM", "PM", "NN"]:
                Bw[nm] = sy.buf("wk_" + nm)
            for nm in ["Wsb", "Usb", "HsG", "HsS"]:
                for i in range(2):
                    Bw[nm + str(i)] = sy.buf("wk_%s%d" % (nm, i))
            wk["Wsb"] = sb(stk, "Wsb", [64, 2, 128])
            wk["Usb"] = sb(stk, "Usb", [64, 2, 128])
            wk["HsG"] = sb(stk, "HsG", [64, 2, 2, 64])
            wk["HsS"] = sb(stk, "HsS", [64, 2, 2, 64])
            for nm in ["at1", "ep1", "rt1"]:
                wk[nm] = sb(stk, "wk_" + nm, [64, GW + 1])
                Bw[nm] = sy.buf("wk_" + nm)
            wk["w2a2"] = sb(stk, "w2a2", [128, 128])
            wk["g2"] = sb(stk, "g2", [128, 128])
            Bw["lw_"] = sy.buf("lora_w", dma=True)
            return wk, Bw

        def load_lora_w(wk, Bw, hp):
            sy.dma(sy.sp, wk["w2a2"][:, :], w2a2_d[:, hp * 128:(hp + 1) * 128], wbuf=Bw["lw_"])
            sy.dma(sy.sp, wk["g2"][:, :], g2_d[:, hp * 128:(hp + 1) * 128], wbuf=Bw["lw_"])

        def lora_inputs(stk, ws, hT, BhT, NT, ntiles, prompt_n, has_smp, shT, BshT, stg):
            TS = cfg.TS
            lor = sb(stk, "lor", [128, ntiles, NT])
            Blor = sy.buf("lor")
            with ExitStack() as tmpstk:
                ptmp = sb(tmpstk, "lor_p", [128, NT])
                dtmp = sb(tmpstk, "lor_d", [128, NT])
                Bp, Bd = sy.buf("lor_p"), sy.buf("lor_d")
                blocks = split_blocks(0, NT)
                for i in range(ntiles):
                    Bsl, sl = ws.get()
                    mm_pass(Bsl, sl, KD, lambda kc, t0, nn: hT[:, kc, t0:t0 + nn], [BhT], blocks,
                            list(range(len(blocks))))
                    for bi, (t0, nn) in enumerate(blocks):
                        A([Bbank[bi]], [Bp], lambda h, bi=bi, t0=t0, nn=nn: h.activation(
                            out=ptmp[:, t0:t0 + nn], in_=banks[bi][:, 0:nn], func=AF.Identity))
                    stop_here("lmm")
                    ti = 3 * NHP + i
                    if stg is not None:
                        G([Bp], [stg["Bshp"]], lambda h, ti=ti: h.tensor_copy(out=stg["shp"][:, ti:ti + 1],
                                                                            in_=ptmp[:, SMP0 - 1:SMP0]))
                        G([Bp], [stg["Bshs"]], lambda h, ti=ti: h.tensor_copy(
                            out=stg["shs"][:, ti, :],
                            in_=ptmp[:, SMP0:NT].rearrange("p (b s) -> p b s", s=TS)[:, :, TS - 1]))
                    pn = prompt_n
                    V([Bp], [Bd], lambda h: h.tensor_tensor(out=dtmp[:, 1:pn], in0=ptmp[:, 0:pn - 1], in1=ptmp[:, 1:pn],
                                                           op=ALU.subtract))
                    V([Bp], [Bd], lambda h: h.tensor_scalar(out=dtmp[:, 0:1], in0=ptmp[:, 0:1], scalar1=-1.0,
                                                           scalar2=None, op0=ALU.mult))
                    if has_smp:
                        p3 = ptmp[:, pn:NT].rearrange("p (b s) -> p b s", s=TS)
                        d3 = dtmp[:, pn:NT].rearrange("p (b s) -> p b s", s=TS)
                        V([Bp], [Bd], lambda h, p3=p3, d3=d3: h.tensor_tensor(
                            out=d3[:, :, 1:TS], in0=p3[:, :, 0:TS - 1], in1=p3[:, :, 1:TS], op=ALU.subtract))
                        V([Bp, BshT], [Bd], lambda h, p3=p3, d3=d3, ti=ti: h.tensor_tensor(
                            out=d3[:, :, 0], in0=shT[:, ti, :], in1=p3[:, :, 0], op=ALU.subtract))
                    stop_here("l1")
                    V([Bp, Bd, Bpar], [Blor], lambda h, i=i, ti=ti: h.scalar_tensor_tensor(
                        out=lor[:, i, :], in0=dtmp[:, :], scalar=pcol("mu", ti), in1=ptmp[:, :],
                        op0=ALU.mult, op1=ALU.add))
                    stop_here("l2")
                    if i == 0:
                        A([Blor], [Blor], lambda h: h.activation(out=lor[0:64, 0, :], in_=lor[0:64, 0, :], func=AF.Tanh))
                    else:
                        A([Blor], [Blor], lambda h: h.activation(out=lor[:, 1, :], in_=lor[:, 1, :], func=AF.Sigmoid))
                sy.barrier()
            return lor, Blor

        SH = 2 if (NSQ % 2 == 0 and NSQ >= 4) else 1
        NSH = NSQ // SH
        NCHG = min(4, TP // 64)
        GTOK = NCHG * 64
        NGRP = TP // GTOK
        rcol = 2 * CW
        lcol = 2 * CW + 3 * RW

        def hp_specs(hp, names):
            off = {"r": 0, "k": RW, "v": 2 * RW}
            return [(w_in, 0, KD, rcol + off[x] + hp * 128) for x in names]

        for hp_ in range(NHP):
            V([Bpar], [BHs], lambda h, hp_=hp_: h.tensor_scalar(
                out=R(HsAll[:, hp_, :, :].rearrange("p h v -> p (h v)")), in0=ident[0:64, 0:128], scalar1=0.0,
                scalar2=None, op0=ALU.mult))
        identr = sb(top, "identr", [128, 128])
        V([Bpar], [Bpar], lambda h: h.tensor_copy(out=R(identr[:, :]), in_=ident))

        Bconv = sy.buf("wconv", dma=True)
        conv_jobs = []
        for j in range(KD):
            conv_jobs.append((wsc_o[j, :, :], wo, KD, j))
        for f in range(KF):
            conv_jobs.append((wsc_g[f, :, :], wg, KD, f))
            conv_jobs.append((wsc_u[f, :, :], wu, KD, f))
        for j in range(KD):
            conv_jobs.append((wsc_d[j, :, :], wd, KF, j))
        conv_next = [0]

        def bg_convert(n=2):
            for _ in range(n):
                if conv_next[0] >= len(conv_jobs):
                    return
                dst, W, kc, ct = conv_jobs[conv_next[0]]
                conv_next[0] += 1
                sy.dma(sy.pool, dst.rearrange("p (k c) -> p k c", c=128),
                       W[:, ct * 128:(ct + 1) * 128].rearrange("(k p) c -> p k c", p=128), wbuf=Bconv)

        def stop_here(tag):
            if getattr(cfg, "stop", None) == tag:
                sy.barrier()
                sy.finish()
                sy.dead = True

        try:
            with ExitStack() as ph:
                hTp = sb(ph, "hTp", [128, KD, TP], BF16)
                BhTp = sy.buf("hTp")
                with ExitStack() as pa:
                    rows = [(xp[i:i + n, :], n, i) for (i, n) in tok_tiles(TP)]
                    load_norm_transpose(pa, rows, "g1", hTp, BhTp)
                    sy.barrier()
                    stop_here("p1a")
                ws = WStream(sy, nc, ph, 4, "p1", 2)
                ws.bg = bg_convert
                ws.plan([(w_in, 0, KD, lcol)])
                for hp in range(NHP):
                    ws.plan(hp_specs(hp, ["k", "v"]))
                lor, Blor = lora_inputs(ph, ws, hTp, BhTp, TP, 1, TP, False, None, None, None)
                wk, Bw = alloc_work(ph, GTOK, NCHG, False)
                stop_here("p1w")
                stop_here("p1l")
                for hp in range(NHP):
                    load_lora_w(wk, Bw, hp)
                    W = {}
                    for x in ["k", "v"]:
                        W[x] = ws.get()
                    rwkv_proj("pre", hp, W, hTp, BhTp, 0, GTOK, False, wk, Bw, None, "0")
                    for g in range(NGRP):
                        hook = None
                        if g + 1 < NGRP:
                            def hook(g=g, W=W, hp=hp):
                                rwkv_proj("pre", hp, W, hTp, BhTp, (g + 1) * GTOK, GTOK, True, wk, Bw, None,
                                          str((g + 1) % 2), part="mm")
                        rwkv_group("pre", hp, lor, Blor, g * GTOK, GTOK, 64, NCHG, g > 0, wk, Bw,
                                   None, None, str(g % 2), mid_hook=hook)
                        if g + 1 < NGRP:
                            rwkv_proj("pre", hp, W, hTp, BhTp, (g + 1) * GTOK, GTOK, True, wk, Bw, None,
                                      str((g + 1) % 2), part="ev")
                sy.barrier()
                stop_here("p1")

            with ExitStack() as ph:
                TS = cfg.TS
                hT = sb(ph, "hT", [128, KD, NT2], BF16)
                BhT = sy.buf("hT")
                with ExitStack() as pa:
                    rows = [(xp[TP - HALO + i:TP - HALO + i + n, :], n, i) for (i, n) in tok_tiles(HALO + TP)]
                    rows += [(xs[i:i + n, :], n, SMP0 + i) for (i, n) in tok_tiles(NTS)]
                    load_norm_transpose(pa, rows, "g1", hT, BhT)
                    sy.barrier()
                    stop_here("p2a")
                shT = sb(ph, "shT", [128, NST, NSQ])
                BshT = sy.buf("shT")
                stg = {"shp": sb(ph, "shp", [128, NST]), "Bshp": sy.buf("shp"),
                       "shs": sb(ph, "shs", [128, NST, NSQ]), "Bshs": sy.buf("shs")}
                with ExitStack() as pa:
                    ssh = sb(pa, "ssh", [NSQ, SHW])
                    Bssh = sy.buf("ssh", dma=True)
                    sy.dma(sy.sp, ssh[:], sshift[:, :], wbuf=Bssh)
                    for i0 in range(0, NST, 4):
                        nq = min(4, NST - i0)
                        bk, Bbk = misc()
                        for q in range(nq):
                            T([Bssh, Bpar], [Bbk], lambda h, q=q, i0=i0, bk=bk: h.transpose(
                                bk[:, q * NSQ:(q + 1) * NSQ], ssh[:, (i0 + q) * 128:(i0 + q + 1) * 128], ident[0:NSQ, 0:NSQ]))
                        A([Bbk], [BshT], lambda h, i0=i0, nq=nq, bk=bk: h.activation(
                            out=shT[:, i0:i0 + nq, :], in_=bk[:, 0:nq * NSQ].rearrange("p (q b) -> p q b", b=NSQ),
                            func=AF.Identity))
                    sy.barrier()
                with ExitStack() as pc_:
                    ws = WStream(sy, nc, pc_, 3, "p2c", 2)
                    ws.bg = bg_convert
                    for ct in range(KC):
                        ws.plan([(w_in, 0, KD, ct * 128), (w_in, 0, KD, CW + ct * 128)])
                    cwt = sb(pc_, "cwt", [128, KC * CONV_K])
                    Bcwt = sy.buf("cwt", dma=True)
                    sy.dma(sy.sp, cwt[:, :], cwc_d[:, :], wbuf=Bcwt)
                    extp = [sb(pc_, "extp%d" % i, [128, SMP0], BF16) for i in range(2)]
                    exts = [sb(pc_, "exts%d" % i, [128, NSQ, HALO + TS], BF16) for i in range(2)]
                    Bext = [sy.buf("ext%d" % i) for i in range(2)]
                    dgw = [sb(pc_, "dgw%d" % i, [128, CONV_K, 128], BF16) for i in range(2)]
                    Bdg = [sy.buf("dgw%d" % i) for i in range(2)]
                    sig = sb(pc_, "sig", [128, 512])
                    Bsig = sy.buf("sig")
                    cps = sb(pc_, "cps", [128, KC, HALO])
                    Bcps = sy.buf("cps")
                    css = sb(pc_, "css", [128, KC, NTS])
                    Bcss = sy.buf("css")
                    NG4 = -(-NSQ // 4)
                    sct = sb(pc_, "sct", [4 * HALO, NG4, 128])
                    Bsct = sy.buf("sct", dma=True)
                    zst = [sb(pc_, "zst%d" % i, [128, 512]) for i in range(2)]
                    Bzst = [sy.buf("zst%d" % i, dma=True) for i in range(2)]
                    zk = 0
                    gblocks = split_blocks(0, NT2)
                    nb_ = len(gblocks)
                    assert 2 * nb_ <= 6, nb_
                    cblocks = split_blocks(0, TP)
                    for ct in range(KC):
                        e = ct % 2
                        for g4 in range(NG4):
                            nb = min(4, NSQ - g4 * 4)
                            sy.dma(sy.sp, sct[0:nb * HALO, g4, :],
                                   sconv[g4 * 4 * HALO:(g4 * 4 + nb) * HALO, ct * 128:(ct + 1) * 128], wbuf=Bsct)
                        bk, Bbk = misc()
                        for g4 in range(NG4):
                            nb = min(4, NSQ - g4 * 4)
                            T([Bsct, Bpar], [Bbk], lambda h, g4=g4, nb=nb, bk=bk: h.transpose(
                                bk[:, g4 * 4 * HALO:(g4 * 4 + nb) * HALO], sct[0:nb * HALO, g4, :],
                                ident[0:nb * HALO, 0:nb * HALO]))
                        V([Bbk], [Bext[e]], lambda h, bk=bk, e=e: h.tensor_copy(
                            out=exts[e][:, :, 0:HALO], in_=bk[:, 0:NSQ * HALO].rearrange("p (b r) -> p b r", r=HALO)))
                        V([Bpar, Bcwt], [Bdg[e]], lambda h, e=e, ct=ct: h.tensor_tensor(
                            out=dgw[e][:, :, :], in0=ident.unsqueeze(1).to_broadcast([128, CONV_K, 128]),
                            in1=cwt[:, ct * CONV_K:(ct + 1) * CONV_K].unsqueeze(2).to_broadcast(
                                [128, CONV_K, 128]), op=ALU.mult))
                        Bsa, sla = ws.get()
                        Bsb, slb = ws.get()
                        mm_pass(Bsa, sla, KD, lambda kc, t0, nn: hT[:, kc, t0:t0 + nn], [BhT], gblocks, list(range(nb_)))
                        mm_pass(Bsb, slb, KD, lambda kc, t0, nn: hT[:, kc, t0:t0 + nn], [BhT], gblocks,
                                list(range(nb_, 2 * nb_)))
                        for bi, (t0, nn) in enumerate(gblocks):
                            A([Bbank[nb_ + bi]], [Bsig], lambda h, bi=bi, nn=nn: h.activation(
                                out=sig[:, 0:nn], in_=banks[nb_ + bi][:, 0:nn], func=AF.Sigmoid))
                            pe_ = min(t0 + nn, SMP0)
                            if pe_ > t0:
                                pn_ = pe_ - t0
                                V([Bbank[bi], Bsig], [Bext[e]], lambda h, bi=bi, t0=t0, pn_=pn_, e=e: h.tensor_tensor(
                                    out=extp[e][:, t0:t0 + pn_], in0=banks[bi][:, 0:pn_], in1=sig[:, 0:pn_], op=ALU.mult))
                                lo = max(t0, SMP0 - HALO)
                                if lo < pe_:
                                    V([Bbank[bi], Bsig], [Bcps], lambda h, bi=bi, t0=t0, lo=lo, ct=ct, pe_=pe_: h.tensor_tensor(
                                        out=cps[:, ct, lo - (SMP0 - HALO):pe_ - (SMP0 - HALO)],
                                        in0=banks[bi][:, lo - t0:pe_ - t0], in1=sig[:, lo - t0:pe_ - t0], op=ALU.mult))
                            if t0 + nn > SMP0:
                                so = max(t0, SMP0) - t0
                                assert max(t0, SMP0) == SMP0 and t0 + nn == NT2
                                V([Bbank[bi], Bsig], [Bext[e]], lambda h, bi=bi, so=so, nn=nn, e=e: h.tensor_tensor(
                                    out=exts[e][:, :, HALO:HALO + TS],
                                    in0=banks[bi][:, so:nn].rearrange("p (b s) -> p b s", s=TS),
                                    in1=sig[:, so:nn].rearrange("p (b s) -> p b s", s=TS), op=ALU.mult))
                                V([Bbank[bi], Bsig], [Bcss], lambda h, bi=bi, so=so, nn=nn, ct=ct: h.tensor_tensor(
                                    out=css[:, ct, :], in0=banks[bi][:, so:nn], in1=sig[:, so:nn], op=ALU.mult))
                        for bi, (t0, nn) in enumerate(cblocks):
                            bk, Bbk = misc()
                            for j in range(CONV_K):
                                T([Bdg[e], Bext[e]], [Bbk], lambda h, j=j, t0=t0, nn=nn, e=e, bk=bk: h.matmul(
                                    bk[:, 0:nn], lhsT=dgw[e][:, j, :], rhs=extp[e][:, t0 + j:t0 + j + nn],
                                    start=(j == 0), stop=(j == CONV_K - 1)))
                            q = zk % 2
                            zk += 1
                            A([Bbk, Bpar], [Bzst[q]], lambda h, nn=nn, ct=ct, bk=bk, q=q: h.activation(
                                out=zst[q][:, 0:nn], in_=bk[:, 0:nn], func=AF.Identity, bias=pcol("cb", ct)))
                            sy.dma(sy.sp, zfsp[ct, :, t0:t0 + nn], zst[q][:, 0:nn], rbuf=Bzst[q])
                        bk, Bbk = misc()
                        for j in range(CONV_K):
                            T([Bdg[e], Bext[e]], [Bbk], lambda h, j=j, e=e, bk=bk: h.matmul(
                                bk[:, 0:NTS].rearrange("p (b s) -> p b s", s=TS), lhsT=dgw[e][:, j, :],
                                rhs=exts[e][:, :, j:j + TS], start=(j == 0), stop=(j == CONV_K - 1)))
                        q = zk % 2
                        zk += 1
                        A([Bbk, Bpar], [Bzst[q]], lambda h, ct=ct, bk=bk, q=q: h.activation(
                            out=zst[q][:, 0:NTS], in_=bk[:, 0:NTS], func=AF.Identity, bias=pcol("cb", ct)))
                        sy.dma(sy.sp, zfsp[ct, :, TP:TP + NTS], zst[q][:, 0:NTS], rbuf=Bzst[q])
                    cvo = sb(pc_, "cvo", [HALO, CW])
                    Bcvo = sy.buf("cvo", dma=True)
                    cso = sb(pc_, "cso", [NTS, CW])
                    Bcso = sy.buf("cso", dma=True)
                    for ct in range(KC):
                        bk, Bbk = misc()
                        T([Bcps, Bpar], [Bbk], lambda h, ct=ct, bk=bk: h.transpose(bk[0:HALO, 0:128], cps[:, ct, :], ident))
                        T([Bcss, Bpar], [Bbk], lambda h, ct=ct, bk=bk: h.transpose(bk[0:NTS, 128:256], css[:, ct, :], ident))
                        V([Bbk], [Bcvo], lambda h, ct=ct, bk=bk: h.tensor_copy(out=cvo[:, ct * 128:(ct + 1) * 128],
                                                                              in_=bk[0:HALO, 0:128]))
                        V([Bbk], [Bcso], lambda h, ct=ct, bk=bk: h.tensor_copy(out=cso[:, ct * 128:(ct + 1) * 128],
                                                                              in_=bk[0:NTS, 128:256]))
                    sy.dma(sy.sp, convp_d[:, :], cvo[:], rbuf=Bcvo)
                    for b in range(NSQ):
                        sy.dma(sy.sp, convs_d[b * HALO + HALO - TS:(b + 1) * HALO, :], cso[b * TS:(b + 1) * TS, :], rbuf=Bcso)
                        sy.dma(sy.sp, convs_d[b * HALO:b * HALO + HALO - TS, :], sconv[b * HALO + TS:(b + 1) * HALO, :],
                               rbuf=Bcso)
                    Bzfsp_done = Bzst
                    sy.barrier()
                    stop_here("p2c")
                with ExitStack() as pl:
                    zfb = sb(pl, "zfb", [128, KC, 512])
                    Bzfb = sy.buf("zfb", dma=True)
                    sq = sb(pl, "sq", [128, 512])
                    Bsq = sy.buf("sq")
                    mean = sb(pl, "mean", [128, 512])
                    rstd = sb(pl, "rstd", [128, 512])
                    Bmr = sy.buf("meanrstd")
                    zt = [sb(pl, "zt%d" % i, [128, 512]) for i in range(2)]
                    Bzt = [sy.buf("zt%d" % i) for i in range(2)]
                    zo = [sb(pl, "zo%d" % i, [128, 512], BF16) for i in range(2)]
                    Bzo = [sy.buf("zo%d" % i, dma=True) for i in range(2)]
                    k = 0
                    for (t0, nn) in split_blocks(0, NB2):
                        for bz in Bzfsp_done:
                            sy._wait(sy.sp, id(bz), bz.dsem, bz.dcnt)
                        sy.dma(sy.sp, zfb[:, :, 0:nn], zfsp[:, :, t0:t0 + nn].rearrange("k p t -> p k t"), wbuf=Bzfb)
                        for ct in range(KC):
                            T([Bpar, Bzfb], [Bbank[0]], lambda h, ct=ct, nn=nn: h.matmul(
                                banks[0][:, 0:nn], lhsT=onesln, rhs=zfb[:, ct, 0:nn], start=(ct == 0), stop=(ct == KC - 1)))
                        for ct in range(KC):
                            A([Bzfb], [Bsq], lambda h, ct=ct, nn=nn: h.activation(
                                out=sq[:, 0:nn], in_=zfb[:, ct, 0:nn], func=AF.Square))
                            T([Bpar, Bsq], [Bbank[1]], lambda h, ct=ct, nn=nn: h.matmul(
                                banks[1][:, 0:nn], lhsT=onesln, rhs=sq[:, 0:nn], start=(ct == 0), stop=(ct == KC - 1)))
                        V([Bbank[0]], [Bmr], lambda h, nn=nn: h.tensor_copy(out=mean[:, 0:nn], in_=banks[0][:, 0:nn]))
                        V([Bmr], [Bsq], lambda h, nn=nn: h.tensor_tensor(out=sq[:, 0:nn], in0=mean[:, 0:nn],
                                                                       in1=mean[:, 0:nn], op=ALU.mult))
                        V([Bbank[1], Bsq], [Bmr], lambda h, nn=nn: h.tensor_tensor(out=rstd[:, 0:nn], in0=banks[1][:, 0:nn],
                                                                                 in1=sq[:, 0:nn], op=ALU.subtract))
                        V([Bmr], [Bmr], lambda h, nn=nn: h.tensor_scalar(out=rstd[:, 0:nn], in0=rstd[:, 0:nn], scalar1=LN_EPS,
                                                                       scalar2=None, op0=ALU.add))
                        rsqrt_inplace(rstd[:, 0:nn], Bmr)
                        for ct in range(KC):
                            e = k % 2
                            k += 1
                            V([Bzfb, Bmr], [Bzt[e]], lambda h, ct=ct, nn=nn, e=e: h.tensor_tensor(
                                out=zt[e][:, 0:nn], in0=zfb[:, ct, 0:nn], in1=mean[:, 0:nn], op=ALU.subtract))
                            G([Bzt[e], Bmr], [Bzt[e]], lambda h, nn=nn, e=e: h.tensor_tensor(
                                out=zt[e][:, 0:nn], in0=zt[e][:, 0:nn], in1=rstd[:, 0:nn], op=ALU.mult))
                            A([Bzt[e], Bpar], [Bzo[e]], lambda h, ct=ct, nn=nn, e=e: h.activation(
                                out=zo[e][:, 0:nn], in_=zt[e][:, 0:nn], func=AF.Silu, bias=pcol("cbeta", ct),
                                scale=pcol("cg", ct)))
                            sy.dma(sy.sp, zsp[ct, :, t0:t0 + nn], zo[e][:, 0:nn], rbuf=Bzo[e])
                    Bzsp_done = Bzo
                    sy.barrier()
                    stop_here("p2l")
                with ExitStack() as pr_:
                    ws = WStream(sy, nc, pr_, 4, "p2r", 3)
                    ws.bg = bg_convert
                    ws.plan([(w_in, 0, KD, lcol), (w_in, 0, KD, lcol + 128)])
                    for hp in range(NHP):
                        ws.plan(hp_specs(hp, ["r", "k", "v"]))
                    lor, Blor = lora_inputs(pr_, ws, hT, BhT, NT2, 2, SMP0, True, shT, BshT, stg)
                    wk, Bw = alloc_work(pr_, max(GTOK, NTS), NCHG, True)
                    Sin = sb(pr_, "Sin", [64, NSQ, 2, 64])
                    BSin = sy.buf("Sin", dma=True)
                    Sout, BSout = Sin, BSin
                    Hfin = sb(pr_, "Hfin", [64, 128])
                    BHfin = sy.buf("Hfin", dma=True)
                    for hp in range(NHP):
                        for hd in range(2):
                            sy.dma(sy.sp, Sin[:, :, hd, :], swkv[:, 2 * hp + hd, :, :].rearrange("b v k -> v b k"),
                                   wbuf=BSin)
                        load_lora_w(wk, Bw, hp)
                        W = {}
                        for x in ["r", "k", "v"]:
                            W[x] = ws.get()
                        GD = [("own", HALO + g * GTOK, GTOK, 64, NCHG, True, g * GTOK, 0) for g in range(NGRP)]
                        GD += [("smp", SMP0 + h_ * NSH * TS, NSH * TS, TS, NSH, False, TP + h_ * NSH * TS, h_ * NSH)
                               for h_ in range(SH)]

                        def do_proj(i, part="both", W=W, hp=hp, GD=GD):
                            m_, tl_, n_, C_, nch_, hpv_, yt_, cb_ = GD[i]
                            rwkv_proj(m_, hp, W, hT, BhT, tl_, n_, hpv_, wk, Bw, stg, str(i % 2), cbase=cb_, nch=nch_,
                                      part=part)
                        do_proj(0)
                        for i in range(len(GD)):
                            hook = (lambda i=i: do_proj(i + 1, "mm")) if i + 1 < len(GD) else None
                            m_, tl_, n_, C_, nch_, hpv_, yt_, cb_ = GD[i]
                            if m_ == "smp" and cb_ == 0:
                                bk, Bbk = misc()
                                for hd in range(2):
                                    T([BHs, Bpar], [Bbk], lambda h, bk=bk, hp=hp, hd=hd: h.transpose(
                                        bk[0:64, hd * 64:(hd + 1) * 64], HsAll[:, hp, hd, :], ident[0:64, 0:64]))
                                V([Bbk], [BHfin], lambda h, bk=bk: h.tensor_copy(out=Hfin[:, :], in_=bk[0:64, 0:128]))
                                sy.dma(sy.sp, wkvp_d[2 * hp:2 * hp + 2, :, :].rearrange("h v k -> v h k"),
                                       Hfin[:, :].rearrange("p (h k) -> p h k", k=64), rbuf=BHfin)
                            rwkv_group(m_, hp, lor, Blor, tl_, n_, C_, nch_, hpv_, wk, Bw, shT, BshT, str(i % 2),
                                       Sin=Sin, BSin=BSin, Sout=Sout, BSout=BSout, ytok=yt_, cbase=cb_, mid_hook=hook,
                                       end_hook=(lambda i=i: do_proj(i + 1, "ev")) if i + 1 < len(GD) else None)
                        for hd in range(2):
                            sy.dma(sy.sp, wkvs_d[:, 2 * hp + hd, :, :].rearrange("b v k -> v b k"),
                                   Sout[:, :, hd, :], rbuf=BSout)
                    Bysp_done = [Bw["yst0"], Bw["yst1"]]
                    sy.barrier()
                    stop_here("p2r")
                with ExitStack() as pa:
                    sho = sb(pa, "sho", [NST, 128])
                    Bsho = sy.buf("sho", dma=True)
                    shso = sb(pa, "shso", [NSQ, SHW])
                    Bshso = sy.buf("shso", dma=True)
                    bk, Bbk = misc()
                    T([stg["Bshp"], Bpar], [Bbk], lambda h, bk=bk: h.transpose(bk[0:NST, 0:128], stg["shp"][:, :], ident))
                    V([Bbk], [Bsho], lambda h, bk=bk: h.tensor_copy(out=sho[:, :], in_=bk[0:NST, 0:128]))
                    sy.dma(sy.sp, shiftp_d[:, :], sho[:, :], rbuf=Bsho)
                    for i0 in range(0, NST, 4):
                        nq = min(4, NST - i0)
                        bk, Bbk = misc()
                        for q in range(nq):
                            T([stg["Bshs"], Bpar], [Bbk], lambda h, q=q, i0=i0, bk=bk: h.transpose(
                                bk[0:NSQ, q * 128:(q + 1) * 128], stg["shs"][:, i0 + q, :], ident))
                        V([Bbk], [Bshso], lambda h, i0=i0, nq=nq, bk=bk: h.tensor_copy(
                            out=shso[:, i0 * 128:(i0 + nq) * 128], in_=bk[0:NSQ, 0:nq * 128]))
                    sy.dma(sy.sp, shifts_d[:, :], shso[:, :], rbuf=Bshso)
                    sy.barrier()
                with ExitStack() as pm:
                    ws = WStream(sy, nc, pm, 4, "p2m", 2)
                    ws.bg = bg_convert
                    gcol = 2 * CW + SHW
                    for j in range(KD):
                        ws.plan([(wco, 0, KC, j * 128), (w_in, 0, KD, gcol + j * 128), (wro, 0, NHP, j * 128),
                                 (w_in, 0, KD, gcol + D + j * 128)])
                    zT = sb(pm, "zT", [128, KC, NB2], BF16)
                    BzT = sy.buf("zT", dma=True)
                    yT = sb(pm, "yT", [128, NHP, NB2], BF16)
                    ByT = sy.buf("yT", dma=True)
                    for bz in Bzsp_done + Bysp_done:
                        sy._wait(sy.sp, id(bz), bz.dsem, bz.dcnt)
                    sy.dma(sy.sp, zT[:, :, :], zsp[:, :, :].rearrange("k p t -> p k t"), wbuf=BzT)
                    sy.dma(sy.sp, yT[:, :, :], ysp[:, :, :].rearrange("k p t -> p k t"), wbuf=ByT)
                    mblocks = split_blocks(0, NB2)
                    nbm = len(mblocks)
                    assert nbm <= 3
                    gc = [sb(pm, "gc%d" % i, [128, 512]) for i in range(2)]
                    Bgc = [sy.buf("gc%d" % i) for i in range(2)]
                    mtmp = sb(pm, "mtmp", [128, NB2])
                    Bmt = sy.buf("mtmp")
                    m2 = [sb(pm, "m2_%d" % i, [128, 512]) for i in range(2)]
                    Bm2 = [sy.buf("m2_%d" % i) for i in range(2)]
                    mo = [sb(pm, "mo%d" % i, [128, NB2], BF16) for i in range(2)]
                    Bmo = [sy.buf("mo%d" % i, dma=True) for i in range(2)]
                    k = 0
                    for j in range(KD):
                        e = j % 2
                        for br in range(2):
                            Bs1, s1 = ws.get()
                            Bs3, s3 = ws.get()
                            if br == 0:
                                mm_pass(Bs1, s1, KC, lambda kc, t0, nn: zT[:, kc, t0:t0 + nn], [BzT], mblocks,
                                        list(range(nbm)))
                            else:
                                mm_pass(Bs1, s1, NHP, lambda kc, t0, nn: yT[:, kc, t0:t0 + nn], [ByT], mblocks,
                                        list(range(nbm)))
                            mm_pass(Bs3, s3, KD, lambda kc, t0, nn: hT[:, kc, HALO + t0:HALO + t0 + nn], [BhT], mblocks,
                                    list(range(3, 3 + nbm)))
                            for bi, (t0, nn) in enumerate(mblocks):
                                q = k % 2
                                k += 1
                                A([Bbank[3 + bi], Bpar], [Bgc[q]], lambda h, bi=bi, nn=nn, q=q, j=j, br=br: h.activation(
                                    out=gc[q][:, 0:nn], in_=banks[3 + bi][:, 0:nn], func=AF.Sigmoid,
                                    bias=pcol("bg", br * KD + j)))
                                if br == 0:
                                    V([Bbank[bi], Bgc[q]], [Bmt], lambda h, bi=bi, nn=nn, q=q, t0=t0: h.tensor_tensor(
                                        out=mtmp[:, t0:t0 + nn], in0=banks[bi][:, 0:nn], in1=gc[q][:, 0:nn], op=ALU.mult))
                                else:
                                    V([Bbank[bi], Bgc[q]], [Bm2[q]], lambda h, bi=bi, nn=nn, q=q: h.tensor_tensor(
                                        out=m2[q][:, 0:nn], in0=banks[bi][:, 0:nn], in1=gc[q][:, 0:nn], op=ALU.mult))
                                    G([Bm2[q], Bmt], [Bmo[e]], lambda h, nn=nn, q=q, t0=t0, e=e: h.tensor_tensor(
                                        out=mo[e][:, t0:t0 + nn], in0=m2[q][:, 0:nn], in1=mtmp[:, t0:t0 + nn], op=ALU.add))
                        sy.dma(sy.sp, mixsp[j, :, :], mo[e][:, :], rbuf=Bmo[e])
                    Bmix_done = Bmo
                    sy.barrier()
                    stop_here("p2m")

            with ExitStack() as ph:
                tiles = tok_tiles(TP) + [(TP + i, n) for (i, n) in tok_tiles(NTS)]
                TB = 3
                blocks = [tiles[i:i + TB] for i in range(0, len(tiles), TB)]
                BTM = TB * 128
                xb = [sb(ph, "xb%d" % i, [128, D]) for i in range(TB)]
                Bxb = [sy.buf("xb%d" % i, dma=True) for i in range(TB)]
                mfraw = sb(ph, "mfraw", [128, TB * D])
                mf = [mfraw[:, i * D:(i + 1) * D] for i in range(TB)]
                Bmf = [sy.buf("mf%d" % i, dma=True) for i in range(TB)]
                assert (TB - 1) * D * 4 >= KD * BTM * 2
                h2T = mfraw[:, D:TB * D].bitcast(BF16)[:, 0:KD * BTM].rearrange("p (k t) -> p k t", t=BTM)
                actraw = sb(ph, "actraw", [128, max(KF, KD) * BTM], BF16)
                Bact = sy.buf("act", dma=True)
                actT = actraw[:, 0:KF * BTM].rearrange("p (k t) -> p k t", t=BTM)
                mixT = actraw[:, 0:KD * BTM].rearrange("p (k t) -> p k t", t=BTM)
                tq = min(2, KD)
                tmpT = sb(ph, "tmpT", [128, tq, BTM])
                BtmpT = sy.buf("tmpT")
                sqT = sb(ph, "sqT", [128, BTM])
                BsqT = sy.buf("sqT")
                sg = [sb(ph, "sg%d" % i, [128, BTM]) for i in range(2)]
                Bsg = [sy.buf("sg%d" % i) for i in range(2)]
                rsm = sb(ph, "rsm", [128, 4])
                Brsm = sy.buf("rsm")
                ws = WStream(sy, nc, ph, 3, "p3", 2)
                kfs = []
                o_ = 0
                while o_ < KF:
                    kfs.append((o_, min(32, KF - o_)))
                    o_ += 32
                nsl = len(kfs)
                base = KF // nsl
                kfs = []
                o_ = 0
                for i in range(nsl):
                    sz = base + (1 if i < KF % nsl else 0)
                    kfs.append((o_, sz))
                    o_ += sz
                bg_convert(len(conv_jobs))
                sy._wait(sy.pool, id(Bconv), Bconv.dsem, Bconv.dcnt)
                for blk in blocks:
                    for j in range(KD):
                        ws.plan([(None, wsc_o[j, :, :], KD, None)])
                    for f in range(KF):
                        ws.plan([(None, wsc_g[f, :, :], KD, None), (None, wsc_u[f, :, :], KD, None)])
                    for j in range(KD):
                        ws.plan([(None, wsc_d[j, :, k0 * 128:(k0 + kn) * 128], kn, None) for (k0, kn) in kfs])
                for bz in Bmix_done:
                    sy._wait(sy.sp, id(bz), bz.dsem, bz.dcnt)
                obank = [0]

                def proj_to_tokmajor(blk, BT, slabs_fn, rhsT, Brhs, gname):
                    for j in range(KD):
                        bi = obank[0] % 3
                        obank[0] += 1
                        sl_list = slabs_fn(j)
                        for si, (k0, kn) in enumerate(sl_list):
                            Bsl, sl = ws.get()
                            for kc in range(kn):
                                T([Bsl] + Brhs, [Bbank[bi]], lambda h, kc=kc, k0=k0, sl=sl, bi=bi, si=si, kn=kn: h.matmul(
                                    banks[bi][:, 0:BT], lhsT=sl[:, kc, :], rhs=rhsT[:, k0 + kc, 0:BT],
                                    start=(si == 0 and kc == 0), stop=(si == len(sl_list) - 1 and kc == kn - 1)))
                        jj = j % tq
                        A([Bbank[bi], Bpar], [BtmpT], lambda h, bi=bi, jj=jj, j=j: h.activation(
                            out=tmpT[:, jj, 0:BT], in_=banks[bi][:, 0:BT], func=AF.Identity, scale=pcol(gname, j)))
                        A([Bbank[bi]], [BsqT], lambda h, bi=bi: h.activation(out=sqT[:, 0:BT], in_=banks[bi][:, 0:BT],
                                                                           func=AF.Square))
                        off = 0
                        for i, (tk0, n) in enumerate(blk):
                            T([BsqT, Bpar], [Bbank[4]], lambda h, i=i, off=off, n=n, j=j: h.matmul(
                                banks[4][0:n, i:i + 1], lhsT=sqT[:, off:off + n], rhs=onescol,
                                start=(j == 0 and i == 0), stop=(j == KD - 1), skip_group_check=True))
                            off += n
                        if jj == tq - 1:
                            j0 = j - (tq - 1)
                            off = 0
                            for i, (tk0, n) in enumerate(blk):
                                bk, Bbk = misc()
                                for q in range(tq):
                                    T([BtmpT, Bpar], [Bbk], lambda h, q=q, off=off, n=n, bk=bk: h.transpose(
                                        bk[0:n, q * 128:(q + 1) * 128], tmpT[:, q, off:off + n], ident))
                                V([Bbk], [Bmf[i]], lambda h, i=i, n=n, j0=j0, bk=bk: h.tensor_copy(
                                    out=mf[i][0:n, j0 * 128:(j0 + tq) * 128], in_=bk[0:n, 0:tq * 128]))
                                off += n

                for blk in blocks:
                    b0 = blk[0][0]
                    BT = sum(n for (_, n) in blk)
                    sy.dma(sy.sp, mixT[:, :, 0:BT], mixsp[:, :, b0:b0 + BT].rearrange("k p t -> p k t"), wbuf=Bact)
                    for i, (tk0, n) in enumerate(blk):
                        src = xp[TP + tk0:TP + tk0 + n, :] if tk0 < TP else xs[tk0 - TP:tk0 - TP + n, :]
                        sy.dma(sy.sp, xb[i][0:n, :], src, wbuf=Bxb[i])
                    proj_to_tokmajor(blk, BT, lambda j: [(0, KD)], mixT, [Bact], "g2")
                    for i, (tk0, n) in enumerate(blk):
                        V([Bbank[4]], [Brsm], lambda h, i=i, n=n: h.tensor_scalar(
                            out=rsm[0:n, i:i + 1], in0=banks[4][0:n, i:i + 1], scalar1=1.0 / D, scalar2=RMS_EPS,
                            op0=ALU.mult, op1=ALU.add))
                        rsqrt_inplace(rsm[0:n, i:i + 1], Brsm)
                        V([Bmf[i], Brsm, Bxb[i]], [Bxb[i]], lambda h, i=i, n=n: h.scalar_tensor_tensor(
                            out=xb[i][0:n, :], in0=mf[i][0:n, :], scalar=rsm[0:n, i:i + 1], in1=xb[i][0:n, :],
                            op0=ALU.mult, op1=ALU.add))
                    off = 0
                    for i, (tk0, n) in enumerate(blk):
                        norm_transpose(xb[i][0:n, :], Bxb[i], n, "g3", h2T, Bmf[1], off, mf[0], Bmf[0], mf[0], Bmf[0],
                                       extra_w=[Bmf[2]] if TB > 2 else [])
                        off += n
                    pk = 0
                    for f in range(KF):
                        Bsg_, slg = ws.get()
                        Bsu_, slu = ws.get()
                        ba, bb = (0, 1) if pk % 2 == 0 else (2, 3)
                        q = pk % 2
                        pk += 1
                        for kc in range(KD):
                            T([Bsg_, Bmf[1], Bmf[2 if TB > 2 else 1]], [Bbank[ba]], lambda h, kc=kc, slg=slg, ba=ba: h.matmul(
                                banks[ba][:, 0:BT], lhsT=slg[:, kc, :], rhs=h2T[:, kc, 0:BT], start=(kc == 0),
                                stop=(kc == KD - 1)))
                        for kc in range(KD):
                            T([Bsu_, Bmf[1], Bmf[2 if TB > 2 else 1]], [Bbank[bb]], lambda h, kc=kc, slu=slu, bb=bb: h.matmul(
                                banks[bb][:, 0:BT], lhsT=slu[:, kc, :], rhs=h2T[:, kc, 0:BT], start=(kc == 0),
                                stop=(kc == KD - 1)))
                        A([Bbank[ba]], [Bsg[q]], lambda h, ba=ba, q=q: h.activation(out=sg[q][:, 0:BT], in_=banks[ba][:, 0:BT],
                                                                                  func=AF.Silu))
                        V([Bbank[bb], Bsg[q]], [Bact], lambda h, bb=bb, q=q, f=f: h.tensor_tensor(
                            out=actT[:, f, 0:BT], in0=banks[bb][:, 0:BT], in1=sg[q][:, 0:BT], op=ALU.mult))
                    proj_to_tokmajor(blk, BT, lambda j: kfs, actT, [Bact], "g4")
                    for i, (tk0, n) in enumerate(blk):
                        V([Bbank[4]], [Brsm], lambda h, i=i, n=n: h.tensor_scalar(
                            out=rsm[0:n, i:i + 1], in0=banks[4][0:n, i:i + 1], scalar1=1.0 / D, scalar2=RMS_EPS,
                            op0=ALU.mult, op1=ALU.add))
                        rsqrt_inplace(rsm[0:n, i:i + 1], Brsm)
                        V([Bmf[i], Brsm, Bxb[i]], [Bmf[i]], lambda h, i=i, n=n: h.scalar_tensor_tensor(
                            out=mf[i][0:n, :], in0=mf[i][0:n, :], scalar=rsm[0:n, i:i + 1], in1=xb[i][0:n, :],
                            op0=ALU.mult, op1=ALU.add))
                        sy.dma(sy.sp, y_d[tk0:tk0 + n, :], mf[i][0:n, :], rbuf=Bmf[i])
        except StopBuild:
            pass
        sy.finish()
        print("kernel built: %d instructions, %d semaphores" % (sy.ninst, sy.nsem), flush=True)
    return nc


_CACHE = {}


def run(cfg, I):
    if "nc" not in _CACHE or _CACHE.get("key") != (cfg.D, cfg.SEQ, cfg.BATCH, cfg.DEC_BATCH):
        _CACHE["nc"] = build(cfg)
        _CACHE["key"] = (cfg.D, cfg.SEQ, cfg.BATCH, cfg.DEC_BATCH)
    nc = _CACHE["nc"]
    f32 = np.float32
    TP, NSQ = cfg.TP, cfg.NSQ
    xpr = np.asarray(I["x_prompt"], f32)
    xsm = np.asarray(I["x_sample"], f32)
    shared = {
        "w_in": np.ascontiguousarray(np.asarray(I["w_in"][0], f32)),
        "wco": np.ascontiguousarray(np.asarray(I["w_conv_out"][0], f32)),
        "wro": np.ascontiguousarray(np.asarray(I["w_rwkv_out"][0], f32)),
        "wo": np.ascontiguousarray(np.asarray(I["w_o"][0], f32)),
        "wg": np.ascontiguousarray(np.asarray(I["w_ffn_gate"][0], f32)),
        "wu": np.ascontiguousarray(np.asarray(I["w_ffn_up"][0], f32)),
        "wd": np.ascontiguousarray(np.asarray(I["w_ffn_down"][0], f32)),
        "pcols": make_pcols(cfg, I),
        "consts": make_consts(cfg),
        "w2a2": np.ascontiguousarray(np.concatenate([np.asarray(I["w2"][0], f32), np.asarray(I["a2"][0], f32)], axis=0)),
        "g2": np.ascontiguousarray(np.asarray(I["g2"][0], f32)),
        "cwc": make_cwcols(cfg, I),
    }
    in_maps = []
    for c in range(cfg.NCORES):
        b, half = c // 2, c % 2
        if half == 0:
            xpc = np.concatenate([np.zeros((TP, cfg.D), f32), xpr[b, :TP]], axis=0)
        else:
            xpc = xpr[b]
        sl = slice(c * NSQ, (c + 1) * NSQ)
        m = dict(shared)
        m["xp"] = np.ascontiguousarray(xpc)
        m["xs"] = np.ascontiguousarray(xsm[sl].reshape(NSQ * cfg.TS, cfg.D))
        m["swkv"] = np.ascontiguousarray(np.asarray(I["state_wkv"][0][sl], f32))
        m["sconv"] = np.ascontiguousarray(np.asarray(I["state_conv"][0][sl], f32).reshape(NSQ * HALO, cfg.CW))
        m["sshift"] = np.ascontiguousarray(np.asarray(I["state_shift"][0][sl], f32))
        in_maps.append(m)
    res = run_bass_kernel_spmd(nc, in_maps, core_ids=list(range(cfg.NCORES)))
    R = res.results
    B = cfg.BATCH
    y_p = np.zeros((B, cfg.SEQ, cfg.D), f32)
    y_s = np.zeros((cfg.DEC_BATCH, cfg.TS, cfg.D), f32)
    wkv_p = np.zeros((1, B, cfg.NH, 64, 64), f32)
    conv_p = np.zeros((1, B, HALO, cfg.CW), f32)
    shift_p = np.zeros((1, B, cfg.SHW), f32)
    wkv_s = np.zeros((1, cfg.DEC_BATCH, cfg.NH, 64, 64), f32)
    conv_s = np.zeros((1, cfg.DEC_BATCH, HALO, cfg.CW), f32)
    shift_s = np.zeros((1, cfg.DEC_BATCH, cfg.SHW), f32)
    for c in range(cfg.NCORES):
        b, half = c // 2, c % 2
        r = R[c]
        y_p[b, half * TP:(half + 1) * TP] = r["y"][:TP]
        sl = slice(c * NSQ, (c + 1) * NSQ)
        y_s[sl] = r["y"][TP:].reshape(NSQ, cfg.TS, cfg.D)
        if half == 1:
            wkv_p[0, b] = r["wkvp"]
            conv_p[0, b] = r["convp"]
            shift_p[0, b] = r["shiftp"].reshape(-1)
        wkv_s[0, sl] = r["wkvs"]
        conv_s[0, sl] = r["convs"].reshape(NSQ, HALO, cfg.CW)
        shift_s[0, sl] = r["shifts"]
    return (y_p, y_s, wkv_p, conv_p, shift_p, wkv_s, conv_s, shift_s)


def kernel(**inputs):
    cfg = Cfg()
    return run(cfg, inputs)
```
